# Optimizing a Trainium2 kernel written in Bass

```python
import math
import numpy as np
import jax
import jax.numpy as jnp
from jax import lax

D_MODEL = 1024
BATCH = 8
SEQ = 4096
DEPTH = 2

CTX_LEN = 256
GRID_W = 64
N_EVEN = (DEPTH + 1) // 2
N_ODD = DEPTH // 2
N_MOD = 9
D_FF = 2816
EPS = 1e-6

DA_HEADS = 4
DA_HD = 64
DA_VD = 2 * DA_HD
ML_HEADS = 4
ML_QK = 64
ML_V = 128
ML_CONV = 3
GLA_HEADS = 4
GLA_K = 128
GLA_V = 256
GLA_RANK = 16
GLA_TAU = 16.0

CHUNK = 64
Q_BLOCK = 128
ROPE_BASE = 10000.0
ROPE_AXIS = DA_HD // 2

EVEN_SPLITS = (DA_HEADS * 2 * DA_HD, DA_HEADS * 2 * DA_HD, DA_HEADS * DA_VD,
               2 * ML_HEADS * ML_QK, ML_HEADS * ML_V, ML_HEADS * ML_V, 4 * ML_HEADS)
EVEN_IN = sum(EVEN_SPLITS)
ODD_SPLITS = (GLA_HEADS * GLA_K, GLA_HEADS * GLA_K, GLA_HEADS * GLA_V, GLA_HEADS * GLA_V, 2 * GLA_RANK)
ODD_IN = sum(ODD_SPLITS)

kernel_name = "hybrid_diffattn_mlstm_gla_macaron_prefix"


def _split(p, sizes):
    idx = np.cumsum(sizes)[:-1].tolist()
    return jnp.split(p, idx, axis=-1)


def rms_norm(x, g):
    xf = x.astype(jnp.float32)
    y = xf * lax.rsqrt(jnp.mean(xf * xf, axis=-1, keepdims=True) + EPS)
    return y.astype(x.dtype) * g


def ada_norm(x, g, shift, scale):
    return rms_norm(x, g) * (1.0 + scale) + shift


def swiglu(h, w_in, w_out):
    a, b = jnp.split(h @ w_in, 2, axis=-1)
    return (jax.nn.silu(a) * b) @ w_out


def axial_rope_tables(rows, dtype):
    row = jnp.repeat(jnp.arange(rows), GRID_W).astype(jnp.float32)
    col = (jnp.arange(rows * GRID_W) % GRID_W).astype(jnp.float32)
    inv = ROPE_BASE ** (-jnp.arange(ROPE_AXIS // 2, dtype=jnp.float32) * 2.0 / ROPE_AXIS)
    ar = row[:, None] * inv
    ac = col[:, None] * inv
    return (jnp.cos(ar).astype(dtype), jnp.sin(ar).astype(dtype),
            jnp.cos(ac).astype(dtype), jnp.sin(ac).astype(dtype))


def _rope_1d(x, cos, sin):
    cos = cos[None, :, None, None, :]
    sin = sin[None, :, None, None, :]
    x1, x2 = jnp.split(x, 2, axis=-1)
    return jnp.concatenate([x1 * cos - x2 * sin, x2 * cos + x1 * sin], axis=-1)


def rope_2d(x, rope):
    cr, sr, cc, sc = rope
    xr, xc = jnp.split(x, 2, axis=-1)
    return jnp.concatenate([_rope_1d(xr, cr, sr), _rope_1d(xc, cc, sc)], axis=-1)


def diff_softmax_core(q, k, v, lam):
    s = jnp.einsum('bqhcd,bkhcd->bhcqk', q.astype(jnp.float32), k.astype(jnp.float32)) * (DA_HD ** -0.5)
    p = jax.nn.softmax(s, axis=-1)
    w = p[:, :, 0] - lam * p[:, :, 1]
    return jnp.einsum('bhqk,bkhe->bqhe', w, v.astype(jnp.float32))


def diff_attn_latent(q, k, v, lam):
    b, t, h, _, d = q.shape
    nb = t // Q_BLOCK
    qb = jnp.moveaxis(q.reshape(b, nb, Q_BLOCK, h, 2, d), 1, 0)
    out = lax.map(lambda qq: diff_softmax_core(qq, k, v, lam), qb)
    return jnp.moveaxis(out, 0, 1).reshape(b, t, h, -1)


def dw_conv(x, w, bias):
    ch = x.shape[-1]
    pad = ML_CONV // 2
    y = lax.conv_general_dilated(x, w[:, None, :].astype(x.dtype), window_strides=(1,),
                                 padding=((pad, pad),), dimension_numbers=('NWC', 'WIO', 'NWC'),
                                 feature_group_count=ch)
    return y + bias


def to_chunks(a):
    b, s, h = a.shape[:3]
    rest = a.shape[3:]
    a = a.reshape((b, s // CHUNK, CHUNK, h) + rest)
    return a.transpose((1, 0, 3, 2) + tuple(range(4, a.ndim)))


def from_chunks(y):
    nc, b, h, l = y.shape[:4]
    rest = y.shape[4:]
    y = y.transpose((1, 0, 3, 2) + tuple(range(4, y.ndim)))
    return y.reshape((b, nc * l, h) + rest)


def mlstm_scan(q, k, v, ig, lf, state):
    xs = tuple(to_chunks(a.astype(jnp.float32)) for a in (q, k, v, ig, lf))
    tril = jnp.tril(jnp.ones((CHUNK, CHUNK), dtype=bool))

    def step(carry, inp):
        cmat, nvec, m = carry
        qc, kc, vc, ic, fc = inp
        bcum = jnp.cumsum(fc, axis=-1)
        dmat = jnp.where(tril, bcum[..., :, None] - bcum[..., None, :] + ic[..., None, :], -jnp.inf)
        m_inter = bcum + m[..., None]
        m_t = jnp.maximum(m_inter, jnp.max(dmat, axis=-1))
        w_inter = jnp.exp(m_inter - m_t)
        s = jnp.einsum('bhtd,bhsd->bhts', qc, kc) * jnp.exp(dmat - m_t[..., None])
        num = jnp.einsum('bhts,bhse->bhte', s, vc) + w_inter[..., None] * jnp.einsum('bhtd,bhed->bhte', qc, cmat)
        den = jnp.sum(s, axis=-1) + w_inter * jnp.einsum('bhtd,bhd->bht', qc, nvec)
        h = num / jnp.maximum(jnp.abs(den), jnp.exp(-m_t))[..., None]
        bl = bcum[..., -1]
        g = bl[..., None] - bcum + ic
        m_new = jnp.maximum(bl + m, jnp.max(g, axis=-1))
        decay = jnp.exp(bl + m - m_new)
        wk = jnp.exp(g - m_new[..., None])
        cmat = decay[..., None, None] * cmat + jnp.einsum('bhs,bhse,bhsd->bhed', wk, vc, kc)
        nvec = decay[..., None] * nvec + jnp.einsum('bhs,bhsd->bhd', wk, kc)
        return (cmat, nvec, m_new), h

    state, hs = lax.scan(step, state, xs)
    return from_chunks(hs), state


def gla_scan(q, k, v, la, state):
    xs = tuple(to_chunks(a.astype(jnp.float32)) for a in (q, k, v, la))
    tril = jnp.tril(jnp.ones((CHUNK, CHUNK), dtype=bool))

    def step(smat, inp):
        qc, kc, vc, ac = inp
        bcum = jnp.cumsum(ac, axis=2)
        qe = qc * jnp.exp(bcum)
        ke = kc * jnp.exp(-bcum)
        a = jnp.where(tril, jnp.einsum('bhtd,bhsd->bhts', qe, ke), 0.0)
        o = jnp.einsum('bhts,bhse->bhte', a, vc) + jnp.einsum('bhtd,bhde->bhte', qe, smat)
        bl = bcum[:, :, -1]
        kd = kc * jnp.exp(bl[:, :, None] - bcum)
        smat = jnp.exp(bl)[..., None] * smat + jnp.einsum('bhsd,bhse->bhde', kd, vc)
        return smat, o

    state, os_ = lax.scan(step, state, xs)
    return from_chunks(os_), state


def bidir_scan(scan, qkv_c, qkv_x, gc_f, gc_b, gx_f, gx_b, st0, need_ctx):
    def rev(arrs):
        return tuple(jnp.flip(a, axis=1) for a in arrs)
    yc_f, s_f = scan(*qkv_c, *gc_f, st0)
    yc_b, s_b = scan(*rev(qkv_c), *rev(gc_b), st0)
    yx_f, _ = scan(*qkv_x, *gx_f, s_f)
    yx_b, _ = scan(*rev(qkv_x), *rev(gx_b), s_b)
    yx = yx_f + jnp.flip(yx_b, axis=1)
    yc = yc_f + jnp.flip(yc_b, axis=1) if need_ctx else None
    return yx, yc


def even_mixer(hx, hc, w_in, w_out, diff_lambda, diff_norm_g, conv_w, conv_b, gate_b, ml_norm_g,
               lam_init, rope, need_ctx):
    dt = hx.dtype
    px = _split(hx @ w_in, EVEN_SPLITS)
    pc = _split(hc @ w_in, EVEN_SPLITS)
    lp = diff_lambda.astype(jnp.float32)
    lam = jnp.exp(jnp.sum(lp[0] * lp[1])) - jnp.exp(jnp.sum(lp[2] * lp[3])) + lam_init

    def da_heads(p):
        b, t = p[0].shape[:2]
        return (p[0].reshape(b, t, DA_HEADS, 2, DA_HD), p[1].reshape(b, t, DA_HEADS, 2, DA_HD),
                p[2].reshape(b, t, DA_HEADS, DA_VD))

    def ml_heads(p):
        b, t = p[0].shape[:2]
        qk = jax.nn.silu(dw_conv(p[3], conv_w, conv_b))
        q, k = jnp.split(qk, 2, axis=-1)
        q = q.reshape(b, t, ML_HEADS, ML_QK) * (ML_QK ** -0.5)
        k = k.reshape(b, t, ML_HEADS, ML_QK)
        v = p[4].reshape(b, t, ML_HEADS, ML_V)
        g = (p[6] + gate_b.reshape(-1)).astype(jnp.float32).reshape(b, t, 4, ML_HEADS)
        fwd = (g[:, :, 0], jax.nn.log_sigmoid(g[:, :, 2]))
        bwd = (g[:, :, 1], jax.nn.log_sigmoid(g[:, :, 3]))
        return (q, k, v), fwd, bwd

    def merge(attn, mem, o_pre):
        b, t = attn.shape[:2]
        a = (rms_norm(attn, diff_norm_g) * (1.0 - lam_init)).reshape(b, t, -1)
        m = rms_norm(mem, ml_norm_g).reshape(b, t, -1) * jax.nn.sigmoid(o_pre)
        return jnp.concatenate([a, m], axis=-1).astype(dt) @ w_out

    qx, kx, vx = da_heads(px)
    qc, kc, vc = da_heads(pc)
    qx = rope_2d(qx, rope)
    kx = rope_2d(kx, rope)
    ax = diff_attn_latent(qx, jnp.concatenate([kc, kx], axis=1), jnp.concatenate([vc, vx], axis=1), lam)

    qkv_x, gx_f, gx_b = ml_heads(px)
    qkv_c, gc_f, gc_b = ml_heads(pc)
    bsz = hx.shape[0]
    st0 = (jnp.zeros((bsz, ML_HEADS, ML_V, ML_QK), jnp.float32),
           jnp.zeros((bsz, ML_HEADS, ML_QK), jnp.float32),
           jnp.zeros((bsz, ML_HEADS), jnp.float32))
    mx, mc = bidir_scan(mlstm_scan, qkv_c, qkv_x, gc_f, gc_b, gx_f, gx_b, st0, need_ctx)

    out_x = merge(ax, mx, px[5])
    out_c = merge(diff_softmax_core(qc, kc, vc, lam), mc, pc[5]) if need_ctx else None
    return out_x, out_c


def odd_mixer(hx, hc, w_in, w_out, w_gate, b_gate, norm_g, need_ctx):
    dt = hx.dtype
    px = _split(hx @ w_in, ODD_SPLITS)
    pc = _split(hc @ w_in, ODD_SPLITS)

    def heads(p):
        b, t = p[0].shape[:2]
        q = p[0].reshape(b, t, GLA_HEADS, GLA_K) * (GLA_K ** -0.5)
        k = p[1].reshape(b, t, GLA_HEADS, GLA_K)
        v = p[2].reshape(b, t, GLA_HEADS, GLA_V)
        lr_f, lr_b = jnp.split(p[4], 2, axis=-1)

        def decay(lr, d):
            z = (lr @ w_gate[d] + b_gate[d]).astype(jnp.float32)
            return (jax.nn.log_sigmoid(z) / GLA_TAU).reshape(b, t, GLA_HEADS, GLA_K)
        return (q, k, v), (decay(lr_f, 0),), (decay(lr_b, 1),)

    def merge(o, r):
        b, t = o.shape[:2]
        y = rms_norm(o, norm_g).reshape(b, t, -1) * jax.nn.silu(r)
        return y.astype(dt) @ w_out

    qkv_x, gx_f, gx_b = heads(px)
    qkv_c, gc_f, gc_b = heads(pc)
    st0 = jnp.zeros((hx.shape[0], GLA_HEADS, GLA_K, GLA_V), jnp.float32)
    ox, oc = bidir_scan(gla_scan, qkv_c, qkv_x, gc_f, gc_b, gx_f, gx_b, st0, need_ctx)
    out_x = merge(ox, px[3])
    out_c = merge(oc, pc[3]) if need_ctx else None
    return out_x, out_c


def setup_inputs(seed: int = 0) -> dict:
    key = jax.random.key(seed)
    ks = iter(jax.random.split(key, 32))

    def nrm(shape, scale):
        return jax.random.normal(next(ks), shape, jnp.float32) * scale

    d = D_MODEL
    inv = d ** -0.5
    x = nrm((BATCH, SEQ, d), 1.0)
    c = nrm((BATCH, d), 1.0)
    ctx = nrm((BATCH, CTX_LEN, d), 1.0)
    c_ctx = nrm((d,), 1.0)
    ada_w = nrm((DEPTH, d, N_MOD * d), 0.3 * inv)
    ada_b = nrm((DEPTH, N_MOD * d), 0.02)
    norm_g = 1.0 + nrm((DEPTH, 3, d), 0.02)
    ffn_w_in = nrm((DEPTH, 2, d, 2 * D_FF), inv)
    ffn_w_out = nrm((DEPTH, 2, D_FF, d), D_FF ** -0.5)
    even_w_in = nrm((N_EVEN, d, EVEN_IN), inv)
    even_w_out = nrm((N_EVEN, d, d), inv)
    diff_lambda = nrm((N_EVEN, 4, DA_HD), 0.1)
    diff_norm_g = 1.0 + nrm((N_EVEN, DA_VD), 0.02)
    mlstm_conv_w = nrm((N_EVEN, ML_CONV, 2 * ML_HEADS * ML_QK), ML_CONV ** -0.5)
    mlstm_conv_b = nrm((N_EVEN, 2 * ML_HEADS * ML_QK), 0.02)
    fbias = jnp.concatenate([jnp.zeros((2, ML_HEADS), jnp.float32),
                             jnp.tile(jnp.linspace(3.0, 6.0, ML_HEADS, dtype=jnp.float32)[None], (2, 1))], axis=0)
    mlstm_gate_b = fbias[None] + nrm((N_EVEN, 4, ML_HEADS), 0.1)
    mlstm_norm_g = 1.0 + nrm((N_EVEN, ML_HEADS, ML_V), 0.02)
    odd_w_in = nrm((N_ODD, d, ODD_IN), inv)
    odd_w_out = nrm((N_ODD, d, d), inv)
    gla_w_gate = nrm((N_ODD, 2, GLA_RANK, GLA_HEADS * GLA_K), GLA_RANK ** -0.5)
    gla_b_gate = nrm((N_ODD, 2, GLA_HEADS * GLA_K), 0.1)
    gla_norm_g = 1.0 + nrm((N_ODD, GLA_V), 0.02)
    final_g = 1.0 + nrm((d,), 0.02)
    return {"x": x, "c": c, "ctx": ctx, "c_ctx": c_ctx, "ada_w": ada_w, "ada_b": ada_b,
            "norm_g": norm_g, "ffn_w_in": ffn_w_in, "ffn_w_out": ffn_w_out,
            "even_w_in": even_w_in, "even_w_out": even_w_out, "diff_lambda": diff_lambda,
            "diff_norm_g": diff_norm_g, "mlstm_conv_w": mlstm_conv_w, "mlstm_conv_b": mlstm_conv_b,
            "mlstm_gate_b": mlstm_gate_b, "mlstm_norm_g": mlstm_norm_g, "odd_w_in": odd_w_in,
            "odd_w_out": odd_w_out, "gla_w_gate": gla_w_gate, "gla_b_gate": gla_b_gate,
            "gla_norm_g": gla_norm_g, "final_g": final_g}


def reference(x, c, ctx, c_ctx, ada_w, ada_b, norm_g, ffn_w_in, ffn_w_out, even_w_in, even_w_out,
              diff_lambda, diff_norm_g, mlstm_conv_w, mlstm_conv_b, mlstm_gate_b, mlstm_norm_g,
              odd_w_in, odd_w_out, gla_w_gate, gla_b_gate, gla_norm_g, final_g):
    bsz, n_tok, d = x.shape
    rows = n_tok // GRID_W
    rope = axial_rope_tables(rows, x.dtype)
    s_lat = jax.nn.silu(c)
    s_ctx = jax.nn.silu(c_ctx)
    hctx = ctx
    for l in range(DEPTH):
        last = l == DEPTH - 1
        mx = (s_lat @ ada_w[l] + ada_b[l]).reshape(bsz, N_MOD, 1, d)
        mc = (s_ctx @ ada_w[l] + ada_b[l]).reshape(N_MOD, d)
        x = x + 0.5 * mx[:, 2] * swiglu(ada_norm(x, norm_g[l, 0], mx[:, 0], mx[:, 1]), ffn_w_in[l, 0], ffn_w_out[l, 0])
        hctx = hctx + 0.5 * mc[2] * swiglu(ada_norm(hctx, norm_g[l, 0], mc[0], mc[1]), ffn_w_in[l, 0], ffn_w_out[l, 0])
        hx = ada_norm(x, norm_g[l, 1], mx[:, 3], mx[:, 4])
        hc = ada_norm(hctx, norm_g[l, 1], mc[3], mc[4])
        if l % 2 == 0:
            e = l // 2
            lam_init = 0.8 - 0.6 * math.exp(-0.3 * l)
            yx, yc = even_mixer(hx, hc, even_w_in[e], even_w_out[e], diff_lambda[e], diff_norm_g[e],
                                mlstm_conv_w[e], mlstm_conv_b[e], mlstm_gate_b[e], mlstm_norm_g[e],
                                lam_init, rope, not last)
        else:
            o = l // 2
            yx, yc = odd_mixer(hx, hc, odd_w_in[o], odd_w_out[o], gla_w_gate[o], gla_b_gate[o],
                               gla_norm_g[o], not last)
        x = x + mx[:, 5] * yx
        x = x + 0.5 * mx[:, 8] * swiglu(ada_norm(x, norm_g[l, 2], mx[:, 6], mx[:, 7]), ffn_w_in[l, 1], ffn_w_out[l, 1])
        if not last:
            hctx = hctx + mc[5] * yc
            hctx = hctx + 0.5 * mc[8] * swiglu(ada_norm(hctx, norm_g[l, 2], mc[6], mc[7]), ffn_w_in[l, 1], ffn_w_out[l, 1])
    return rms_norm(x, final_g)
```

```python
import math
from contextlib import ExitStack

import numpy as np
import concourse.bass as bass
import concourse.mybir as mybir
from concourse.bass_utils import run_bass_kernel_spmd

F32 = mybir.dt.float32
BF16 = mybir.dt.bfloat16
AF = mybir.ActivationFunctionType
ALU = mybir.AluOpType
AX = mybir.AxisListType

D = 1024
T = 4096
TC = 256
TT = T + TC
NCH = TT // 128
DFF = 2816
NJ = DFF // 128
EPS = 1e-6
EVEN_X = 4112
ODD_IN = 3104
LAT_TILES = [(4 * t, 4) for t in range(8)]
CTX_TILE = (32, 2)
ALL_TILES = LAT_TILES + [CTX_TILE]


class Op:
    __slots__ = ("eng", "fn", "deps", "is_dma", "dma_key", "signaled", "val", "idx", "waits")


class Prog:
    ENG = ("pe", "act", "dve", "pool", "sp")
    uid = 0

    def __init__(self, nc, same_engine_sync=True):
        self.nc = nc
        self.ops = {e: [] for e in self.ENG}
        self.lw = {}
        self.rd = {}
        self.dma_cnt = {}
        self.ses = same_engine_sync

    def add(self, eng, fn, reads=(), writes=(), dma_key=None):
        op = Op()
        op.eng = eng
        op.fn = fn
        op.is_dma = dma_key is not None
        op.dma_key = dma_key
        op.signaled = op.is_dma
        op.val = 0
        deps = {}
        for k in reads:
            w = self.lw.get(k)
            if w is not None:
                deps[id(w)] = w
        for k in writes:
            w = self.lw.get(k)
            if w is not None:
                deps[id(w)] = w
            for r in self.rd.get(k, ()):
                deps[id(r)] = r
        op.deps = list(deps.values())
        for k in reads:
            lst = self.rd.setdefault(k, [])
            if not op.is_dma:
                lst[:] = [r for r in lst if r.is_dma or r.eng != eng]
            lst.append(op)
        for k in writes:
            self.lw[k] = op
            self.rd[k] = []
        if op.is_dma:
            c = self.dma_cnt.get(dma_key, 0) + 1
            self.dma_cnt[dma_key] = c
            op.val = 16 * c
        op.idx = len(self.ops[eng])
        self.ops[eng].append(op)
        return op

    def wait_all(self, eng, ops):
        op = self.add(eng, None)
        op.deps = list(ops)
        return op

    def emit(self, stack):
        nc = self.nc
        for e in self.ENG:
            for a in self.ops[e]:
                need = []
                for b in a.deps:
                    if b.is_dma:
                        need.append(b)
                    elif b.eng == a.eng and not a.is_dma:
                        if a.eng == "pe" or not self.ses:
                            continue
                        need.append(b)
                        b.signaled = True
                    else:
                        need.append(b)
                        b.signaled = True
                a.waits = need
        for e in self.ENG:
            c = 0
            for a in self.ops[e]:
                if a.is_dma or a.fn is None:
                    continue
                if a.signaled:
                    c += 1
                    a.val = c
        Prog.uid += 1
        esem = {e: nc.alloc_semaphore(name="s%d_%s" % (Prog.uid, e)) for e in self.ENG}
        dsem = {}
        for k in self.dma_cnt:
            dsem[k] = nc.alloc_semaphore(name="d%d_%d" % (Prog.uid, len(dsem)))
        self.sems = list(esem.values()) + list(dsem.values())
        self.n_sems = len(self.sems)

        def run(engobj, ename):
            waited = {}
            for a in self.ops[ename]:
                mx = {}
                for b in a.waits:
                    if b.is_dma:
                        sk, sem = b.dma_key, dsem[b.dma_key]
                    else:
                        sk, sem = b.eng, esem[b.eng]
                    if sk not in mx or mx[sk][1] < b.val:
                        mx[sk] = (sem, b.val)
                for sk, (sem, val) in mx.items():
                    if waited.get(sk, 0) >= val:
                        continue
                    waited[sk] = val
                    engobj.wait_ge(sem, val)
                if a.fn is None:
                    continue
                ins = a.fn(engobj)
                if a.is_dma:
                    ins.then_inc(dsem[a.dma_key], 16)
                elif a.signaled:
                    ins.then_inc(esem[ename], 1)

        block = stack.enter_context(nc.Block())

        @block.tensor
        def _(eng):
            run(eng, "pe")

        @block.scalar
        def _(eng):
            run(eng, "act")

        @block.vector
        def _(eng):
            run(eng, "dve")

        @block.gpsimd
        def _(eng):
            run(eng, "pool")

        @block.sync
        def _(eng):
            run(eng, "sp")


def rr_run(gens):
    gens = list(gens)
    while gens:
        for g in list(gens):
            try:
                next(g)
            except StopIteration:
                gens.remove(g)


class Ring:
    def __init__(self, stage, name, shape, dt, n, psum=False):
        mk = stage.ps if psum else stage.sb
        self.bufs = [(mk("%s%d" % (name, i), shape, dt), "%s%d" % (name, i)) for i in range(n)]
        self.i = 0

    def next(self):
        b = self.bufs[self.i % len(self.bufs)]
        self.i += 1
        return b


class PRing:
    def __init__(self, stage, name, width, n, per):
        self.bufs = []
        self.tensors = []
        nb = (n + per - 1) // per
        slot = 512 // per
        assert width <= slot
        for b in range(nb):
            t = stage.ps("%s_b%d" % (name, b), [128, 512], F32)
            self.tensors.append(t)
            for j in range(per):
                if len(self.bufs) < n:
                    self.bufs.append((t[:, j * slot:j * slot + width], "%s%d" % (name, len(self.bufs))))
        self.i = 0

    def next(self):
        b = self.bufs[self.i % len(self.bufs)]
        self.i += 1
        return b


class Stage:
    def __init__(self, mk, name):
        self.mk = mk
        self.nc = mk.nc
        self.name = name
        self.st = ExitStack()
        self.P = Prog(self.nc)
        self.outs = []
        self.nslot = 0

    def sb(self, name, shape, dt):
        return self.st.enter_context(self.nc.sbuf_tensor("%s_%s" % (self.name, name), shape, dt))

    def ps(self, name, shape, dt):
        return self.st.enter_context(self.nc.psum_tensor("%s_%s" % (self.name, name), shape, dt))

    def op(self, eng, fn, reads=(), writes=()):
        return self.P.add(eng, fn, reads, writes)

    def dma(self, eng, out, in_, reads, writes, slow=False, out_final=False, skey=None):
        key = ("dma", writes[0] if skey is None else skey)
        if slow:
            fn = lambda e: e.dma_start(out=out, in_=in_, allow_slow_non_contiguous=True)
        else:
            fn = lambda e: e.dma_start(out=out, in_=in_)
        op = self.P.add(eng, fn, reads, writes, dma_key=key)
        if out_final:
            self.outs.append(op)
        return op

    def store(self, eng, out, in_, reads, writes, slow=False, skey=None):
        return self.dma(eng, out, in_, reads, writes, slow=slow, out_final=True, skey=skey)

    def load_cols(self, dst, key, row_ap, rkeys=()):
        return self.dma("sp", dst, row_ap.rearrange("o (j p) -> p (o j)", p=128), list(rkeys), [key], slow=True)

    def load_bc(self, dst, key, row_ap, rkeys=(), parts=128):
        return self.dma("sp", dst, row_ap.partition_broadcast(parts), list(rkeys), [key])

    def finish(self):
        alld = [a for e in Prog.ENG for a in self.P.ops[e] if a.is_dma]
        if alld:
            self.P.wait_all("sp", alld)
        self.P.emit(self.st)
        self.st.close()
        self.nc.all_engine_barrier()
        self.nc.clear_and_free_semaphores(self.P.sems)
        self.nc.all_engine_barrier()


class MK:
    def __init__(self, upto=None, dbg=False, ext=()):
        self.upto = upto
        self.dbg = dbg
        nc = self.nc = bass.Bass("TRN2", target_bir_lowering=False)
        inp = lambda name, shape, dt=F32: nc.dram_tensor(name, list(shape), dt, kind="ExternalInput").ap()
        itn = lambda name, shape, dt=F32: nc.dram_tensor(name, list(shape), dt,
                                                         kind="ExternalInput" if name in ext else "Internal").ap()
        self.x = inp("x", [T, D])
        self.ctx = inp("ctx", [TC, D])
        self.c = inp("c", [1, D])
        self.c_ctx = inp("c_ctx", [1, D])
        self.ada_w = inp("ada_w", [2, D, 9 * D])
        self.ada_b = inp("ada_b", [2, 9 * D])
        self.norm_g = inp("norm_g", [2, 3 * D])
        self.ffn_w_in = inp("ffn_w_in", [2, 2, D, 2 * DFF])
        self.ffn_w_out = inp("ffn_w_out", [2, 2, DFF, D])
        self.even_w_in = inp("even_w_in", [D, EVEN_X])
        self.even_w_out = inp("even_w_out", [D, D])
        self.diff_lambda = inp("diff_lambda", [1, 256])
        self.diff_norm_g = inp("diff_norm_g", [1, 128])
        self.conv_w = inp("mlstm_conv_w", [3, 512])
        self.conv_b = inp("mlstm_conv_b", [1, 512])
        self.gate_b = inp("mlstm_gate_b", [1, 16])
        self.ml_norm_g = inp("mlstm_norm_g", [1, 512])
        self.odd_w_in = inp("odd_w_in", [D, ODD_IN])
        self.odd_w_out = inp("odd_w_out", [D, D])
        self.gla_w_gate = inp("gla_w_gate", [2, 16, 512])
        self.gla_b_gate = inp("gla_b_gate", [2, 512])
        self.gla_norm_g = inp("gla_norm_g", [1, 256])
        self.final_g = inp("final_g", [1, D])
        self.rope_c = inp("rope_c", [128, T])
        self.rope_s = inp("rope_s", [128, T])
        self.cmask = inp("cmask", [6, 128, 128])
        self.out = nc.dram_tensor("out", [T, D], F32, kind="ExternalOutput").ap()
        self.xres = itn("xres", [TT, D])
        self.modrow = itn("modrow", [2, 2, 9 * D])
        self.w1b = itn("w1b", [2, 2, D, 2 * DFF], BF16)
        self.w2b = itn("w2b", [2, 2, DFF, D], BF16)
        self.ewin_b = itn("ewin_b", [D, EVEN_X], BF16)
        self.ewout_b = itn("ewout_b", [D, D], BF16)
        self.owin_b = itn("owin_b", [D, ODD_IN], BF16)
        self.owout_b = itn("owout_b", [D, D], BF16)
        self.merged = itn("merged", [TT, D], BF16)
        self.QT = itn("QT", [512, TT], BF16)
        self.KT = itn("KT", [512, TT], BF16)
        self.Vs = itn("Vs", [TT, 512], BF16)
        self.MQK = itn("MQK", [512, TT], F32)
        self.MV = itn("MV", [TT, 512], BF16)
        self.OPRE = itn("OPRE", [TT, 512], F32)
        self.GATE = itn("GATE", [TT, 16], F32)
        self.nrm = itn("nrm", [128, 16])
        self.MQKB = itn("MQKB", [512, TT], BF16)
        self.MKT = itn("MKT", [TT, 256], BF16)
        self.GQT = itn("GQT", [512, TT], BF16)
        self.GKT = itn("GKT", [512, TT], BF16)
        self.GK = itn("GK", [TT, 512], BF16)
        self.GV = itn("GV", [TT, D], BF16)
        self.GR = itn("GR", [TT, D], F32)
        self.GLA = itn("GLA", [2, TT, 512], F32)
        self.OF = itn("OF", [TT, D], F32)
        if dbg:
            self.dbg_x = nc.dram_tensor("dbg_x", [TT, D], F32, kind="ExternalOutput").ap()
            self.dbg_m = nc.dram_tensor("dbg_m", [TT, D], BF16, kind="ExternalOutput").ap()
            self.dbg_mod = nc.dram_tensor("dbg_mod", [4, 9 * D], F32, kind="ExternalOutput").ap()

    def cast_rows(self, S, dst, src, nrows, key, eng="pool"):
        for r in range(0, nrows, 128):
            S.store(eng, dst[r:r + 128, :], src[r:r + 128, :], [], [(key, r)], skey="cast")

    def consts(self, S, want=("ident",)):
        P = S.P
        res = {}
        names = {"ident": 0, "U": 1, "L": 2, "ones": 3, "SU": 4, "SL": 5}
        for w in want:
            base = w.rstrip("fb")
            f = S.sb("c_" + base + "f", [128, 128], F32) if ("c_" + base + "f") not in res else None
            kf = "c_" + base + "f"
            if kf not in res:
                S.dma("sp", f[:], self.cmask[names[base]], [], [kf])
                res[kf] = f
            if w.endswith("b"):
                b = S.sb("c_" + w, [128, 128], BF16)
                S.op("dve", lambda e, b=b, f=res[kf]: e.tensor_copy(out=b[:], in_=f[:]), [kf], ["c_" + w])
                res[w] = b
            else:
                res[w] = res[kf]
        return res

    def stage_prologue(self):
        S = Stage(self, "pro")
        for r in range(0, T, 512):
            S.store("sp", self.xres[r:r + 512, :], self.x[r:r + 512, :], [], [("xres", r)], skey="cp")
        S.store("sp", self.xres[T:TT, :], self.ctx[:, :], [], [("xres", T)], skey="cp")
        self.cast_rows(S, self.w1b[0, 0], self.ffn_w_in[0, 0], D, "w1b00")
        self.cast_rows(S, self.w2b[0, 0], self.ffn_w_out[0, 0], DFF, "w2b00")
        S.finish()

    def stage_ada(self, l, extra_casts=()):
        S = Stage(self, "ada%d" % l)
        for (dst, src, nrows, key) in extra_casts:
            self.cast_rows(S, dst, src, nrows, key)
        ccol = S.sb("ccol", [128, 2, 8], F32)
        sT = S.sb("sT", [128, 8, 2], BF16)
        bias = S.sb("bias", [2, 9 * D], F32)
        rows = S.sb("rows", [2, 9 * D], F32)
        S.load_cols(ccol[:, 0, :], "ccol0", self.c)
        S.load_cols(ccol[:, 1, :], "ccol1", self.c_ctx)
        S.load_bc(bias[:], "bias", self.ada_b[l:l + 1, :], parts=2)
        for w in range(2):
            S.op("act", lambda e, w=w: e.activation(out=sT[:, :, w], in_=ccol[:, w, :], func=AF.Silu),
                 ["ccol%d" % w], ["sT"])
        wring = Ring(S, "aw", [128, 8, 512], BF16, 4)
        pring = Ring(S, "pm", [2, 512], F32, 2, psum=True)
        awv = self.ada_w[l].rearrange("(k p) n -> p k n", p=128)
        for ch in range(18):
            wt, wk = wring.next()
            S.dma("pool", wt[:], awv[:, :, ch * 512:(ch + 1) * 512], [], [wk])
            pt, pk = pring.next()
            for k in range(8):
                S.op("pe", lambda e, pt=pt, wt=wt, k=k: e.matmul(pt[:], lhsT=sT[:, k, :], rhs=wt[:, k, :],
                                                                 start=(k == 0), stop=(k == 7)),
                     ["sT", wk], [pk])
            S.op("dve", lambda e, pt=pt, ch=ch: e.tensor_tensor(out=rows[:, ch * 512:(ch + 1) * 512], in0=pt[:],
                                                                in1=bias[:, ch * 512:(ch + 1) * 512], op=ALU.add),
                 [pk, "bias"], ["rows"])
        S.store("sp", self.modrow[l], rows[:], ["rows"], [("modrow", l)])
        S.finish()

    def mod_cols(self, S, l, si, tag):
        g = S.sb(tag + "g", [128, 8], F32)
        sc = S.sb(tag + "sc", [128, 2, 8], F32)
        A = S.sb(tag + "A", [128, 2, 8], F32)
        B = S.sb(tag + "B", [128, 2, 8], F32)
        S.load_cols(g[:], tag + "g", self.norm_g[l:l + 1, si * D:(si + 1) * D])
        for w in range(2):
            S.load_cols(B[:, w, :], tag + "B%d" % w, self.modrow[l, w:w + 1, (3 * si) * D:(3 * si + 1) * D])
            S.load_cols(sc[:, w, :], tag + "sc%d" % w, self.modrow[l, w:w + 1, (3 * si + 1) * D:(3 * si + 2) * D])
            S.op("dve", lambda e, w=w: e.scalar_tensor_tensor(out=A[:, w, :], in0=sc[:, w, :], scalar=1.0, in1=g[:],
                                                             op0=ALU.add, op1=ALU.mult),
                 [tag + "sc%d" % w, tag + "g"], [tag + "A%d" % w])
        return A, B

    def gate_bc(self, S, l, si, tag, half):
        G = S.sb(tag + "G", [128, 2, D], F32)
        for w in range(2):
            S.load_bc(G[:, w, :], tag + "G%d" % w, self.modrow[l, w:w + 1, (3 * si + 2) * D:(3 * si + 3) * D])
            if half:
                S.op("pool", lambda e, w=w: e.tensor_scalar(out=G[:, w, :], in0=G[:, w, :], scalar1=0.5, scalar2=None,
                                                            op0=ALU.mult),
                     [tag + "G%d" % w], [tag + "G%d" % w])
        return G

    def norm_front(self, S, R, xt, xk, nsub, who, A, B, tagA, tagB, ident):
        ss, ssk = R["ss"].next()
        rs, rsk = R["rs"].next()
        xn, xnk = R["xn"].next()
        xh, xhk = R["xh"].next()
        junk = R["junk"]
        S.op("dve", lambda e: e.memset(ss[:], 0.0), [], [ssk])
        for i in range(nsub):
            S.op("act", lambda e, i=i: e.activation(out=junk[:], in_=xt[:, i, :], func=AF.Square,
                                                    accum_out=ss[:, i:i + 1]),
                 [xk, ssk], ["junk", ssk])
        S.op("dve", lambda e: e.tensor_scalar(out=rs[:, 0:nsub], in0=ss[:, 0:nsub], scalar1=1.0 / D, scalar2=EPS,
                                              op0=ALU.mult, op1=ALU.add), [ssk], [rsk])
        S.op("act", lambda e: e.activation(out=rs[:, 0:nsub], in_=rs[:, 0:nsub], func=AF.Sqrt), [rsk], [rsk])
        S.op("dve", lambda e: e.reciprocal(out=rs[:, 0:nsub], in_=rs[:, 0:nsub]), [rsk], [rsk])
        for i in range(nsub):
            S.op("act", lambda e, i=i: e.activation(out=xn[:, i, :], in_=xt[:, i, :], func=AF.Copy,
                                                    scale=rs[:, i:i + 1]), [xk, rsk], [xnk])
        for k in range(8):
            pT, pTk = R["pT"].next()
            for i in range(nsub):
                S.op("pe", lambda e, pT=pT, i=i, k=k: e.transpose(out=pT[:, i, :], in_=xn[:, i, k * 128:(k + 1) * 128],
                                                                  identity=ident[:]),
                     [xnk, "c_identb"], [pTk])
            S.op("dve", lambda e, pT=pT, k=k: e.tensor_scalar(
                out=xh[:, k, 0:nsub * 128], in0=pT[:, 0:nsub, :].rearrange("p i t -> p (i t)"), scalar1=A[:, who, k:k + 1],
                scalar2=B[:, who, k:k + 1], op0=ALU.mult, op1=ALU.add),
                 [pTk, tagA + "A%d" % who, tagB + "B%d" % who], [xhk])
        return xh, xhk

    def front_rings(self, S, nx=2):
        return {
            "xt": Ring(S, "xt", [128, 4, D], F32, nx),
            "ss": Ring(S, "ss", [128, 4], F32, 2),
            "rs": Ring(S, "rs", [128, 4], F32, 2),
            "xn": Ring(S, "xn", [128, 4, D], BF16, 2),
            "xh": Ring(S, "xh", [128, 8, 512], BF16, 2),
            "pT": Ring(S, "pT", [128, 4, 128], BF16, 2, psum=True),
            "junk": S.sb("junk", [128, D], F32),
        }

    def load_x(self, S, R, tile):
        c0, nsub = tile
        xt, xk = R["xt"].next()
        rk = [("xres", c) for c in range(c0, c0 + nsub)]
        S.dma("sp", xt[:, 0:nsub, :], self.xres[c0 * 128:(c0 + nsub) * 128, :].rearrange("(i p) d -> p i d", p=128),
              rk, [xk])
        return xt, xk

    def stage_ffn(self, l, f, tiles, extra_casts=(), final_norm=False):
        S = Stage(self, "ffn%d%d" % (l, f))
        si = 0 if f == 0 else 2
        for (dst, src, nrows, key) in extra_casts:
            self.cast_rows(S, dst, src, nrows, key)
        C = self.consts(S, ["identb"])
        ident = C["identb"]
        A, B = self.mod_cols(S, l, si, "m")
        G = self.gate_bc(S, l, si, "m", True)
        R = self.front_rings(S)
        w2 = S.sb("w2", [128, NJ, D], BF16)
        S.dma("sp", w2[:], self.w2b[l, f].rearrange("(j p) n -> p j n", p=128), [], ["w2"])
        H = S.sb("H", [128, NJ, 512], BF16)
        wring = Ring(S, "wi", [128, 8, 256], BF16, 4)
        sar = Ring(S, "sa", [128, 512], F32, 2)
        tmr = Ring(S, "tm", [128, 512], F32, 2)
        par = Ring(S, "pa", [128, 512], F32, 2, psum=True)
        pbr = Ring(S, "pb", [128, 512], F32, 2, psum=True)
        por = Ring(S, "po", [128, 512], F32, 2, psum=True)
        w1v = self.w1b[l, f].rearrange("(k p) c -> p k c", p=128)
        if final_norm:
            fg = S.sb("fg", [128, D], F32)
            S.load_bc(fg[:], "fg", self.final_g)
            fss, frs = S.sb("fss", [128, 4], F32), S.sb("frs", [128, 4], F32)
        def in_chunk(j, xh, xhk, ntok):
            wt, wk = wring.next()
            S.dma("sp", wt[:, :, 0:128], w1v[:, :, j * 128:(j + 1) * 128], [], [(wk, 0)])
            S.dma("sp", wt[:, :, 128:256], w1v[:, :, DFF + j * 128:DFF + (j + 1) * 128], [], [(wk, 128)])
            pa, pak = par.next()
            pb, pbk = pbr.next()
            for (pp, ppk, off) in ((pa, pak, 0), (pb, pbk, 128)):
                for k in range(8):
                    S.op("pe", lambda e, pp=pp, k=k, off=off: e.matmul(
                        pp[:, 0:ntok], lhsT=wt[:, k, off:off + 128], rhs=xh[:, k, 0:ntok],
                        start=(k == 0), stop=(k == 7)), [(wk, off), xhk], [ppk])
            sa, sak = sar.next()
            S.op("act", lambda e: e.activation(out=sa[:, 0:ntok], in_=pa[:, 0:ntok], func=AF.Silu), [pak], [sak])
            S.op("dve", lambda e: e.tensor_tensor(out=H[:, j, 0:ntok], in0=sa[:, 0:ntok], in1=pb[:, 0:ntok],
                                                  op=ALU.mult), [sak, pbk], ["H"])

        def out_chunk(i, n, xt, xk, who):
            po, pok = por.next()
            for j in range(NJ):
                S.op("pe", lambda e, j=j: e.matmul(
                    po[:], lhsT=H[:, j, i * 128:(i + 1) * 128], rhs=w2[:, j, n * 512:(n + 1) * 512],
                    start=(j == 0), stop=(j == NJ - 1)), ["H", "w2"], [pok])
            tm, tmk = tmr.next()
            S.op("dve", lambda e: e.tensor_tensor(out=tm[:], in0=po[:], in1=G[:, who, n * 512:(n + 1) * 512],
                                                  op=ALU.mult), [pok, "mG%d" % who], [tmk])
            S.op("pool", lambda e: e.tensor_tensor(out=xt[:, i, n * 512:(n + 1) * 512],
                                                   in0=xt[:, i, n * 512:(n + 1) * 512], in1=tm[:], op=ALU.add),
                 [tmk, xk], [xk])

        def fin_sub(i, xt, xk):
            S.op("act", lambda e: e.activation(out=R["junk"][:], in_=xt[:, i, :], func=AF.Square,
                                               accum_out=fss[:, i:i + 1]), [xk, "fss"], ["junk", "fss"])

        def fin_scale(i, xt, xk):
            S.op("dve", lambda e: e.scalar_tensor_tensor(out=xt[:, i, :], in0=xt[:, i, :], scalar=frs[:, i:i + 1],
                                                         in1=fg[:], op0=ALU.mult, op1=ALU.mult),
                 [xk, "frs", "fg"], [xk])

        def front(tile, xt, xk):
            c0, nsub = tile
            who = 1 if c0 >= 32 else 0
            xh, xhk = self.norm_front(S, R, xt, xk, nsub, who, A, B, "m", "m", ident)
            return (tile, xt, xk, who, xh, xhk)

        def part_in(ctx):
            (c0, nsub), xt, xk, who, xh, xhk = ctx
            for j in range(NJ):
                in_chunk(j, xh, xhk, nsub * 128)

        def part_out(ctx):
            (c0, nsub), xt, xk, who, xh, xhk = ctx
            for i in range(nsub):
                for n in range(2):
                    out_chunk(i, n, xt, xk, who)
            dst = self.out if final_norm else self.xres
            dkey = "out" if final_norm else "xres"
            if final_norm:
                S.op("dve", lambda e: e.memset(fss[:], 0.0), [], ["fss"])
                for i in range(nsub):
                    fin_sub(i, xt, xk)
                S.op("dve", lambda e: e.tensor_scalar(out=frs[:], in0=fss[:], scalar1=1.0 / D, scalar2=EPS,
                                                      op0=ALU.mult, op1=ALU.add), ["fss"], ["frs"])
                S.op("act", lambda e: e.activation(out=frs[:], in_=frs[:], func=AF.Sqrt), ["frs"], ["frs"])
                S.op("dve", lambda e: e.reciprocal(out=frs[:], in_=frs[:]), ["frs"], ["frs"])
                for i in range(nsub):
                    fin_scale(i, xt, xk)
            S.store("pool", dst[c0 * 128:(c0 + nsub) * 128, :].rearrange("(i p) d -> p i d", p=128),
                    xt[:, 0:nsub, :], [xk], [(dkey, c) for c in range(c0, c0 + nsub)], skey=("st", xk))

        xt0, xk0 = self.load_x(S, R, tiles[0])
        cur = front(tiles[0], xt0, xk0)
        for ti, tile in enumerate(tiles):
            nx = None
            if ti + 1 < len(tiles):
                nx = self.load_x(S, R, tiles[ti + 1])
            part_in(cur)
            nxt_ctx = front(tiles[ti + 1], nx[0], nx[1]) if nx is not None else None
            part_out(cur)
            cur = nxt_ctx
        S.finish()

    def load_w(self, S, W, key, src, ncols):
        v = src.rearrange("(k p) c -> p k c", p=128)
        for k in range(8):
            S.dma("sp", W[:, k, 0:ncols], v[:, k, :], [], [(key, k)])
        return [(key, k) for k in range(8)]

    def stage_even_inproj(self):
        S = Stage(self, "ein")
        C = self.consts(S, ["identb", "onesb"])
        ident, onesb = C["identb"], C["onesb"]
        A, B = self.mod_cols(S, 0, 1, "m")
        R = self.front_rings(S)
        W = S.sb("W", [128, 8, EVEN_X], BF16)
        wkeys = self.load_w(S, W, "W", self.ewin_b, EVEN_X)
        gb = S.sb("gb", [128, 16], F32)
        S.load_bc(gb[:], "gb", self.gate_b)
        nacc = S.sb("nacc", [128, 8], F32)
        S.op("dve", lambda e: e.memset(nacc[:], 0.0), [], ["nacc"])
        rcr = Ring(S, "rc", [128, 512], F32, 2)
        rsr = Ring(S, "rs_", [128, 512], F32, 2)
        t1r = Ring(S, "t1", [128, 512], F32, 2)
        t2r = Ring(S, "t2", [128, 512], F32, 2)
        obr = Ring(S, "ob", [128, 512], BF16, 4)
        ofr = Ring(S, "of", [128, 512], F32, 3)
        sqr = Ring(S, "sq", [128, 512], BF16, 2)
        mxr = Ring(S, "mx", [128, 1], F32, 2)
        pjr = Ring(S, "pj", [128, 512], F32, 4, psum=True)
        pnr = Ring(S, "pn", [128, 512], F32, 2, psum=True)

        def proj_fm(col, xh, xhk, ntok):
            pj, pjk = pjr.next()
            for k in range(8):
                S.op("pe", lambda e, k=k: e.matmul(pj[:, 0:ntok], lhsT=W[:, k, col:col + 128], rhs=xh[:, k, 0:ntok],
                                                   start=(k == 0), stop=(k == 7)), [("W", k), xhk], [pjk])
            return pj, pjk

        def proj_tm(col, n, i, xh, xhk):
            pj, pjk = pjr.next()
            for k in range(8):
                S.op("pe", lambda e, k=k: e.matmul(pj[:, 0:n], lhsT=xh[:, k, i * 128:(i + 1) * 128],
                                                   rhs=W[:, k, col:col + n], start=(k == 0), stop=(k == 7)),
                     [("W", k), xhk], [pjk])
            return pj, pjk

        def rope_chunk(g, col, pcol, dst, ncol, xh, xhk, ntok, tok0, lat, rc, rck, rs, rsk):
            p1, p1k = proj_fm(col + g * 128, xh, xhk, ntok)
            ob, obk = obr.next()
            if lat:
                p2, p2k = proj_fm(pcol + g * 128, xh, xhk, ntok)
                t1, t1k = t1r.next()
                t2, t2k = t2r.next()
                S.op("dve", lambda e: e.tensor_tensor(out=t1[:, 0:ntok], in0=p1[:, 0:ntok], in1=rc[:, 0:ntok],
                                                      op=ALU.mult), [p1k, rck], [t1k])
                S.op("dve", lambda e: e.tensor_tensor(out=t2[:, 0:ntok], in0=p2[:, 0:ntok], in1=rs[:, 0:ntok],
                                                      op=ALU.mult), [p2k, rsk], [t2k])
                S.op("pool", lambda e: e.tensor_tensor(out=ob[:, 0:ntok], in0=t1[:, 0:ntok], in1=t2[:, 0:ntok],
                                                       op=ALU.add), [t1k, t2k], [obk])
            else:
                S.op("act", lambda e: e.activation(out=ob[:, 0:ntok], in_=p1[:, 0:ntok], func=AF.Copy), [p1k], [obk])
            sq, sqk = sqr.next()
            S.op("act", lambda e: e.activation(out=sq[:, 0:ntok], in_=ob[:, 0:ntok], func=AF.Square), [obk], [sqk])
            pn, pnk = pnr.next()
            S.op("pe", lambda e: e.matmul(pn[:, 0:ntok], lhsT=onesb[:], rhs=sq[:, 0:ntok], start=True, stop=True),
                 [sqk, "c_onesb"], [pnk])
            mx, mxk = mxr.next()
            S.op("dve", lambda e: e.reduce_max(out=mx[:], in_=pn[:, 0:ntok], axis=AX.X), [pnk], [mxk])
            S.op("dve", lambda e: e.tensor_tensor(out=nacc[:, ncol:ncol + 1], in0=nacc[:, ncol:ncol + 1], in1=mx[:],
                                                  op=ALU.max), [mxk, "nacc"], ["nacc"])
            S.store("pool", dst[g * 128:(g + 1) * 128, tok0:tok0 + ntok], ob[:, 0:ntok], [obk], [("fm", id(dst), g, tok0)],
                    skey=("st", obk))

        def mqk_chunk(g, xh, xhk, ntok, tok0):
            p1, p1k = proj_fm(1536 + g * 128, xh, xhk, ntok)
            of, ofk = ofr.next()
            S.op("act", lambda e: e.activation(out=of[:, 0:ntok], in_=p1[:, 0:ntok], func=AF.Copy), [p1k], [ofk])
            S.store("pool", self.MQK[g * 128:(g + 1) * 128, tok0:tok0 + ntok], of[:, 0:ntok], [ofk], [("mqk", g, tok0)],
                    skey=("st", ofk))

        def tm_sub(i, xh, xhk, tok0):
            r0 = tok0 + i * 128
            for (col, dst, bf) in ((1024, self.Vs, True), (2048, self.MV, True), (2560, self.OPRE, False)):
                pj, pjk = proj_tm(col, 512, i, xh, xhk)
                if bf:
                    ob, obk = obr.next()
                    S.op("act", lambda e, ob=ob, pj=pj: e.activation(out=ob[:], in_=pj[:], func=AF.Copy), [pjk], [obk])
                    S.store("pool", dst[r0:r0 + 128, :], ob[:], [obk], [("tm", col, r0)], skey=("st", obk))
                else:
                    of, ofk = ofr.next()
                    S.op("act", lambda e, of=of, pj=pj: e.activation(out=of[:], in_=pj[:], func=AF.Copy), [pjk], [ofk])
                    S.store("pool", dst[r0:r0 + 128, :], of[:], [ofk], [("tm", col, r0)], skey=("st", ofk))
            pj, pjk = proj_tm(3072, 16, i, xh, xhk)
            of, ofk = ofr.next()
            S.op("dve", lambda e: e.tensor_tensor(out=of[:, 0:16], in0=pj[:, 0:16], in1=gb[:], op=ALU.add),
                 [pjk, "gb"], [ofk])
            S.store("pool", self.GATE[r0:r0 + 128, :], of[:, 0:16], [ofk], [("tmg", r0)], skey=("st", ofk))

        def front(tile, xt, xk):
            c0, nsub = tile
            lat = c0 < 32
            who = 0 if lat else 1
            ntok, tok0 = nsub * 128, c0 * 128
            rc = rck = rs = rsk = None
            if lat:
                rc, rck = rcr.next()
                rs, rsk = rsr.next()
                S.dma("sp", rc[:, 0:ntok], self.rope_c[:, tok0:tok0 + ntok], [], [rck])
                S.dma("sp", rs[:, 0:ntok], self.rope_s[:, tok0:tok0 + ntok], [], [rsk])
            xh, xhk = self.norm_front(S, R, xt, xk, nsub, who, A, B, "m", "m", ident)
            return (nsub, lat, ntok, tok0, rc, rck, rs, rsk, xh, xhk)

        def part1(ctx):
            nsub, lat, ntok, tok0, rc, rck, rs, rsk, xh, xhk = ctx
            for g in range(4):
                rope_chunk(g, 0, 3088, self.QT, g, xh, xhk, ntok, tok0, lat, rc, rck, rs, rsk)
            for g in range(4):
                rope_chunk(g, 512, 3600, self.KT, 4 + g, xh, xhk, ntok, tok0, lat, rc, rck, rs, rsk)

        def part2(ctx):
            nsub, lat, ntok, tok0, rc, rck, rs, rsk, xh, xhk = ctx
            for g in range(4):
                mqk_chunk(g, xh, xhk, ntok, tok0)
            for i in range(nsub):
                tm_sub(i, xh, xhk, tok0)

        x0 = self.load_x(S, R, ALL_TILES[0])
        cur = front(ALL_TILES[0], x0[0], x0[1])
        for ti, tile in enumerate(ALL_TILES):
            nx = self.load_x(S, R, ALL_TILES[ti + 1]) if ti + 1 < len(ALL_TILES) else None
            part1(cur)
            nxt_ctx = front(ALL_TILES[ti + 1], nx[0], nx[1]) if nx is not None else None
            part2(cur)
            cur = nxt_ctx
        S.store("sp", self.nrm[:, 0:8], nacc[:], ["nacc"], [("nrm", 0)])
        S.finish()

    def stage_attn(self, heads=(0, 1, 2, 3), tiles=None):
        tiles = ALL_TILES if tiles is None else tiles
        S = Stage(self, "att")
        lam_init = 0.8 - 0.6 * math.exp(-0.3 * 0)
        nr = S.sb("nr", [128, 8], F32)
        negM = S.sb("negM", [128, 4], F32)
        S.dma("sp", nr[:], self.nrm[:, 0:8], [], ["nr"])
        S.op("dve", lambda e: e.tensor_tensor(out=negM[:], in0=nr[:, 0:4], in1=nr[:, 4:8], op=ALU.mult), ["nr"], ["negM"])
        S.op("act", lambda e: e.activation(out=negM[:], in_=negM[:], func=AF.Sqrt), ["negM"], ["negM"])
        S.op("dve", lambda e: e.tensor_scalar(out=negM[:], in0=negM[:], scalar1=-0.125, scalar2=None, op0=ALU.mult),
             ["negM"], ["negM"])
        dl = S.sb("dl", [1, 256], F32)
        lw = S.sb("lw", [1, 8], F32)
        S.dma("sp", dl[:], self.diff_lambda, [], ["dl"])
        S.op("dve", lambda e: e.memset(lw[:], 0.0), [], ["lw"])
        S.op("dve", lambda e: e.tensor_tensor(out=dl[:, 0:64], in0=dl[:, 0:64], in1=dl[:, 64:128], op=ALU.mult),
             ["dl"], ["dl"])
        S.op("dve", lambda e: e.tensor_tensor(out=dl[:, 128:192], in0=dl[:, 128:192], in1=dl[:, 192:256], op=ALU.mult),
             ["dl"], ["dl"])
        S.op("dve", lambda e: e.reduce_sum(out=lw[:, 0:1], in_=dl[:, 0:64], axis=AX.X), ["dl"], ["lw"])
        S.op("dve", lambda e: e.reduce_sum(out=lw[:, 1:2], in_=dl[:, 128:192], axis=AX.X), ["dl", "lw"], ["lw"])
        S.op("act", lambda e: e.activation(out=lw[:, 2:4], in_=lw[:, 0:2], func=AF.Exp), ["lw"], ["lw"])
        S.op("dve", lambda e: e.tensor_tensor(out=lw[:, 4:5], in0=lw[:, 3:4], in1=lw[:, 2:3], op=ALU.subtract),
             ["lw"], ["lw"])
        S.op("dve", lambda e: e.tensor_scalar(out=lw[:, 5:6], in0=lw[:, 4:5], scalar1=-lam_init, scalar2=None,
                                              op0=ALU.add), ["lw"], ["lw"])
        S.dma("sp", self.nrm[0:1, 8:9], lw[:, 5:6], ["lw"], [("nrm", 8)])
        dgs = S.sb("dgs", [128, 128], F32)
        S.load_bc(dgs[:], "dgs", self.diff_norm_g)
        S.op("pool", lambda e: e.tensor_scalar(out=dgs[:], in0=dgs[:], scalar1=1.0 - lam_init, scalar2=None,
                                               op0=ALU.mult), ["dgs"], ["dgs"])
        identf = self.consts(S, ["ident"])["ident"]
        ktr = Ring(S, "kt", [128, TT], BF16, 2)
        qtr = Ring(S, "qt", [128, TT], BF16, 2)
        vr = Ring(S, "vh", [128, NCH, 128], BF16, 2)
        Esel = S.sb("Esel", [128, 2, 2], BF16)
        S.op("pool", lambda e: e.memset(Esel[:], 0.0), [], ["Esel"])
        for c in range(2):
            S.op("pool", lambda e, c=c: e.memset(Esel[:, c, c:c + 1], 1.0), ["Esel"], ["Esel"])
        lamc = S.sb("lamc", [2, 1], F32)
        S.op("pool", lambda e: e.memset(lamc[:], 1.0), [], ["lamc"])
        S.dma("sp", lamc[1:2, :], self.nrm[0:1, 8:9], [("nrm", 8), "lamc"], ["lamc"])
        ptr = Ring(S, "pt", [128, 512], BF16, 6)
        spr = Ring(S, "sp", [128, 512], F32, 4, psum=True)
        oTp = [(S.ps("oT%d" % c, [128, 512], F32), "oT%d" % c) for c in range(2)]
        Zp = S.ps("Zp", [2, 512], F32)
        tpp = S.ps("tp", [128, 2, 132], F32)
        oTs = Ring(S, "oTs", [128, 2, 512], F32, 2)
        Zs = Ring(S, "Zs", [2, 512], F32, 2)
        mor = Ring(S, "mo", [128, 4, 128], BF16, 2)
        sm = Ring(S, "sm", [128, 8], F32, 4)
        otr = Ring(S, "ot", [128, 128], F32, 3)
        junk = S.sb("junk", [128, 128], F32)

        def s_mm(c, kc, kt, ktk, qt, qtk, q0, nq):
            sp, spk = spr.next()
            S.op("pe", lambda e: e.matmul(sp[:, 0:nq], lhsT=kt[c * 64:(c + 1) * 64, kc * 128:(kc + 1) * 128],
                                          rhs=qt[c * 64:(c + 1) * 64, q0:q0 + nq], start=True, stop=True),
                 [ktk, qtk], [spk])
            return sp, spk

        def p_mm(h, c, kc, first, last, sp, spk, vt, vk, nq):
            pt, ptk = ptr.next()
            S.op("act", lambda e: e.activation(out=pt[:, 0:nq], in_=sp[:, 0:nq], func=AF.Exp, bias=negM[:, h:h + 1],
                                               scale=0.125), [spk, "negM"], [ptk])
            oT, oTk = oTp[c]
            S.op("pe", lambda e: e.matmul(oT[:, 0:nq], lhsT=vt[:, kc, :], rhs=pt[:, 0:nq], start=first, stop=last),
                 [ptk, vk], [oTk])
            S.op("pe", lambda e: e.matmul(Zp[:, 0:nq], lhsT=Esel[:, c, :], rhs=pt[:, 0:nq],
                                          start=(first and c == 0), stop=(last and c == 1)), [ptk, "Esel"], ["Zp"])

        def fin_sub(h, qs, ots, otsk, zs, zsk, mo, mok):
            q_ = slice(qs * 128, (qs + 1) * 128)
            for c in range(2):
                S.op("pe", lambda e, c=c: e.transpose(out=tpp[:, c, 0:128], in_=ots[:, c, q_], identity=identf[:]),
                     otsk + ["c_identf"], ["tp"])
            S.op("pe", lambda e: e.transpose(out=tpp[:, 0, 128:130], in_=zs[0:2, q_], identity=identf[0:2, 0:2]),
                 [zsk, "c_identf"], ["tp"])
            s_, sk = sm.next()
            ot, otk = otr.next()
            S.op("dve", lambda e: e.tensor_copy(out=s_[:, 0:2], in_=tpp[:, 0, 128:130]), ["tp"], [sk])
            S.op("act", lambda e: e.activation(out=ot[:], in_=tpp[:, 0, 0:128], func=AF.Copy, scale=s_[:, 0:1]),
                 ["tp", sk], [otk])
            S.op("dve", lambda e: e.scalar_tensor_tensor(out=ot[:], in0=tpp[:, 1, 0:128], scalar=s_[:, 1:2], in1=ot[:],
                                                         op0=ALU.mult, op1=ALU.add), ["tp", sk, otk], [otk])
            S.op("dve", lambda e: e.memset(s_[:, 3:4], 0.0), [sk], [sk])
            S.op("act", lambda e: e.activation(out=junk[:], in_=ot[:], func=AF.Square, accum_out=s_[:, 3:4]),
                 [otk, sk], ["junk", sk])
            S.op("dve", lambda e: e.tensor_scalar(out=s_[:, 4:5], in0=s_[:, 3:4], scalar1=1.0 / 128, scalar2=EPS,
                                                  op0=ALU.mult, op1=ALU.add), [sk], [sk])
            S.op("act", lambda e: e.activation(out=s_[:, 4:5], in_=s_[:, 4:5], func=AF.Ln), [sk], [sk])
            S.op("act", lambda e: e.activation(out=s_[:, 5:6], in_=s_[:, 4:5], func=AF.Exp, scale=-0.5), [sk], [sk])
            S.op("dve", lambda e: e.scalar_tensor_tensor(out=mo[:, qs, :], in0=ot[:], scalar=s_[:, 5:6], in1=dgs[:],
                                                         op0=ALU.mult, op1=ALU.mult), [otk, sk, "dgs"], [mok])

        def qtile(h, tile, kt, ktk, qt, qtk, vt, vk):
            c0, nsub = tile
            q0, nq = c0 * 128, nsub * 128
            keys = list(range(NCH)) if c0 < 32 else [32, 33]
            nxt = [s_mm(c, keys[0], kt, ktk, qt, qtk, q0, nq) for c in range(2)]
            for ki, kc in enumerate(keys):
                cur = nxt
                if ki + 1 < len(keys):
                    nxt = [s_mm(c, keys[ki + 1], kt, ktk, qt, qtk, q0, nq) for c in range(2)]
                for c in range(2):
                    p_mm(h, c, kc, ki == 0, ki == len(keys) - 1, cur[c][0], cur[c][1], vt, vk, nq)
            ots, otsk = oTs.next()
            zs, zsk = Zs.next()
            S.op("act", lambda e: e.activation(out=ots[:, 0, 0:nq], in_=oTp[0][0][:, 0:nq], func=AF.Copy), ["oT0"],
                 [(otsk, 0)])
            S.op("dve", lambda e: e.tensor_copy(out=ots[:, 1, 0:nq], in_=oTp[1][0][:, 0:nq]), ["oT1"], [(otsk, 1)])
            S.op("dve", lambda e: e.reciprocal(out=zs[:, 0:nq], in_=Zp[:, 0:nq]), ["Zp"], [zsk])
            S.op("dve", lambda e: e.tensor_scalar(out=zs[:, 0:nq], in0=zs[:, 0:nq], scalar1=lamc[:, 0:1], scalar2=None,
                                                  op0=ALU.mult), [zsk, "lamc"], [zsk])
            mo, mok = mor.next()
            for qs in range(nsub):
                fin_sub(h, qs, ots, [(otsk, 0), (otsk, 1)], zs, zsk, mo, mok)
            S.store("pool", self.merged[q0:q0 + nsub * 128, h * 128:(h + 1) * 128].rearrange("(i p) e -> p i e", p=128),
                    mo[:, 0:nsub, :], [mok], [("mrg", h, c0)], skey=("st", mok))

        def head(h):
            kt, ktk = ktr.next()
            qt, qtk = qtr.next()
            vt, vk = vr.next()
            S.dma("sp", kt[:], self.KT[h * 128:(h + 1) * 128, :], [], [ktk])
            S.dma("sp", qt[:], self.QT[h * 128:(h + 1) * 128, :], [], [qtk])
            S.dma("sp", vt[:], self.Vs[:, h * 128:(h + 1) * 128].rearrange("(c p) e -> p c e", p=128), [], [vk])
            for tile in tiles:
                qtile(h, tile, kt, ktk, qt, qtk, vt, vk)

        for h in heads:
            head(h)
        S.finish()

    def stage_mlprep(self, chunks=None):
        chunks = list(range(NCH)) if chunks is None else chunks
        S = Stage(self, "mlp")
        C = self.consts(S, ["identb"])
        ident = C["identb"]
        cw = S.sb("cw", [128, 3, 4], F32)
        cb = S.sb("cb", [128, 4], F32)
        for j in range(3):
            S.load_cols(cw[:, j, :], ("cw", j), self.conv_w[j:j + 1, :])
        S.load_cols(cb[:], "cb", self.conv_b)
        xqr = Ring(S, "xq", [128, 4, 130], F32, 3)
        for (xq, xqk) in xqr.bufs:
            S.op("pool", lambda e, xq=xq: e.memset(xq[:], 0.0), [], [xqk])
        acr = Ring(S, "ac", [128, 128], F32, 3)
        qkr = Ring(S, "qk", [128, 4, 128], BF16, 3)
        ktr = Ring(S, "ktm", [128, 256], BF16, 3)
        pkr = Ring(S, "pk", [128, 256], BF16, 2, psum=True)
        mqv = self.MQK.rearrange("(g p) t -> p g t", p=128)
        mqbv = self.MQKB.rearrange("(g p) t -> p g t", p=128)

        def conv_g(g, xq, xqk, qk, qkk):
            ac, ack = acr.next()
            S.op("dve", lambda e: e.tensor_scalar(out=ac[:], in0=xq[:, g, 0:128], scalar1=cw[:, 0, g:g + 1], scalar2=None,
                                                  op0=ALU.mult), [xqk, ("cw", 0)], [ack])
            for j in (1, 2):
                S.op("dve", lambda e, j=j: e.scalar_tensor_tensor(out=ac[:], in0=xq[:, g, j:j + 128],
                                                                  scalar=cw[:, j, g:g + 1], in1=ac[:], op0=ALU.mult,
                                                                  op1=ALU.add), [xqk, ("cw", j), ack], [ack])
            S.op("act", lambda e: e.activation(out=qk[:, g, :], in_=ac[:], func=AF.Silu, bias=cb[:, g:g + 1]),
                 [ack, "cb"], [(qkk, g)])

        def chunk(c):
            t0 = c * 128
            lo = t0 - 1 if c not in (0, 32) else t0
            hi = t0 + 129 if c not in (31, 33) else t0 + 128
            xq, xqk = xqr.next()
            if lo == t0:
                S.op("pool", lambda e: e.memset(xq[:, :, 0:1], 0.0), [], [xqk])
            if hi == t0 + 128:
                S.op("pool", lambda e: e.memset(xq[:, :, 129:130], 0.0), [], [xqk])
            S.dma("sp", xq[:, :, 1 - (t0 - lo):1 - (t0 - lo) + (hi - lo)], mqv[:, :, lo:hi], [], [xqk])
            qk, qkk = qkr.next()
            for g in range(4):
                conv_g(g, xq, xqk, qk, qkk)
            S.store("pool", mqbv[:, :, t0:t0 + 128], qk[:], [(qkk, g) for g in range(4)], [("mqkb", c)],
                    skey=("st", qkk))
            pk, pkk = pkr.next()
            for g in (2, 3):
                S.op("pe", lambda e, g=g: e.transpose(out=pk[:, (g - 2) * 128:(g - 1) * 128], in_=qk[:, g, :],
                                                      identity=ident[:]), [(qkk, g), "c_identb"], [pkk])
            kt, ktk = ktr.next()
            S.op("dve", lambda e: e.tensor_copy(out=kt[:], in_=pk[:]), [pkk], [ktk])
            S.store("pool", self.MKT[t0:t0 + 128, :], kt[:], [ktk], [("mkt", c)], skey=("st", ktk))

        for c in chunks:
            chunk(c)
        gr = Ring(S, "g", [128, NCH, 8], F32, 1)
        g_, gk = gr.next()
        gv = self.GATE.rearrange("(c p) k -> p c k", p=128)
        S.dma("sp", g_[:], gv[:, :, 8:16], [], [gk])
        S.op("act", lambda e: e.activation(out=g_[:], in_=g_[:], func=AF.Exp, scale=-1.0), [gk], [gk])
        S.op("act", lambda e: e.activation(out=g_[:], in_=g_[:], func=AF.Ln, bias=1.0), [gk], [gk])
        S.op("dve", lambda e: e.tensor_scalar(out=g_[:], in0=g_[:], scalar1=-1.0, scalar2=None, op0=ALU.mult), [gk], [gk])
        S.store("sp", gv[:, :, 8:16], g_[:], [gk], [("lf", 0)])
        S.finish()

    def stage_mlstm(self, order_f=None, order_b=None):
        order_f = [32, 33] + list(range(32)) if order_f is None else order_f
        order_b = [33, 32] + list(range(31, -1, -1)) if order_b is None else order_b
        S = Stage(self, "mls")
        C = self.consts(S, ["U", "L", "ones"])
        Uf, Lf, onesf = C["U"], C["L"], C["ones"]
        mg = S.sb("mg", [128, 512], F32)
        S.load_bc(mg[:], "mg", self.ml_norm_g)
        hsum = S.sb("hsum", [128, NCH, 512], F32)
        Cst = S.sb("Cst", [128, 2, 132], F32)
        Cbf = S.sb("Cbf", [128, 2, 132], BF16)
        qkr = Ring(S, "qk", [128, 4, 128], BF16, 3)
        ktr = Ring(S, "ktm", [128, 256], BF16, 3)
        var = Ring(S, "va", [128, 4, 132], BF16, 3)
        for (va, vak) in var.bufs:
            S.op("pool", lambda e, va=va: e.memset(va[:, :, 128:129], 1.0), [], [(vak, "one")])
        gtr = Ring(S, "gt", [128, 16], F32, 4)
        csr = Ring(S, "cs", [128, 24], F32, 4)
        lfrr = Ring(S, "lfr", [128, 128], F32, 6)
        dmr = Ring(S, "dm", [128, 128], F32, 6)
        er = Ring(S, "E", [128, 128], F32, 6)
        epr = Ring(S, "Ep", [128, 128], F32, 6)
        ebr = Ring(S, "eb", [128, 128], F32, 6)
        qtr = Ring(S, "qtl", [128, 128], BF16, 6)
        atr = Ring(S, "AT", [128, 128], BF16, 6)
        khr = Ring(S, "kh", [128, 128], BF16, 6)
        rr = Ring(S, "r", [128, 2], F32, 8)
        bankX = S.ps("pDX", [128, 512], F32)
        pcs = bankX[:, 256:264]
        pBr = PRing(S, "pB", 128, 2, 1)
        pSr = PRing(S, "pS", 128, 2, 1)
        par = PRing(S, "pacc", 132, 2, 1)
        pDr = PRing(S, "pD", 132, 1, 1)
        pDr.bufs.append((bankX[:, 0:132], "pcs"))
        mqbv = self.MQKB.rearrange("(g p) t -> p g t", p=128)
        LN8 = math.log(0.125)

        def head(d, c, h, Mx, Mk, qk, qkk, kt, ktk, va, vak, cs, csk):
            rows = slice((h % 2) * 64, (h % 2) * 64 + 64)
            hp = h // 2
            lfr, lfrk = lfrr.next()
            S.op("dve", lambda e: e.tensor_scalar(out=lfr[:], in0=onesf[:], scalar1=cs[:, 16 + h:17 + h], scalar2=None,
                                                  op0=ALU.mult), ["c_onesf", csk], [lfrk])
            pB, pBk = pBr.next()
            yield
            S.op("pe", lambda e: e.matmul(pB[:], lhsT=lfr[:], rhs=Mx[:], start=True, stop=True), [lfrk, Mk], [pBk])
            dm, dmk = dmr.next()
            yield
            S.op("dve", lambda e: e.tensor_scalar(out=dm[:], in0=pB[:], scalar1=cs[:, h:h + 1], scalar2=0.0,
                                                  op0=ALU.subtract, op1=ALU.min), [pBk, csk], [dmk])
            E, Ek = er.next()
            yield
            S.op("act", lambda e: e.activation(out=E[:], in_=dm[:], func=AF.Exp, bias=cs[:, 12 + h:13 + h]),
                 [dmk, csk], [Ek])
            Ep, Epk = epr.next()
            yield
            S.op("pool", lambda e: e.tensor_tensor(out=Ep[:], in0=E[:], in1=Mx[:], op=ALU.mult), [Ek, Mk], [Epk])
            eb, ebk = ebr.next()
            yield
            S.op("act", lambda e: e.activation(out=eb[rows, :], in_=pB[rows, :], func=AF.Exp), [pBk], [ebk])
            qt, qtk = qtr.next()
            yield
            S.op("dve", lambda e: e.scalar_tensor_tensor(out=qt[rows, :], in0=qk[rows, hp, :], scalar=0.125,
                                                         in1=eb[rows, :], op0=ALU.mult, op1=ALU.mult),
                 [(qkk, 0), ebk], [qtk])
            pS, pSk = pSr.next()
            yield
            S.op("pe", lambda e: e.matmul(pS[:], lhsT=qk[rows, 2 + hp, :], rhs=qk[rows, hp, :], start=True, stop=True),
                 [(qkk, 0)], [pSk])
            AT, ATk = atr.next()
            yield
            S.op("dve", lambda e: e.tensor_tensor(out=AT[:], in0=pS[:], in1=Ep[:], op=ALU.mult), [pSk, Epk], [ATk])
            pa, pak = par.next()
            yield
            S.op("pe", lambda e: e.matmul(pa[:, 0:129], lhsT=AT[:], rhs=va[:, h, 0:129], start=True, stop=False),
                 [ATk, vak, (vak, "one")], [pak])
            S.op("pe", lambda e: e.matmul(pa[:, 0:129], lhsT=qt[rows, :], rhs=Cbf[rows, hp, 0:129], start=False,
                                          stop=True), [qtk, ("Cbf", h)], [pak])
            kh, khk = khr.next()
            yield
            S.op("dve", lambda e: e.tensor_scalar(out=kh[:], in0=kt[:, hp * 128:(hp + 1) * 128],
                                                  scalar1=cs[:, 8 + h:9 + h], scalar2=None, op0=ALU.mult),
                 [ktk, csk], [khk])
            pD, pDk = pDr.next()
            yield
            S.op("pe", lambda e: e.matmul(pD[:, 0:129], lhsT=kh[:], rhs=va[:, h, 0:129], start=True, stop=True),
                 [khk, vak, (vak, "one")], [pDk])
            yield
            S.op("dve", lambda e: e.scalar_tensor_tensor(out=Cst[rows, hp, 0:129], in0=Cst[rows, hp, 0:129],
                                                         scalar=cs[rows, 20 + h:21 + h], in1=pD[rows, 0:129],
                                                         op0=ALU.mult, op1=ALU.add), [("Cst", h), csk, pDk], [("Cst", h)])
            yield
            S.op("act", lambda e: e.activation(out=Cbf[rows, hp, 0:129], in_=Cst[rows, hp, 0:129], func=AF.Copy),
                 [("Cst", h)], [("Cbf", h)])
            r, rk = rr.next()
            yield
            S.op("dve", lambda e: e.tensor_scalar(out=r[:, 0:1], in0=pa[:, 128:129], scalar1=-1.0, scalar2=1.0,
                                                  op0=ALU.mult, op1=ALU.max), [pak], [rk])
            yield
            S.op("dve", lambda e: e.tensor_tensor(out=r[:, 0:1], in0=r[:, 0:1], in1=pa[:, 128:129], op=ALU.max),
                 [pak, rk], [rk])
            yield
            S.op("dve", lambda e: e.reciprocal(out=r[:, 1:2], in_=r[:, 0:1]), [rk], [rk])
            hs = hsum[:, c, h * 128:(h + 1) * 128]
            if d == 0:
                yield
                S.op("act", lambda e: e.activation(out=hs, in_=pa[:, 0:128], func=AF.Copy, scale=r[:, 1:2]),
                     [pak, rk], [("hs", c, h)])
            else:
                yield
                S.op("dve", lambda e: e.scalar_tensor_tensor(out=hs, in0=pa[:, 0:128], scalar=r[:, 1:2], in1=hs,
                                                             op0=ALU.mult, op1=ALU.add), [pak, rk, ("hs", c, h)],
                     [("hs", c, h)])

        def chunk(d, c):
            t0 = c * 128
            Mx, Mk = (Uf, "c_Uf") if d == 0 else (Lf, "c_Lf")
            qk, qkk = qkr.next()
            kt, ktk = ktr.next()
            va, vak = var.next()
            gt, gtk = gtr.next()
            S.dma("sp", qk[:], mqbv[:, :, t0:t0 + 128], [], [(qkk, 0)])
            S.dma("sp", kt[:], self.MKT[t0:t0 + 128, :], [], [ktk])
            S.dma("sp", va[:, :, 0:128], self.MV[t0:t0 + 128, :].rearrange("p (h e) -> p h e", h=4), [], [vak])
            S.dma("sp", gt[:], self.GATE[t0:t0 + 128, :], [], [gtk])
            cs, csk = csr.next()
            S.op("act", lambda e: e.activation(out=cs[:, 16:20], in_=gt[:, 8 + 4 * d:12 + 4 * d], func=AF.Copy),
                 [gtk], [(csk, "lf")])
            S.op("pe", lambda e: e.matmul(pcs[:, 0:4], lhsT=Mx[:], rhs=cs[:, 16:20], start=True, stop=True),
                 [Mk, (csk, "lf")], ["pcs"])
            S.op("pe", lambda e: e.matmul(pcs[:, 4:8], lhsT=onesf[:], rhs=cs[:, 16:20], start=True, stop=True,
                                          skip_group_check=True), ["c_onesf", (csk, "lf")], ["pcs"])
            S.op("dve", lambda e: e.tensor_copy(out=cs[:, 0:8], in_=pcs[:, 0:8]), ["pcs"], [csk])
            S.op("dve", lambda e: e.tensor_tensor(out=cs[:, 8:12], in0=cs[:, 4:8], in1=cs[:, 0:4], op=ALU.subtract),
                 [csk], [csk])
            S.op("dve", lambda e: e.tensor_tensor(out=cs[:, 8:12], in0=cs[:, 8:12], in1=gt[:, 4 * d:4 * d + 4],
                                                  op=ALU.add), [csk, gtk], [csk])
            S.op("act", lambda e: e.activation(out=cs[:, 8:12], in_=cs[:, 8:12], func=AF.Exp), [csk], [csk])
            S.op("dve", lambda e: e.tensor_scalar(out=cs[:, 12:16], in0=gt[:, 4 * d:4 * d + 4], scalar1=LN8,
                                                  scalar2=None, op0=ALU.add), [gtk, csk], [csk])
            S.op("act", lambda e: e.activation(out=cs[:, 20:24], in_=cs[:, 4:8], func=AF.Exp), [csk], [csk])
            csk_all = csk
            for hh in (0, 2):
                rr_run([head(d, c, h, Mx, Mk, qk, qkk, kt, ktk, va, vak, cs, csk_all) for h in (hh, hh + 1)])

        rsr = Ring(S, "fr", [128, 12], F32, 2)
        opr = Ring(S, "op", [128, 512], F32, 2)
        yr = Ring(S, "y", [128, 512], F32, 2)
        mor = Ring(S, "mo", [128, 512], BF16, 2)
        junk = S.sb("junk", [128, 128], F32)

        def fin_head(c, h, fr, frk, op_, opk, y, yk):
            hs = hsum[:, c, h * 128:(h + 1) * 128]
            S.op("dve", lambda e: e.scalar_tensor_tensor(out=y[:, h * 128:(h + 1) * 128], in0=hs, scalar=fr[:, 8 + h:9 + h],
                                                         in1=mg[:, h * 128:(h + 1) * 128], op0=ALU.mult, op1=ALU.mult),
                 [("hs", c, h), frk, "mg"], [(yk, h)])

        def finalize(c):
            fr, frk = rsr.next()
            S.op("dve", lambda e: e.memset(fr[:], 0.0), [], [frk])
            for h in range(4):
                S.op("act", lambda e, h=h: e.activation(out=junk[:], in_=hsum[:, c, h * 128:(h + 1) * 128], func=AF.Square,
                                                        accum_out=fr[:, h:h + 1]), [("hs", c, h), frk], ["junk", frk])
            S.op("dve", lambda e: e.tensor_scalar(out=fr[:, 4:8], in0=fr[:, 0:4], scalar1=1.0 / 128, scalar2=EPS,
                                                  op0=ALU.mult, op1=ALU.add), [frk], [frk])
            S.op("act", lambda e: e.activation(out=fr[:, 4:8], in_=fr[:, 4:8], func=AF.Ln), [frk], [frk])
            S.op("act", lambda e: e.activation(out=fr[:, 8:12], in_=fr[:, 4:8], func=AF.Exp, scale=-0.5), [frk], [frk])
            op_, opk = opr.next()
            S.dma("sp", op_[:], self.OPRE[c * 128:(c + 1) * 128, :], [], [opk])
            S.op("act", lambda e: e.activation(out=op_[:], in_=op_[:], func=AF.Exp, scale=-1.0), [opk], [opk])
            S.op("pool", lambda e: e.tensor_scalar(out=op_[:], in0=op_[:], scalar1=1.0, scalar2=None, op0=ALU.add),
                 [opk], [opk])
            S.op("dve", lambda e: e.reciprocal(out=op_[:], in_=op_[:]), [opk], [opk])
            y, yk = yr.next()
            for h in range(4):
                fin_head(c, h, fr, frk, op_, opk, y, yk)
            mo, mok = mor.next()
            S.op("pool", lambda e: e.tensor_tensor(out=mo[:], in0=y[:], in1=op_[:], op=ALU.mult),
                 [(yk, h) for h in range(4)] + [opk], [mok])
            S.store("pool", self.merged[c * 128:(c + 1) * 128, 512:1024], mo[:], [mok], [("mrg2", c)], skey=("st", mok))

        for d, order in ((0, order_f), (1, order_b)):
            S.op("dve", lambda e: e.memset(Cst[:], 0.0), [("Cst", h) for h in range(4)], [("Cst", h) for h in range(4)])
            S.op("pool", lambda e: e.memset(Cbf[:], 0.0), [("Cbf", h) for h in range(4)], [("Cbf", h) for h in range(4)])
            for c in order:
                chunk(d, c)
                if d == 1:
                    finalize(c)
        S.finish()

    def stage_outproj(self, l, wsrc, tiles):
        S = Stage(self, "op%d" % l)
        C = self.consts(S, ["identb"])
        ident = C["identb"]
        G = self.gate_bc(S, l, 1, "m", False)
        Wo = S.sb("Wo", [128, 8, D], BF16)
        self.load_w(S, Wo, "Wo", wsrc, D)
        xtr = Ring(S, "xt", [128, 4, D], F32, 2)
        mr = Ring(S, "m", [128, 4, D], BF16, 2)
        mTr = Ring(S, "mT", [128, 8, 512], BF16, 2)
        tmr = Ring(S, "tm", [128, 512], F32, 2)
        pTr = Ring(S, "pT", [128, 4, 128], BF16, 2, psum=True)
        por = Ring(S, "po", [128, 512], F32, 4, psum=True)

        def tr_k(k, m, mk_, mT, mTk, nsub):
            pT, pTk = pTr.next()
            for i in range(nsub):
                S.op("pe", lambda e, i=i: e.transpose(out=pT[:, i, :], in_=m[:, i, k * 128:(k + 1) * 128],
                                                      identity=ident[:]), [mk_, "c_identb"], [pTk])
            S.op("act" if k % 2 else "dve",
                 (lambda e: e.activation(out=mT[:, k, 0:nsub * 128], in_=pT[:, 0:nsub, :].rearrange("p i t -> p (i t)"),
                                         func=AF.Copy)) if k % 2 else
                 (lambda e: e.tensor_copy(out=mT[:, k, 0:nsub * 128], in_=pT[:, 0:nsub, :].rearrange("p i t -> p (i t)"))),
                 [pTk], [(mTk, k)])

        def out_chunk(i, n, xt, xk, mT, mTk, who):
            po, pok = por.next()
            for k in range(8):
                S.op("pe", lambda e, k=k: e.matmul(po[:], lhsT=mT[:, k, i * 128:(i + 1) * 128],
                                                   rhs=Wo[:, k, n * 512:(n + 1) * 512], start=(k == 0), stop=(k == 7)),
                     [(mTk, k), ("Wo", k)], [pok])
            tm, tmk = tmr.next()
            S.op("dve", lambda e: e.tensor_tensor(out=tm[:], in0=po[:], in1=G[:, who, n * 512:(n + 1) * 512],
                                                  op=ALU.mult), [pok, "mG%d" % who], [tmk])
            S.op("pool", lambda e: e.tensor_tensor(out=xt[:, i, n * 512:(n + 1) * 512],
                                                   in0=xt[:, i, n * 512:(n + 1) * 512], in1=tm[:], op=ALU.add),
                 [tmk, xk], [xk])

        def do_tile(tile):
            c0, nsub = tile
            who = 1 if c0 >= 32 else 0
            xt, xk = xtr.next()
            m, mk_ = mr.next()
            rows = slice(c0 * 128, (c0 + nsub) * 128)
            S.dma("sp", xt[:, 0:nsub, :], self.xres[rows, :].rearrange("(i p) d -> p i d", p=128), [], [xk])
            S.dma("sp", m[:, 0:nsub, :], self.merged[rows, :].rearrange("(i p) d -> p i d", p=128), [], [mk_])
            mT, mTk = mTr.next()
            for k in range(8):
                tr_k(k, m, mk_, mT, mTk, nsub)
            for i in range(nsub):
                for n in range(2):
                    out_chunk(i, n, xt, xk, mT, mTk, who)
            S.store("pool", self.xres[rows, :].rearrange("(i p) d -> p i d", p=128), xt[:, 0:nsub, :], [xk],
                    [("xres", c0)], skey=("st", xk))

        for tile in tiles:
            do_tile(tile)
        S.finish()

    def stage_odd_inproj(self):
        S = Stage(self, "oin")
        C = self.consts(S, ["identb"])
        ident = C["identb"]
        A, B = self.mod_cols(S, 1, 1, "m")
        R = self.front_rings(S)
        W = S.sb("W", [128, 8, ODD_IN], BF16)
        self.load_w(S, W, "W", self.owin_b, ODD_IN)
        wg = S.sb("wg", [17, 2, 512], BF16)
        for d in range(2):
            S.dma("pool", wg[0:16, d, :], self.gla_w_gate[d], [], [("wg", d, 0)])
            S.dma("pool", wg[16:17, d, :], self.gla_b_gate[d:d + 1, :], [], [("wg", d, 1)])
        lrr = Ring(S, "lrT", [17, 512], BF16, 4)
        obr = Ring(S, "ob", [128, 512], BF16, 4)
        ofr = Ring(S, "of", [128, 512], F32, 4)
        pjr = Ring(S, "pj", [128, 512], F32, 4, psum=True)
        plr = Ring(S, "pl", [16, 512], F32, 2, psum=True)
        QS = 128.0 ** -0.5

        def proj_fm(col, xh, xhk, ntok):
            pj, pjk = pjr.next()
            for k in range(8):
                S.op("pe", lambda e, k=k: e.matmul(pj[:, 0:ntok], lhsT=W[:, k, col:col + 128], rhs=xh[:, k, 0:ntok],
                                                   start=(k == 0), stop=(k == 7)), [("W", k), xhk], [pjk])
            return pj, pjk

        def proj_tm(col, n, i, xh, xhk):
            pj, pjk = pjr.next()
            for k in range(8):
                S.op("pe", lambda e, k=k: e.matmul(pj[:, 0:n], lhsT=xh[:, k, i * 128:(i + 1) * 128],
                                                   rhs=W[:, k, col:col + n], start=(k == 0), stop=(k == 7)),
                     [("W", k), xhk], [pjk])
            return pj, pjk

        def fm_chunk(g, col, dst, scale, xh, xhk, ntok, tok0):
            p1, p1k = proj_fm(col + g * 128, xh, xhk, ntok)
            ob, obk = obr.next()
            S.op("act", lambda e: e.activation(out=ob[:, 0:ntok], in_=p1[:, 0:ntok], func=AF.Copy, scale=scale),
                 [p1k], [obk])
            S.store("pool", dst[g * 128:(g + 1) * 128, tok0:tok0 + ntok], ob[:, 0:ntok], [obk],
                    [("fm", id(dst), g, tok0)], skey=("st", obk))

        def lr_dir(d, xh, xhk, ntok):
            pl, plk = plr.next()
            for k in range(8):
                S.op("pe", lambda e, k=k: e.matmul(pl[:, 0:ntok], lhsT=W[:, k, 3072 + 16 * d:3088 + 16 * d],
                                                   rhs=xh[:, k, 0:ntok], start=(k == 0), stop=(k == 7)),
                     [("W", k), xhk], [plk])
            lrT, lrk = lrr.next()
            S.op("pool", lambda e: e.memset(lrT[:], 1.0), [], [lrk])
            S.op("act", lambda e: e.activation(out=lrT[0:16, 0:ntok], in_=pl[:, 0:ntok], func=AF.Copy), [plk, lrk], [lrk])
            return lrT, lrk

        def gate_sub(d, i, lrT, lrk, tok0):
            r0 = tok0 + i * 128
            pj, pjk = pjr.next()
            S.op("pe", lambda e: e.matmul(pj[:], lhsT=lrT[0:17, i * 128:(i + 1) * 128], rhs=wg[0:17, d, :], start=True,
                                          stop=True), [lrk, ("wg", d, 0), ("wg", d, 1)], [pjk])
            of, ofk = ofr.next()
            S.op("act", lambda e: e.activation(out=of[:], in_=pj[:], func=AF.Exp, scale=-1.0), [pjk], [ofk])
            S.op("act", lambda e: e.activation(out=of[:], in_=of[:], func=AF.Ln, bias=1.0), [ofk], [ofk])
            S.op("dve", lambda e: e.tensor_scalar(out=of[:], in0=of[:], scalar1=-1.0 / 16.0, scalar2=None, op0=ALU.mult),
                 [ofk], [ofk])
            S.store("pool", self.GLA[d, r0:r0 + 128, :], of[:], [ofk], [("gla", d, r0)], skey=("st", ofk))

        def tm_one(col, dst, dcol, bf, i, xh, xhk, r0):
            pj, pjk = proj_tm(col, 512, i, xh, xhk)
            if bf:
                ob, obk = obr.next()
                S.op("act", lambda e: e.activation(out=ob[:], in_=pj[:], func=AF.Copy), [pjk], [obk])
                S.store("pool", dst[r0:r0 + 128, dcol:dcol + 512], ob[:], [obk], [("tm", col, r0)], skey=("st", obk))
            else:
                of, ofk = ofr.next()
                S.op("dve", lambda e: e.tensor_copy(out=of[:], in_=pj[:]), [pjk], [ofk])
                S.store("pool", dst[r0:r0 + 128, dcol:dcol + 512], of[:], [ofk], [("tm", col, r0)], skey=("st", ofk))

        def front(tile, xt, xk):
            c0, nsub = tile
            who = 0 if c0 < 32 else 1
            xh, xhk = self.norm_front(S, R, xt, xk, nsub, who, A, B, "m", "m", ident)
            return (nsub, nsub * 128, c0 * 128, xh, xhk)

        def part1(ctx):
            nsub, ntok, tok0, xh, xhk = ctx
            for g in range(4):
                fm_chunk(g, 0, self.GQT, QS, xh, xhk, ntok, tok0)
            for g in range(4):
                fm_chunk(g, 512, self.GKT, 1.0, xh, xhk, ntok, tok0)
            for d in range(2):
                lrT, lrk = lr_dir(d, xh, xhk, ntok)
                for i in range(nsub):
                    gate_sub(d, i, lrT, lrk, tok0)

        def part2(ctx):
            nsub, ntok, tok0, xh, xhk = ctx
            for i in range(nsub):
                r0 = tok0 + i * 128
                tm_one(512, self.GK, 0, True, i, xh, xhk, r0)
                tm_one(1024, self.GV, 0, True, i, xh, xhk, r0)
                tm_one(1536, self.GV, 512, True, i, xh, xhk, r0)
                tm_one(2048, self.GR, 0, False, i, xh, xhk, r0)
                tm_one(2560, self.GR, 512, False, i, xh, xhk, r0)

        x0 = self.load_x(S, R, ALL_TILES[0])
        cur = front(ALL_TILES[0], x0[0], x0[1])
        for ti, tile in enumerate(ALL_TILES):
            nx = self.load_x(S, R, ALL_TILES[ti + 1]) if ti + 1 < len(ALL_TILES) else None
            part1(cur)
            nxt_ctx = front(ALL_TILES[ti + 1], nx[0], nx[1]) if nx is not None else None
            part2(cur)
            cur = nxt_ctx
        S.finish()

    def stage_gla(self, order_f=None, order_b=None):
        order_f = [32, 33] + list(range(32)) if order_f is None else order_f
        order_b = [33, 32] + list(range(31, -1, -1)) if order_b is None else order_b
        S = Stage(self, "gla")
        C = self.consts(S, ["U", "L", "SU", "SL"])
        gn = S.sb("gn", [128, 256], F32)
        S.load_bc(gn[:], "gn", self.gla_norm_g)
        Sst = S.sb("Sst", [128, 4, 256], F32)
        Sbf = S.sb("Sbf", [128, 4, 256], BF16)
        qTr = Ring(S, "qT", [128, 4, 128], BF16, 3)
        kTr = Ring(S, "kT", [128, 4, 128], BF16, 3)
        kMr = Ring(S, "kM", [128, 512], BF16, 3)
        vr = Ring(S, "v", [128, D], BF16, 3)
        lar = Ring(S, "la", [128, 512], F32, 3)
        epr = Ring(S, "Ep", [128, 128], F32, 8)
        enr = Ring(S, "En", [128, 128], F32, 6)
        ekr = Ring(S, "Ek", [128, 128], F32, 6)
        qtr = Ring(S, "qtl", [128, 128], BF16, 6)
        ktr = Ring(S, "ktl", [128, 128], BF16, 6)
        khr = Ring(S, "kh", [128, 128], BF16, 6)
        atr = Ring(S, "AT", [128, 128], BF16, 6)
        ofr = Ring(S, "of", [128, D], F32, 3)
        pBr = PRing(S, "pB", 128, 2, 1)
        pSr = PRing(S, "pSf", 128, 2, 1)
        pAr = PRing(S, "pA", 128, 2, 1)
        por = PRing(S, "po", 256, 2, 1)
        pDr = PRing(S, "pD", 256, 0, 1)
        for t_, (_, pk_) in zip(por.tensors, por.bufs):
            pDr.bufs.append((t_[:, 256:512], pk_))

        def head(d, c, h, need_out, qT, qTk, kT, kTk, kM, kMk, v, vk, la, lak, of, ofk):
            Mx, Mk = (C["U"], "c_Uf") if d == 0 else (C["L"], "c_Lf")
            Ms, Msk = (C["SL"], "c_SLf") if d == 0 else (C["SU"], "c_SUf")
            lah = la[:, h * 128:(h + 1) * 128]
            vh = v[:, h * 256:(h + 1) * 256]
            pB, pBk = pBr.next()
            S.op("pe", lambda e: e.matmul(pB[:], lhsT=lah, rhs=Mx[:], start=True, stop=True), [lak, Mk], [pBk])
            pSf, pSfk = pSr.next()
            yield
            S.op("pe", lambda e: e.matmul(pSf[:], lhsT=Ms[:], rhs=lah, start=True, stop=True), [lak, Msk], [pSfk])
            Ep, Epk = epr.next()
            yield
            S.op("act", lambda e: e.activation(out=Ep[:], in_=pB[:], func=AF.Exp), [pBk], [Epk])
            Ek, Ekk = ekr.next()
            yield
            S.op("act", lambda e: e.activation(out=Ek[:], in_=pSf[:], func=AF.Exp), [pSfk], [Ekk])
            qt, qtk = qtr.next()
            yield
            S.op("dve", lambda e: e.tensor_tensor(out=qt[:], in0=qT[:, h, :], in1=Ep[:], op=ALU.mult), [qTk, Epk], [qtk])
            kh, khk = khr.next()
            yield
            S.op("dve", lambda e: e.tensor_tensor(out=kh[:], in0=kM[:, h * 128:(h + 1) * 128], in1=Ek[:], op=ALU.mult),
                 [kMk, Ekk], [khk])
            if need_out:
                En, Enk = enr.next()
                yield
                S.op("act", lambda e: e.activation(out=En[:], in_=pB[:], func=AF.Exp, scale=-1.0), [pBk], [Enk])
                kt, ktk = ktr.next()
                yield
                S.op("pool", lambda e: e.tensor_tensor(out=kt[:], in0=kT[:, h, :], in1=En[:], op=ALU.mult),
                     [kTk, Enk], [ktk])
                pA, pAk = pAr.next()
                yield
                S.op("pe", lambda e: e.matmul(pA[:], lhsT=kt[:], rhs=qt[:], start=True, stop=True), [ktk, qtk], [pAk])
                AT, ATk = atr.next()
                yield
                S.op("dve", lambda e: e.tensor_tensor(out=AT[:], in0=pA[:], in1=Mx[:], op=ALU.mult), [pAk, Mk], [ATk])
                po, pok = por.next()
                yield
                S.op("pe", lambda e: e.matmul(po[:], lhsT=AT[:], rhs=vh, start=True, stop=False), [ATk, vk], [pok])
                S.op("pe", lambda e: e.matmul(po[:], lhsT=qt[:], rhs=Sbf[:, h, :], start=False, stop=True),
                     [qtk, ("Sbf", h)], [pok])
                oh = of[:, h * 256:(h + 1) * 256]
                if d == 0:
                    S.op("act", lambda e: e.activation(out=oh, in_=po[:], func=AF.Copy), [pok], [(ofk, h)])
                else:
                    S.op("dve", lambda e: e.tensor_tensor(out=oh, in0=po[:], in1=oh, op=ALU.add), [pok, (ofk, h)],
                         [(ofk, h)])
            pD, pDk = pDr.next()
            yield
            S.op("pe", lambda e: e.matmul(pD[:], lhsT=kh[:], rhs=vh, start=True, stop=True), [khk, vk], [pDk])
            col = 127 if d == 0 else 0
            yield
            S.op("dve", lambda e: e.scalar_tensor_tensor(out=Sst[:, h, :], in0=Sst[:, h, :], scalar=Ep[:, col:col + 1],
                                                         in1=pD[:], op0=ALU.mult, op1=ALU.add),
                 [("Sst", h), Epk, pDk], [("Sst", h)])
            yield
            S.op("act", lambda e: e.activation(out=Sbf[:, h, :], in_=Sst[:, h, :], func=AF.Copy), [("Sst", h)],
                 [("Sbf", h)])

        frr = Ring(S, "fr", [128, 12], F32, 2)
        rr = Ring(S, "r", [128, D], F32, 2)
        srr = Ring(S, "sr", [128, D], F32, 2)
        yr = Ring(S, "y", [128, D], F32, 2)
        mor = Ring(S, "mo", [128, D], BF16, 2)
        junk = S.sb("junk", [128, 256], F32)

        def fin_sq(h, of, ofk, fr, frk):
            S.op("act", lambda e: e.activation(out=junk[:], in_=of[:, h * 256:(h + 1) * 256], func=AF.Square,
                                               accum_out=fr[:, h:h + 1]), [(ofk, h), frk], ["junk", frk])

        def fin_y(h, of, ofk, fr, frk, y, yk):
            S.op("dve", lambda e: e.scalar_tensor_tensor(out=y[:, h * 256:(h + 1) * 256], in0=of[:, h * 256:(h + 1) * 256],
                                                         scalar=fr[:, 8 + h:9 + h], in1=gn[:], op0=ALU.mult, op1=ALU.mult),
                 [(ofk, h), frk, "gn"], [(yk, h)])

        def finalize(c, of, ofk):
            fr, frk = frr.next()
            S.op("dve", lambda e: e.memset(fr[:], 0.0), [], [frk])
            for h in range(4):
                fin_sq(h, of, ofk, fr, frk)
            S.op("dve", lambda e: e.tensor_scalar(out=fr[:, 4:8], in0=fr[:, 0:4], scalar1=1.0 / 256, scalar2=EPS,
                                                  op0=ALU.mult, op1=ALU.add), [frk], [frk])
            S.op("act", lambda e: e.activation(out=fr[:, 4:8], in_=fr[:, 4:8], func=AF.Ln), [frk], [frk])
            S.op("act", lambda e: e.activation(out=fr[:, 8:12], in_=fr[:, 4:8], func=AF.Exp, scale=-0.5), [frk], [frk])
            r, rk = rr.next()
            sr, srk = srr.next()
            S.dma("sp", r[:], self.GR[c * 128:(c + 1) * 128, :], [], [rk])
            S.op("act", lambda e: e.activation(out=sr[:], in_=r[:], func=AF.Exp, scale=-1.0), [rk], [srk])
            S.op("pool", lambda e: e.tensor_scalar(out=sr[:], in0=sr[:], scalar1=1.0, scalar2=None, op0=ALU.add),
                 [srk], [srk])
            S.op("dve", lambda e: e.reciprocal(out=sr[:], in_=sr[:]), [srk], [srk])
            S.op("pool", lambda e: e.tensor_tensor(out=sr[:], in0=sr[:], in1=r[:], op=ALU.mult), [srk, rk], [srk])
            y, yk = yr.next()
            for h in range(4):
                fin_y(h, of, ofk, fr, frk, y, yk)
            mo, mok = mor.next()
            S.op("pool", lambda e: e.tensor_tensor(out=mo[:], in0=y[:], in1=sr[:], op=ALU.mult),
                 [(yk, h) for h in range(4)] + [srk], [mok])
            S.store("pool", self.merged[c * 128:(c + 1) * 128, :], mo[:], [mok], [("mrg", c)], skey=("st", mok))

        def chunk(d, c):
            t0 = c * 128
            need_out = c < 32
            qT, qTk = qTr.next()
            kT, kTk = kTr.next()
            kM, kMk = kMr.next()
            v, vk = vr.next()
            la, lak = lar.next()
            S.dma("sp", qT[:], self.GQT.rearrange("(h p) t -> p h t", p=128)[:, :, t0:t0 + 128], [], [qTk])
            S.dma("sp", kT[:], self.GKT.rearrange("(h p) t -> p h t", p=128)[:, :, t0:t0 + 128], [], [kTk])
            S.dma("sp", kM[:], self.GK[t0:t0 + 128, :], [], [kMk])
            S.dma("sp", v[:], self.GV[t0:t0 + 128, :], [], [vk])
            S.dma("sp", la[:], self.GLA[d, t0:t0 + 128, :], [], [lak])
            of, ofk = ofr.next()
            if need_out and d == 1:
                for h in range(4):
                    S.dma("sp", of[:, h * 256:(h + 1) * 256], self.OF[t0:t0 + 128, h * 256:(h + 1) * 256], [("OF", c)],
                          [(ofk, h)])
            for hh in (0, 2):
                rr_run([head(d, c, h, need_out, qT, qTk, kT, kTk, kM, kMk, v, vk, la, lak, of, ofk) for h in (hh, hh + 1)])
            if need_out:
                if d == 0:
                    S.store("pool", self.OF[t0:t0 + 128, :], of[:], [(ofk, h) for h in range(4)], [("OF", c)],
                            skey=("st", ofk))
                else:
                    finalize(c, of, ofk)

        for d, order in ((0, order_f), (1, order_b)):
            S.op("dve", lambda e: e.memset(Sst[:], 0.0), [("Sst", h) for h in range(4)], [("Sst", h) for h in range(4)])
            S.op("pool", lambda e: e.memset(Sbf[:], 0.0), [("Sbf", h) for h in range(4)], [("Sbf", h) for h in range(4)])
            for c in order:
                chunk(d, c)
        S.finish()

    def stage_dump(self, extra=()):
        S = Stage(self, "dump")
        for nm in extra:
            src = getattr(self, nm)
            dst = self.nc.dram_tensor("dbg_" + nm, list(src.shape), src.dtype, kind="ExternalOutput").ap()
            if len(src.shape) == 2 and src.shape[0] > 512:
                for r in range(0, src.shape[0], 512):
                    n = min(512, src.shape[0] - r)
                    S.store("sp", dst[r:r + n, :], src[r:r + n, :], [], [("dd", nm, r)], skey="cp")
            else:
                S.store("sp", dst, src, [], [("dd", nm)], skey="cp")
        S.store("sp", self.dbg_mod, self.modrow.rearrange("l w n -> (l w) n"), [], [("dmod", 0)], skey="cp")
        for r in range(0, TT, 512):
            n = min(512, TT - r)
            S.store("sp", self.dbg_x[r:r + n, :], self.xres[r:r + n, :], [], [("dx", r)], skey="cp")
            S.store("sp", self.dbg_m[r:r + n, :], self.merged[r:r + n, :], [], [("dm", r)], skey="cp")
        S.finish()

    def build(self):
        upto = self.upto
        self.stage_prologue()
        self.stage_ada(0)
        steps = [
            ("ffn00", lambda: self.stage_ffn(0, 0, ALL_TILES, extra_casts=[(self.ewin_b, self.even_w_in, D, "ewin"), (self.ewout_b, self.even_w_out, D, "ewout"), (self.w1b[0, 1], self.ffn_w_in[0, 1], D, "w1b01"), (self.w2b[0, 1], self.ffn_w_out[0, 1], DFF, "w2b01")])),
            ("ein", lambda: self.stage_even_inproj()),
            ("att", lambda: self.stage_attn()),
            ("mlp", lambda: self.stage_mlprep()),
            ("mls", lambda: self.stage_mlstm()),
            ("op0", lambda: self.stage_outproj(0, self.ewout_b, ALL_TILES)),
            ("ffn01", lambda: self.stage_ffn(0, 1, ALL_TILES)),
            ("ada1", lambda: self.stage_ada(1, extra_casts=[(self.w1b[1, 0], self.ffn_w_in[1, 0], D, "w1b10"), (self.w2b[1, 0], self.ffn_w_out[1, 0], DFF, "w2b10")])),
            ("ffn10", lambda: self.stage_ffn(1, 0, ALL_TILES, extra_casts=[(self.owin_b, self.odd_w_in, D, "owin"), (self.owout_b, self.odd_w_out, D, "owout"), (self.w1b[1, 1], self.ffn_w_in[1, 1], D, "w1b11"), (self.w2b[1, 1], self.ffn_w_out[1, 1], DFF, "w2b11")])),
            ("oin", lambda: self.stage_odd_inproj()),
            ("gla", lambda: self.stage_gla()),
            ("op1", lambda: self.stage_outproj(1, self.owout_b, LAT_TILES)),
            ("ffn11", lambda: self.stage_ffn(1, 1, LAT_TILES, final_norm=True)),
        ]
        for name, fn in steps:
            fn()
            if upto == name:
                break
        if self.dbg:
            self.stage_dump()
        return self.nc


def host_constants():
    inv = 10000.0 ** (-np.arange(16, dtype=np.float32) * 2.0 / 32).astype(np.float32)
    t = np.arange(T)
    row = (t // 64).astype(np.float32)
    col = (t % 64).astype(np.float32)
    ar = row[None, :] * inv[:, None]
    ac = col[None, :] * inv[:, None]
    cr, sr, cc, sc = np.cos(ar), np.sin(ar), np.cos(ac), np.sin(ac)
    c64 = np.concatenate([cr, cr, cc, cc], axis=0)
    s64 = np.concatenate([-sr, sr, -sc, sc], axis=0)
    rope_c = np.concatenate([c64, c64], axis=0).astype(np.float32)
    rope_s = np.concatenate([s64, s64], axis=0).astype(np.float32)
    r = np.arange(128)
    ident = np.eye(128, dtype=np.float32)
    U = (r[:, None] <= r[None, :]).astype(np.float32)
    L = (r[:, None] >= r[None, :]).astype(np.float32)
    ones = np.ones((128, 128), np.float32)
    SU = (r[:, None] < r[None, :]).astype(np.float32)
    SL = (r[:, None] > r[None, :]).astype(np.float32)
    return rope_c, rope_s, np.stack([ident, U, L, ones, SU, SL])


def make_in_maps(inputs, cores):
    f = lambda a: np.ascontiguousarray(np.asarray(a, dtype=np.float32))
    rope_c, rope_s, cmask = host_constants()
    ew = np.asarray(inputs["even_w_in"], dtype=np.float32)[0]
    d = np.arange(512)
    blk, r = d // 32 * 32, d % 32
    perm = blk + (r + 16) % 32
    ewx = np.concatenate([ew, ew[:, 0:512][:, perm], ew[:, 512:1024][:, perm]], axis=1)
    shared = {
        "c_ctx": f(inputs["c_ctx"]).reshape(1, D),
        "ada_w": f(inputs["ada_w"]), "ada_b": f(inputs["ada_b"]),
        "norm_g": f(inputs["norm_g"]).reshape(2, 3 * D),
        "ffn_w_in": f(inputs["ffn_w_in"]), "ffn_w_out": f(inputs["ffn_w_out"]),
        "even_w_in": f(ewx), "even_w_out": f(inputs["even_w_out"])[0],
        "diff_lambda": f(inputs["diff_lambda"]).reshape(1, 256),
        "diff_norm_g": f(inputs["diff_norm_g"]).reshape(1, 128),
        "mlstm_conv_w": f(inputs["mlstm_conv_w"])[0], "mlstm_conv_b": f(inputs["mlstm_conv_b"]).reshape(1, 512),
        "mlstm_gate_b": f(inputs["mlstm_gate_b"]).reshape(1, 16),
        "mlstm_norm_g": f(inputs["mlstm_norm_g"]).reshape(1, 512),
        "odd_w_in": f(inputs["odd_w_in"])[0], "odd_w_out": f(inputs["odd_w_out"])[0],
        "gla_w_gate": f(inputs["gla_w_gate"])[0], "gla_b_gate": f(inputs["gla_b_gate"])[0],
        "gla_norm_g": f(inputs["gla_norm_g"]).reshape(1, 256), "final_g": f(inputs["final_g"]).reshape(1, D),
        "rope_c": rope_c, "rope_s": rope_s, "cmask": cmask,
    }
    maps = []
    for b in cores:
        m = dict(shared)
        m["x"] = f(inputs["x"][b])
        m["ctx"] = f(inputs["ctx"][b])
        m["c"] = f(inputs["c"][b]).reshape(1, D)
        maps.append(m)
    return maps


def kernel(**inputs):
    mk = MK()
    nc = mk.build()
    maps = make_in_maps(inputs, list(range(8)))
    res = run_bass_kernel_spmd(nc, maps, core_ids=list(range(8)))
    return np.stack([np.asarray(r["out"], dtype=np.float32) for r in res.results], axis=0)
```

```python
import math
from contextlib import ExitStack

import numpy as np
import concourse.bass as bass
import concourse.mybir as mybir
from concourse.bass_utils import run_bass_kernel_spmd

F32 = mybir.dt.float32
BF16 = mybir.dt.bfloat16
AF = mybir.ActivationFunctionType
ALU = mybir.AluOpType
AX = mybir.AxisListType

D = 1024
T = 4096
TC = 256
TT = T + TC
NCH = TT // 128
DFF = 2816
NJ = DFF // 128
EPS = 1e-6
EVEN_X = 4112
ODD_IN = 3104
LAT_TILES = [(4 * t, 4) for t in range(8)]
CTX_TILE = (32, 2)
ALL_TILES = LAT_TILES + [CTX_TILE]


class Op:
    __slots__ = ("eng", "fn", "deps", "is_dma", "dma_key", "signaled", "val", "idx", "waits")


class Prog:
    ENG = ("pe", "act", "dve", "pool", "sp")
    uid = 0

    def __init__(self, nc, same_engine_sync=True):
        self.nc = nc
        self.ops = {e: [] for e in self.ENG}
        self.lw = {}
        self.rd = {}
        self.dma_cnt = {}
        self.ses = same_engine_sync

    def add(self, eng, fn, reads=(), writes=(), dma_key=None):
        op = Op()
        op.eng = eng
        op.fn = fn
        op.is_dma = dma_key is not None
        op.dma_key = dma_key
        op.signaled = op.is_dma
        op.val = 0
        deps = {}
        for k in reads:
            w = self.lw.get(k)
            if w is not None:
                deps[id(w)] = w
        for k in writes:
            w = self.lw.get(k)
            if w is not None:
                deps[id(w)] = w
            for r in self.rd.get(k, ()):
                deps[id(r)] = r
        op.deps = list(deps.values())
        for k in reads:
            lst = self.rd.setdefault(k, [])
            if not op.is_dma:
                lst[:] = [r for r in lst if r.is_dma or r.eng != eng]
            lst.append(op)
        for k in writes:
            self.lw[k] = op
            self.rd[k] = []
        if op.is_dma:
            c = self.dma_cnt.get(dma_key, 0) + 1
            self.dma_cnt[dma_key] = c
            op.val = 16 * c
        op.idx = len(self.ops[eng])
        self.ops[eng].append(op)
        return op

    def wait_all(self, eng, ops):
        op = self.add(eng, None)
        op.deps = list(ops)
        return op

    def emit(self, stack):
        nc = self.nc
        for e in self.ENG:
            for a in self.ops[e]:
                need = []
                for b in a.deps:
                    if b.is_dma:
                        need.append(b)
                    elif b.eng == a.eng and not a.is_dma:
                        if a.eng == "pe" or not self.ses:
                            continue
                        need.append(b)
                        b.signaled = True
                    else:
                        need.append(b)
                        b.signaled = True
                a.waits = need
        for e in self.ENG:
            c = 0
            for a in self.ops[e]:
                if a.is_dma or a.fn is None:
                    continue
                if a.signaled:
                    c += 1
                    a.val = c
        Prog.uid += 1
        esem = {e: nc.alloc_semaphore(name="s%d_%s" % (Prog.uid, e)) for e in self.ENG}
        dsem = {}
        for k in self.dma_cnt:
            dsem[k] = nc.alloc_semaphore(name="d%d_%d" % (Prog.uid, len(dsem)))
        self.sems = list(esem.values()) + list(dsem.values())
        self.n_sems = len(self.sems)

        def run(engobj, ename):
            waited = {}
            for a in self.ops[ename]:
                mx = {}
                for b in a.waits:
                    if b.is_dma:
                        sk, sem = b.dma_key, dsem[b.dma_key]
                    else:
                        sk, sem = b.eng, esem[b.eng]
                    if sk not in mx or mx[sk][1] < b.val:
                        mx[sk] = (sem, b.val)
                for sk, (sem, val) in mx.items():
                    if waited.get(sk, 0) >= val:
                        continue
                    waited[sk] = val
                    engobj.wait_ge(sem, val)
                if a.fn is None:
                    continue
                ins = a.fn(engobj)
                if a.is_dma:
                    ins.then_inc(dsem[a.dma_key], 16)
                elif a.signaled:
                    ins.then_inc(esem[ename], 1)

        block = stack.enter_context(nc.Block())

        @block.tensor
        def _(eng):
            run(eng, "pe")

        @block.scalar
        def _(eng):
            run(eng, "act")

        @block.vector
        def _(eng):
            run(eng, "dve")

        @block.gpsimd
        def _(eng):
            run(eng, "pool")

        @block.sync
        def _(eng):
            run(eng, "sp")


def rr_run(gens):
    gens = list(gens)
    while gens:
        for g in list(gens):
            try:
                next(g)
            except StopIteration:
                gens.remove(g)


class Ring:
    def __init__(self, stage, name, shape, dt, n, psum=False):
        mk = stage.ps if psum else stage.sb
        self.bufs = [(mk("%s%d" % (name, i), shape, dt), "%s%d" % (name, i)) for i in range(n)]
        self.i = 0

    def next(self):
        b = self.bufs[self.i % len(self.bufs)]
        self.i += 1
        return b


class PRing:
    def __init__(self, stage, name, width, n, per):
        self.bufs = []
        self.tensors = []
        nb = (n + per - 1) // per
        slot = 512 // per
        assert width <= slot
        for b in range(nb):
            t = stage.ps("%s_b%d" % (name, b), [128, 512], F32)
            self.tensors.append(t)
            for j in range(per):
                if len(self.bufs) < n:
                    self.bufs.append((t[:, j * slot:j * slot + width], "%s%d" % (name, len(self.bufs))))
        self.i = 0

    def next(self):
        b = self.bufs[self.i % len(self.bufs)]
        self.i += 1
        return b


class Stage:
    def __init__(self, mk, name):
        self.mk = mk
        self.nc = mk.nc
        self.name = name
        self.st = ExitStack()
        self.P = Prog(self.nc)
        self.outs = []
        self.nslot = 0

    def sb(self, name, shape, dt):
        return self.st.enter_context(self.nc.sbuf_tensor("%s_%s" % (self.name, name), shape, dt))

    def ps(self, name, shape, dt):
        return self.st.enter_context(self.nc.psum_tensor("%s_%s" % (self.name, name), shape, dt))

    def op(self, eng, fn, reads=(), writes=()):
        return self.P.add(eng, fn, reads, writes)

    def dma(self, eng, out, in_, reads, writes, slow=False, out_final=False, skey=None):
        key = ("dma", writes[0] if skey is None else skey)
        if slow:
            fn = lambda e: e.dma_start(out=out, in_=in_, allow_slow_non_contiguous=True)
        else:
            fn = lambda e: e.dma_start(out=out, in_=in_)
        op = self.P.add(eng, fn, reads, writes, dma_key=key)
        if out_final:
            self.outs.append(op)
        return op

    def store(self, eng, out, in_, reads, writes, slow=False, skey=None):
        return self.dma(eng, out, in_, reads, writes, slow=slow, out_final=True, skey=skey)

    def load_cols(self, dst, key, row_ap, rkeys=()):
        return self.dma("sp", dst, row_ap.rearrange("o (j p) -> p (o j)", p=128), list(rkeys), [key], slow=True)

    def load_bc(self, dst, key, row_ap, rkeys=(), parts=128):
        return self.dma("sp", dst, row_ap.partition_broadcast(parts), list(rkeys), [key])

    def finish(self):
        alld = [a for e in Prog.ENG for a in self.P.ops[e] if a.is_dma]
        if alld:
            self.P.wait_all("sp", alld)
        self.P.emit(self.st)
        self.st.close()
        self.nc.all_engine_barrier()
        self.nc.clear_and_free_semaphores(self.P.sems)
        self.nc.all_engine_barrier()


class MK:
    def __init__(self, upto=None, dbg=False, ext=()):
        self.upto = upto
        self.dbg = dbg
        nc = self.nc = bass.Bass("TRN2", target_bir_lowering=False)
        inp = lambda name, shape, dt=F32: nc.dram_tensor(name, list(shape), dt, kind="ExternalInput").ap()
        itn = lambda name, shape, dt=F32: nc.dram_tensor(name, list(shape), dt,
                                                         kind="ExternalInput" if name in ext else "Internal").ap()
        self.x = inp("x", [T, D])
        self.ctx = inp("ctx", [TC, D])
        self.c = inp("c", [1, D])
        self.c_ctx = inp("c_ctx", [1, D])
        self.ada_w = inp("ada_w", [2, D, 9 * D])
        self.ada_b = inp("ada_b", [2, 9 * D])
        self.norm_g = inp("norm_g", [2, 3 * D])
        self.ffn_w_in = inp("ffn_w_in", [2, 2, D, 2 * DFF])
        self.ffn_w_out = inp("ffn_w_out", [2, 2, DFF, D])
        self.even_w_in = inp("even_w_in", [D, EVEN_X])
        self.even_w_out = inp("even_w_out", [D, D])
        self.diff_lambda = inp("diff_lambda", [1, 256])
        self.diff_norm_g = inp("diff_norm_g", [1, 128])
        self.conv_w = inp("mlstm_conv_w", [3, 512])
        self.conv_b = inp("mlstm_conv_b", [1, 512])
        self.gate_b = inp("mlstm_gate_b", [1, 16])
        self.ml_norm_g = inp("mlstm_norm_g", [1, 512])
        self.odd_w_in = inp("odd_w_in", [D, ODD_IN])
        self.odd_w_out = inp("odd_w_out", [D, D])
        self.gla_w_gate = inp("gla_w_gate", [2, 16, 512])
        self.gla_b_gate = inp("gla_b_gate", [2, 512])
        self.gla_norm_g = inp("gla_norm_g", [1, 256])
        self.final_g = inp("final_g", [1, D])
        self.rope_c = inp("rope_c", [128, T])
        self.rope_s = inp("rope_s", [128, T])
        self.cmask = inp("cmask", [6, 128, 128])
        self.out = nc.dram_tensor("out", [T, D], F32, kind="ExternalOutput").ap()
        self.xres = itn("xres", [TT, D])
        self.modrow = itn("modrow", [2, 2, 9 * D])
        self.w1b = itn("w1b", [2, 2, D, 2 * DFF], BF16)
        self.w2b = itn("w2b", [2, 2, DFF, D], BF16)
        self.ewin_b = itn("ewin_b", [D, EVEN_X], BF16)
        self.ewout_b = itn("ewout_b", [D, D], BF16)
        self.owin_b = itn("owin_b", [D, ODD_IN], BF16)
        self.owout_b = itn("owout_b", [D, D], BF16)
        self.merged = itn("merged", [TT, D], BF16)
        self.QT = itn("QT", [512, TT], BF16)
        self.KT = itn("KT", [512, TT], BF16)
        self.Vs = itn("Vs", [TT, 512], BF16)
        self.MQK = itn("MQK", [512, TT], F32)
        self.MV = itn("MV", [TT, 512], BF16)
        self.OPRE = itn("OPRE", [TT, 512], F32)
        self.GATE = itn("GATE", [TT, 16], F32)
        self.nrm = itn("nrm", [128, 16])
        self.MQKB = itn("MQKB", [512, TT], BF16)
        self.MKT = itn("MKT", [TT, 256], BF16)
        self.GQT = itn("GQT", [512, TT], BF16)
        self.GKT = itn("GKT", [512, TT], BF16)
        self.GK = itn("GK", [TT, 512], BF16)
        self.GV = itn("GV", [TT, D], BF16)
        self.GR = itn("GR", [TT, D], F32)
        self.GLA = itn("GLA", [2, TT, 512], F32)
        self.OF = itn("OF", [TT, D], F32)
        if dbg:
            self.dbg_x = nc.dram_tensor("dbg_x", [TT, D], F32, kind="ExternalOutput").ap()
            self.dbg_m = nc.dram_tensor("dbg_m", [TT, D], BF16, kind="ExternalOutput").ap()
            self.dbg_mod = nc.dram_tensor("dbg_mod", [4, 9 * D], F32, kind="ExternalOutput").ap()

    def cast_rows(self, S, dst, src, nrows, key, eng="pool"):
        for r in range(0, nrows, 128):
            S.store(eng, dst[r:r + 128, :], src[r:r + 128, :], [], [(key, r)], skey="cast")

    def consts(self, S, want=("ident",)):
        P = S.P
        res = {}
        names = {"ident": 0, "U": 1, "L": 2, "ones": 3, "SU": 4, "SL": 5}
        for w in want:
            base = w.rstrip("fb")
            f = S.sb("c_" + base + "f", [128, 128], F32) if ("c_" + base + "f") not in res else None
            kf = "c_" + base + "f"
            if kf not in res:
                S.dma("sp", f[:], self.cmask[names[base]], [], [kf])
                res[kf] = f
            if w.endswith("b"):
                b = S.sb("c_" + w, [128, 128], BF16)
                S.op("dve", lambda e, b=b, f=res[kf]: e.tensor_copy(out=b[:], in_=f[:]), [kf], ["c_" + w])
                res[w] = b
            else:
                res[w] = res[kf]
        return res

    def stage_prologue(self):
        S = Stage(self, "pro")
        for r in range(0, T, 512):
            S.store("sp", self.xres[r:r + 512, :], self.x[r:r + 512, :], [], [("xres", r)], skey="cp")
        S.store("sp", self.xres[T:TT, :], self.ctx[:, :], [], [("xres", T)], skey="cp")
        self.cast_rows(S, self.w1b[0, 0], self.ffn_w_in[0, 0], D, "w1b00")
        self.cast_rows(S, self.w2b[0, 0], self.ffn_w_out[0, 0], DFF, "w2b00")
        S.finish()

    def stage_ada(self, l, extra_casts=()):
        S = Stage(self, "ada%d" % l)
        for (dst, src, nrows, key) in extra_casts:
            self.cast_rows(S, dst, src, nrows, key)
        ccol = S.sb("ccol", [128, 2, 8], F32)
        sT = S.sb("sT", [128, 8, 2], BF16)
        bias = S.sb("bias", [2, 9 * D], F32)
        rows = S.sb("rows", [2, 9 * D], F32)
        S.load_cols(ccol[:, 0, :], "ccol0", self.c)
        S.load_cols(ccol[:, 1, :], "ccol1", self.c_ctx)
        S.load_bc(bias[:], "bias", self.ada_b[l:l + 1, :], parts=2)
        for w in range(2):
            S.op("act", lambda e, w=w: e.activation(out=sT[:, :, w], in_=ccol[:, w, :], func=AF.Silu),
                 ["ccol%d" % w], ["sT"])
        wring = Ring(S, "aw", [128, 8, 512], BF16, 4)
        pring = Ring(S, "pm", [2, 512], F32, 2, psum=True)
        awv = self.ada_w[l].rearrange("(k p) n -> p k n", p=128)
        for ch in range(18):
            wt, wk = wring.next()
            S.dma("pool", wt[:], awv[:, :, ch * 512:(ch + 1) * 512], [], [wk])
            pt, pk = pring.next()
            for k in range(8):
                S.op("pe", lambda e, pt=pt, wt=wt, k=k: e.matmul(pt[:], lhsT=sT[:, k, :], rhs=wt[:, k, :],
                                                                 start=(k == 0), stop=(k == 7)),
                     ["sT", wk], [pk])
            S.op("dve", lambda e, pt=pt, ch=ch: e.tensor_tensor(out=rows[:, ch * 512:(ch + 1) * 512], in0=pt[:],
                                                                in1=bias[:, ch * 512:(ch + 1) * 512], op=ALU.add),
                 [pk, "bias"], ["rows"])
        S.store("sp", self.modrow[l], rows[:], ["rows"], [("modrow", l)])
        S.finish()

    def mod_cols(self, S, l, si, tag):
        g = S.sb(tag + "g", [128, 8], F32)
        sc = S.sb(tag + "sc", [128, 2, 8], F32)
        A = S.sb(tag + "A", [128, 2, 8], F32)
        B = S.sb(tag + "B", [128, 2, 8], F32)
        S.load_cols(g[:], tag + "g", self.norm_g[l:l + 1, si * D:(si + 1) * D])
        for w in range(2):
            S.load_cols(B[:, w, :], tag + "B%d" % w, self.modrow[l, w:w + 1, (3 * si) * D:(3 * si + 1) * D])
            S.load_cols(sc[:, w, :], tag + "sc%d" % w, self.modrow[l, w:w + 1, (3 * si + 1) * D:(3 * si + 2) * D])
            S.op("dve", lambda e, w=w: e.scalar_tensor_tensor(out=A[:, w, :], in0=sc[:, w, :], scalar=1.0, in1=g[:],
                                                             op0=ALU.add, op1=ALU.mult),
                 [tag + "sc%d" % w, tag + "g"], [tag + "A%d" % w])
        return A, B

    def gate_bc(self, S, l, si, tag, half):
        G = S.sb(tag + "G", [128, 2, D], F32)
        for w in range(2):
            S.load_bc(G[:, w, :], tag + "G%d" % w, self.modrow[l, w:w + 1, (3 * si + 2) * D:(3 * si + 3) * D])
            if half:
                S.op("pool", lambda e, w=w: e.tensor_scalar(out=G[:, w, :], in0=G[:, w, :], scalar1=0.5, scalar2=None,
                                                            op0=ALU.mult),
                     [tag + "G%d" % w], [tag + "G%d" % w])
        return G

    def norm_front(self, S, R, xt, xk, nsub, who, A, B, tagA, tagB, ident):
        ss, ssk = R["ss"].next()
        rs, rsk = R["rs"].next()
        xn, xnk = R["xn"].next()
        xh, xhk = R["xh"].next()
        junk = R["junk"]
        S.op("dve", lambda e: e.memset(ss[:], 0.0), [], [ssk])
        for i in range(nsub):
            S.op("act", lambda e, i=i: e.activation(out=junk[:], in_=xt[:, i, :], func=AF.Square,
                                                    accum_out=ss[:, i:i + 1]),
                 [xk, ssk], ["junk", ssk])
        S.op("dve", lambda e: e.tensor_scalar(out=rs[:, 0:nsub], in0=ss[:, 0:nsub], scalar1=1.0 / D, scalar2=EPS,
                                              op0=ALU.mult, op1=ALU.add), [ssk], [rsk])
        S.op("act", lambda e: e.activation(out=rs[:, 0:nsub], in_=rs[:, 0:nsub], func=AF.Sqrt), [rsk], [rsk])
        S.op("dve", lambda e: e.reciprocal(out=rs[:, 0:nsub], in_=rs[:, 0:nsub]), [rsk], [rsk])
        for i in range(nsub):
            S.op("act", lambda e, i=i: e.activation(out=xn[:, i, :], in_=xt[:, i, :], func=AF.Copy,
                                                    scale=rs[:, i:i + 1]), [xk, rsk], [xnk])
        for k in range(8):
            pT, pTk = R["pT"].next()
            for i in range(nsub):
                S.op("pe", lambda e, pT=pT, i=i, k=k: e.transpose(out=pT[:, i, :], in_=xn[:, i, k * 128:(k + 1) * 128],
                                                                  identity=ident[:]),
                     [xnk, "c_identb"], [pTk])
            S.op("dve", lambda e, pT=pT, k=k: e.tensor_scalar(
                out=xh[:, k, 0:nsub * 128], in0=pT[:, 0:nsub, :].rearrange("p i t -> p (i t)"), scalar1=A[:, who, k:k + 1],
                scalar2=B[:, who, k:k + 1], op0=ALU.mult, op1=ALU.add),
                 [pTk, tagA + "A%d" % who, tagB + "B%d" % who], [xhk])
        return xh, xhk

    def front_rings(self, S, nx=2):
        return {
            "xt": Ring(S, "xt", [128, 4, D], F32, nx),
            "ss": Ring(S, "ss", [128, 4], F32, 2),
            "rs": Ring(S, "rs", [128, 4], F32, 2),
            "xn": Ring(S, "xn", [128, 4, D], BF16, 2),
            "xh": Ring(S, "xh", [128, 8, 512], BF16, 2),
            "pT": Ring(S, "pT", [128, 4, 128], BF16, 2, psum=True),
            "junk": S.sb("junk", [128, D], F32),
        }

    def load_x(self, S, R, tile):
        c0, nsub = tile
        xt, xk = R["xt"].next()
        rk = [("xres", c) for c in range(c0, c0 + nsub)]
        S.dma("sp", xt[:, 0:nsub, :], self.xres[c0 * 128:(c0 + nsub) * 128, :].rearrange("(i p) d -> p i d", p=128),
              rk, [xk])
        return xt, xk

    def stage_ffn(self, l, f, tiles, extra_casts=(), final_norm=False):
        S = Stage(self, "ffn%d%d" % (l, f))
        si = 0 if f == 0 else 2
        for (dst, src, nrows, key) in extra_casts:
            self.cast_rows(S, dst, src, nrows, key)
        C = self.consts(S, ["identb"])
        ident = C["identb"]
        A, B = self.mod_cols(S, l, si, "m")
        G = self.gate_bc(S, l, si, "m", True)
        R = self.front_rings(S)
        w2 = S.sb("w2", [128, NJ, D], BF16)
        S.dma("sp", w2[:], self.w2b[l, f].rearrange("(j p) n -> p j n", p=128), [], ["w2"])
        H = S.sb("H", [128, NJ, 512], BF16)
        wring = Ring(S, "wi", [128, 8, 256], BF16, 4)
        sar = Ring(S, "sa", [128, 512], F32, 2)
        tmr = Ring(S, "tm", [128, 512], F32, 2)
        par = Ring(S, "pa", [128, 512], F32, 2, psum=True)
        pbr = Ring(S, "pb", [128, 512], F32, 2, psum=True)
        por = Ring(S, "po", [128, 512], F32, 2, psum=True)
        w1v = self.w1b[l, f].rearrange("(k p) c -> p k c", p=128)
        if final_norm:
            fg = S.sb("fg", [128, D], F32)
            S.load_bc(fg[:], "fg", self.final_g)
            fss, frs = S.sb("fss", [128, 4], F32), S.sb("frs", [128, 4], F32)
        def in_chunk(j, xh, xhk, ntok):
            wt, wk = wring.next()
            S.dma("sp", wt[:, :, 0:128], w1v[:, :, j * 128:(j + 1) * 128], [], [(wk, 0)])
            S.dma("sp", wt[:, :, 128:256], w1v[:, :, DFF + j * 128:DFF + (j + 1) * 128], [], [(wk, 128)])
            pa, pak = par.next()
            pb, pbk = pbr.next()
            for (pp, ppk, off) in ((pa, pak, 0), (pb, pbk, 128)):
                for k in range(8):
                    S.op("pe", lambda e, pp=pp, k=k, off=off: e.matmul(
                        pp[:, 0:ntok], lhsT=wt[:, k, off:off + 128], rhs=xh[:, k, 0:ntok],
                        start=(k == 0), stop=(k == 7)), [(wk, off), xhk], [ppk])
            sa, sak = sar.next()
            S.op("act", lambda e: e.activation(out=sa[:, 0:ntok], in_=pa[:, 0:ntok], func=AF.Silu), [pak], [sak])
            S.op("dve", lambda e: e.tensor_tensor(out=H[:, j, 0:ntok], in0=sa[:, 0:ntok], in1=pb[:, 0:ntok],
                                                  op=ALU.mult), [sak, pbk], ["H"])

        def out_chunk(i, n, xt, xk, who):
            po, pok = por.next()
            for j in range(NJ):
                S.op("pe", lambda e, j=j: e.matmul(
                    po[:], lhsT=H[:, j, i * 128:(i + 1) * 128], rhs=w2[:, j, n * 512:(n + 1) * 512],
                    start=(j == 0), stop=(j == NJ - 1)), ["H", "w2"], [pok])
            tm, tmk = tmr.next()
            S.op("dve", lambda e: e.tensor_tensor(out=tm[:], in0=po[:], in1=G[:, who, n * 512:(n + 1) * 512],
                                                  op=ALU.mult), [pok, "mG%d" % who], [tmk])
            S.op("pool", lambda e: e.tensor_tensor(out=xt[:, i, n * 512:(n + 1) * 512],
                                                   in0=xt[:, i, n * 512:(n + 1) * 512], in1=tm[:], op=ALU.add),
                 [tmk, xk], [xk])

        def fin_sub(i, xt, xk):
            S.op("act", lambda e: e.activation(out=R["junk"][:], in_=xt[:, i, :], func=AF.Square,
                                               accum_out=fss[:, i:i + 1]), [xk, "fss"], ["junk", "fss"])

        def fin_scale(i, xt, xk):
            S.op("dve", lambda e: e.scalar_tensor_tensor(out=xt[:, i, :], in0=xt[:, i, :], scalar=frs[:, i:i + 1],
                                                         in1=fg[:], op0=ALU.mult, op1=ALU.mult),
                 [xk, "frs", "fg"], [xk])

        def do_tile(tile, xt, xk):
            c0, nsub = tile
            who = 1 if c0 >= 32 else 0
            ntok = nsub * 128
            xh, xhk = self.norm_front(S, R, xt, xk, nsub, who, A, B, "m", "m", ident)
            for j in range(NJ):
                in_chunk(j, xh, xhk, ntok)
            for i in range(nsub):
                for n in range(2):
                    out_chunk(i, n, xt, xk, who)
            dst = self.out if final_norm else self.xres
            dkey = "out" if final_norm else "xres"
            if final_norm:
                S.op("dve", lambda e: e.memset(fss[:], 0.0), [], ["fss"])
                for i in range(nsub):
                    fin_sub(i, xt, xk)
                S.op("dve", lambda e: e.tensor_scalar(out=frs[:], in0=fss[:], scalar1=1.0 / D, scalar2=EPS,
                                                      op0=ALU.mult, op1=ALU.add), ["fss"], ["frs"])
                S.op("act", lambda e: e.activation(out=frs[:], in_=frs[:], func=AF.Sqrt), ["frs"], ["frs"])
                S.op("dve", lambda e: e.reciprocal(out=frs[:], in_=frs[:]), ["frs"], ["frs"])
                for i in range(nsub):
                    fin_scale(i, xt, xk)
            S.store("pool", dst[c0 * 128:(c0 + nsub) * 128, :].rearrange("(i p) d -> p i d", p=128),
                    xt[:, 0:nsub, :], [xk], [(dkey, c) for c in range(c0, c0 + nsub)], skey=("st", xk))

        nxt = self.load_x(S, R, tiles[0])
        for ti, tile in enumerate(tiles):
            xt, xk = nxt
            if ti + 1 < len(tiles):
                nxt = self.load_x(S, R, tiles[ti + 1])
            do_tile(tile, xt, xk)
        S.finish()

    def load_w(self, S, W, key, src, ncols):
        v = src.rearrange("(k p) c -> p k c", p=128)
        for k in range(8):
            S.dma("sp", W[:, k, 0:ncols], v[:, k, :], [], [(key, k)])
        return [(key, k) for k in range(8)]

    def stage_even_inproj(self):
        S = Stage(self, "ein")
        C = self.consts(S, ["identb", "onesb"])
        ident, onesb = C["identb"], C["onesb"]
        A, B = self.mod_cols(S, 0, 1, "m")
        R = self.front_rings(S)
        W = S.sb("W", [128, 8, EVEN_X], BF16)
        wkeys = self.load_w(S, W, "W", self.ewin_b, EVEN_X)
        gb = S.sb("gb", [128, 16], F32)
        S.load_bc(gb[:], "gb", self.gate_b)
        nacc = S.sb("nacc", [128, 8], F32)
        S.op("dve", lambda e: e.memset(nacc[:], 0.0), [], ["nacc"])
        rcr = Ring(S, "rc", [128, 512], F32, 2)
        rsr = Ring(S, "rs_", [128, 512], F32, 2)
        t1r = Ring(S, "t1", [128, 512], F32, 2)
        t2r = Ring(S, "t2", [128, 512], F32, 2)
        obr = Ring(S, "ob", [128, 512], BF16, 4)
        ofr = Ring(S, "of", [128, 512], F32, 3)
        sqr = Ring(S, "sq", [128, 512], BF16, 2)
        mxr = Ring(S, "mx", [128, 1], F32, 2)
        pjr = Ring(S, "pj", [128, 512], F32, 4, psum=True)
        pnr = Ring(S, "pn", [128, 512], F32, 2, psum=True)

        def proj_fm(col, xh, xhk, ntok):
            pj, pjk = pjr.next()
            for k in range(8):
                S.op("pe", lambda e, k=k: e.matmul(pj[:, 0:ntok], lhsT=W[:, k, col:col + 128], rhs=xh[:, k, 0:ntok],
                                                   start=(k == 0), stop=(k == 7)), [("W", k), xhk], [pjk])
            return pj, pjk

        def proj_tm(col, n, i, xh, xhk):
            pj, pjk = pjr.next()
            for k in range(8):
                S.op("pe", lambda e, k=k: e.matmul(pj[:, 0:n], lhsT=xh[:, k, i * 128:(i + 1) * 128],
                                                   rhs=W[:, k, col:col + n], start=(k == 0), stop=(k == 7)),
                     [("W", k), xhk], [pjk])
            return pj, pjk

        def rope_chunk(g, col, pcol, dst, ncol, xh, xhk, ntok, tok0, lat, rc, rck, rs, rsk):
            p1, p1k = proj_fm(col + g * 128, xh, xhk, ntok)
            ob, obk = obr.next()
            if lat:
                p2, p2k = proj_fm(pcol + g * 128, xh, xhk, ntok)
                t1, t1k = t1r.next()
                t2, t2k = t2r.next()
                S.op("dve", lambda e: e.tensor_tensor(out=t1[:, 0:ntok], in0=p1[:, 0:ntok], in1=rc[:, 0:ntok],
                                                      op=ALU.mult), [p1k, rck], [t1k])
                S.op("dve", lambda e: e.tensor_tensor(out=t2[:, 0:ntok], in0=p2[:, 0:ntok], in1=rs[:, 0:ntok],
                                                      op=ALU.mult), [p2k, rsk], [t2k])
                S.op("pool", lambda e: e.tensor_tensor(out=ob[:, 0:ntok], in0=t1[:, 0:ntok], in1=t2[:, 0:ntok],
                                                       op=ALU.add), [t1k, t2k], [obk])
            else:
                S.op("act", lambda e: e.activation(out=ob[:, 0:ntok], in_=p1[:, 0:ntok], func=AF.Copy), [p1k], [obk])
            S.store("pool", dst[g * 128:(g + 1) * 128, tok0:tok0 + ntok], ob[:, 0:ntok], [obk], [("fm", id(dst), g, tok0)],
                    skey=("st", obk))

            def norm_part():
                sq, sqk = sqr.next()
                S.op("act", lambda e: e.activation(out=sq[:, 0:ntok], in_=ob[:, 0:ntok], func=AF.Square), [obk], [sqk])
                pn, pnk = pnr.next()
                S.op("pe", lambda e: e.matmul(pn[:, 0:ntok], lhsT=onesb[:], rhs=sq[:, 0:ntok], start=True, stop=True),
                     [sqk, "c_onesb"], [pnk])
                mx, mxk = mxr.next()
                S.op("dve", lambda e: e.reduce_max(out=mx[:], in_=pn[:, 0:ntok], axis=AX.X), [pnk], [mxk])
                S.op("dve", lambda e: e.tensor_tensor(out=nacc[:, ncol:ncol + 1], in0=nacc[:, ncol:ncol + 1], in1=mx[:],
                                                      op=ALU.max), [mxk, "nacc"], ["nacc"])
            return norm_part

        def mqk_chunk(g, xh, xhk, ntok, tok0):
            p1, p1k = proj_fm(1536 + g * 128, xh, xhk, ntok)
            of, ofk = ofr.next()
            S.op("act", lambda e: e.activation(out=of[:, 0:ntok], in_=p1[:, 0:ntok], func=AF.Copy), [p1k], [ofk])
            S.store("pool", self.MQK[g * 128:(g + 1) * 128, tok0:tok0 + ntok], of[:, 0:ntok], [ofk], [("mqk", g, tok0)],
                    skey=("st", ofk))

        def tm_sub(i, xh, xhk, tok0):
            r0 = tok0 + i * 128
            for (col, dst, bf) in ((1024, self.Vs, True), (2048, self.MV, True), (2560, self.OPRE, False)):
                pj, pjk = proj_tm(col, 512, i, xh, xhk)
                if bf:
                    ob, obk = obr.next()
                    S.op("act", lambda e, ob=ob, pj=pj: e.activation(out=ob[:], in_=pj[:], func=AF.Copy), [pjk], [obk])
                    S.store("pool", dst[r0:r0 + 128, :], ob[:], [obk], [("tm", col, r0)], skey=("st", obk))
                else:
                    of, ofk = ofr.next()
                    S.op("act", lambda e, of=of, pj=pj: e.activation(out=of[:], in_=pj[:], func=AF.Copy), [pjk], [ofk])
                    S.store("pool", dst[r0:r0 + 128, :], of[:], [ofk], [("tm", col, r0)], skey=("st", ofk))
            pj, pjk = proj_tm(3072, 16, i, xh, xhk)
            of, ofk = ofr.next()
            S.op("dve", lambda e: e.tensor_tensor(out=of[:, 0:16], in0=pj[:, 0:16], in1=gb[:], op=ALU.add),
                 [pjk, "gb"], [ofk])
            S.store("pool", self.GATE[r0:r0 + 128, :], of[:, 0:16], [ofk], [("tmg", r0)], skey=("st", ofk))

        def do_tile(tile, xt, xk):
            c0, nsub = tile
            lat = c0 < 32
            who = 0 if lat else 1
            ntok, tok0 = nsub * 128, c0 * 128
            rc = rck = rs = rsk = None
            if lat:
                rc, rck = rcr.next()
                rs, rsk = rsr.next()
                S.dma("sp", rc[:, 0:ntok], self.rope_c[:, tok0:tok0 + ntok], [], [rck])
                S.dma("sp", rs[:, 0:ntok], self.rope_s[:, tok0:tok0 + ntok], [], [rsk])
            xh, xhk = self.norm_front(S, R, xt, xk, nsub, who, A, B, "m", "m", ident)
            prev = None
            for g in range(4):
                cur_ = rope_chunk(g, 0, 3088, self.QT, g, xh, xhk, ntok, tok0, lat, rc, rck, rs, rsk)
                if prev is not None:
                    prev()
                prev = cur_
            for g in range(4):
                cur_ = rope_chunk(g, 512, 3600, self.KT, 4 + g, xh, xhk, ntok, tok0, lat, rc, rck, rs, rsk)
                prev()
                prev = cur_
            prev()
            for g in range(4):
                mqk_chunk(g, xh, xhk, ntok, tok0)
            for i in range(nsub):
                tm_sub(i, xh, xhk, tok0)

        nxt = self.load_x(S, R, ALL_TILES[0])
        for ti, tile in enumerate(ALL_TILES):
            xt, xk = nxt
            if ti + 1 < len(ALL_TILES):
                nxt = self.load_x(S, R, ALL_TILES[ti + 1])
            do_tile(tile, xt, xk)
        S.store("sp", self.nrm[:, 0:8], nacc[:], ["nacc"], [("nrm", 0)])
        S.finish()

    def stage_attn(self, heads=(0, 1, 2, 3), tiles=None):
        tiles = ALL_TILES if tiles is None else tiles
        S = Stage(self, "att")
        lam_init = 0.8 - 0.6 * math.exp(-0.3 * 0)
        nr = S.sb("nr", [128, 8], F32)
        negM = S.sb("negM", [128, 4], F32)
        S.dma("sp", nr[:], self.nrm[:, 0:8], [], ["nr"])
        S.op("dve", lambda e: e.tensor_tensor(out=negM[:], in0=nr[:, 0:4], in1=nr[:, 4:8], op=ALU.mult), ["nr"], ["negM"])
        S.op("act", lambda e: e.activation(out=negM[:], in_=negM[:], func=AF.Sqrt), ["negM"], ["negM"])
        S.op("dve", lambda e: e.tensor_scalar(out=negM[:], in0=negM[:], scalar1=-0.125, scalar2=None, op0=ALU.mult),
             ["negM"], ["negM"])
        dl = S.sb("dl", [1, 256], F32)
        lw = S.sb("lw", [1, 8], F32)
        S.dma("sp", dl[:], self.diff_lambda, [], ["dl"])
        S.op("dve", lambda e: e.memset(lw[:], 0.0), [], ["lw"])
        S.op("dve", lambda e: e.tensor_tensor(out=dl[:, 0:64], in0=dl[:, 0:64], in1=dl[:, 64:128], op=ALU.mult),
             ["dl"], ["dl"])
        S.op("dve", lambda e: e.tensor_tensor(out=dl[:, 128:192], in0=dl[:, 128:192], in1=dl[:, 192:256], op=ALU.mult),
             ["dl"], ["dl"])
        S.op("dve", lambda e: e.reduce_sum(out=lw[:, 0:1], in_=dl[:, 0:64], axis=AX.X), ["dl"], ["lw"])
        S.op("dve", lambda e: e.reduce_sum(out=lw[:, 1:2], in_=dl[:, 128:192], axis=AX.X), ["dl", "lw"], ["lw"])
        S.op("act", lambda e: e.activation(out=lw[:, 2:4], in_=lw[:, 0:2], func=AF.Exp), ["lw"], ["lw"])
        S.op("dve", lambda e: e.tensor_tensor(out=lw[:, 4:5], in0=lw[:, 3:4], in1=lw[:, 2:3], op=ALU.subtract),
             ["lw"], ["lw"])
        S.op("dve", lambda e: e.tensor_scalar(out=lw[:, 5:6], in0=lw[:, 4:5], scalar1=-lam_init, scalar2=None,
                                              op0=ALU.add), ["lw"], ["lw"])
        S.dma("sp", self.nrm[0:1, 8:9], lw[:, 5:6], ["lw"], [("nrm", 8)])
        dgs = S.sb("dgs", [128, 128], F32)
        S.load_bc(dgs[:], "dgs", self.diff_norm_g)
        S.op("pool", lambda e: e.tensor_scalar(out=dgs[:], in0=dgs[:], scalar1=1.0 - lam_init, scalar2=None,
                                               op0=ALU.mult), ["dgs"], ["dgs"])
        identf = self.consts(S, ["ident"])["ident"]
        ktr = Ring(S, "kt", [128, TT], BF16, 2)
        qtr = Ring(S, "qt", [128, TT], BF16, 2)
        vr = Ring(S, "vh", [128, NCH, 128], BF16, 2)
        Esel = S.sb("Esel", [128, 2, 2], BF16)
        S.op("pool", lambda e: e.memset(Esel[:], 0.0), [], ["Esel"])
        for c in range(2):
            S.op("pool", lambda e, c=c: e.memset(Esel[:, c, c:c + 1], 1.0), ["Esel"], ["Esel"])
        lamc = S.sb("lamc", [2, 1], F32)
        S.op("pool", lambda e: e.memset(lamc[:], 1.0), [], ["lamc"])
        S.dma("sp", lamc[1:2, :], self.nrm[0:1, 8:9], [("nrm", 8), "lamc"], ["lamc"])
        ptr = Ring(S, "pt", [128, 512], BF16, 6)
        spr = Ring(S, "sp", [128, 512], F32, 4, psum=True)
        oTp = [(S.ps("oT%d" % c, [128, 512], F32), "oT%d" % c) for c in range(2)]
        Zp = S.ps("Zp", [2, 512], F32)
        tpp = S.ps("tp", [128, 2, 132], F32)
        oTs = Ring(S, "oTs", [128, 2, 512], F32, 2)
        Zs = Ring(S, "Zs", [2, 512], F32, 2)
        mor = Ring(S, "mo", [128, 4, 128], BF16, 2)
        sm = Ring(S, "sm", [128, 8], F32, 4)
        otr = Ring(S, "ot", [128, 128], F32, 3)
        junk = S.sb("junk", [128, 128], F32)

        def s_mm(c, kc, kt, ktk, qt, qtk, q0, nq):
            sp, spk = spr.next()
            S.op("pe", lambda e: e.matmul(sp[:, 0:nq], lhsT=kt[c * 64:(c + 1) * 64, kc * 128:(kc + 1) * 128],
                                          rhs=qt[c * 64:(c + 1) * 64, q0:q0 + nq], start=True, stop=True),
                 [ktk, qtk], [spk])
            return sp, spk

        def p_mm(h, c, kc, first, last, sp, spk, vt, vk, nq):
            pt, ptk = ptr.next()
            S.op("act", lambda e: e.activation(out=pt[:, 0:nq], in_=sp[:, 0:nq], func=AF.Exp, bias=negM[:, h:h + 1],
                                               scale=0.125), [spk, "negM"], [ptk])
            oT, oTk = oTp[c]
            S.op("pe", lambda e: e.matmul(oT[:, 0:nq], lhsT=vt[:, kc, :], rhs=pt[:, 0:nq], start=first, stop=last),
                 [ptk, vk], [oTk])
            S.op("pe", lambda e: e.matmul(Zp[:, 0:nq], lhsT=Esel[:, c, :], rhs=pt[:, 0:nq],
                                          start=(first and c == 0), stop=(last and c == 1)), [ptk, "Esel"], ["Zp"])

        def fin_sub(h, qs, ots, otsk, zs, zsk, mo, mok):
            q_ = slice(qs * 128, (qs + 1) * 128)
            for c in range(2):
                S.op("pe", lambda e, c=c: e.transpose(out=tpp[:, c, 0:128], in_=ots[:, c, q_], identity=identf[:]),
                     otsk + ["c_identf"], ["tp"])
            S.op("pe", lambda e: e.transpose(out=tpp[:, 0, 128:130], in_=zs[0:2, q_], identity=identf[0:2, 0:2]),
                 [zsk, "c_identf"], ["tp"])
            s_, sk = sm.next()
            ot, otk = otr.next()
            S.op("dve", lambda e: e.tensor_copy(out=s_[:, 0:2], in_=tpp[:, 0, 128:130]), ["tp"], [sk])
            S.op("act", lambda e: e.activation(out=ot[:], in_=tpp[:, 0, 0:128], func=AF.Copy, scale=s_[:, 0:1]),
                 ["tp", sk], [otk])
            S.op("dve", lambda e: e.scalar_tensor_tensor(out=ot[:], in0=tpp[:, 1, 0:128], scalar=s_[:, 1:2], in1=ot[:],
                                                         op0=ALU.mult, op1=ALU.add), ["tp", sk, otk], [otk])
            S.op("dve", lambda e: e.memset(s_[:, 3:4], 0.0), [sk], [sk])
            S.op("act", lambda e: e.activation(out=junk[:], in_=ot[:], func=AF.Square, accum_out=s_[:, 3:4]),
                 [otk, sk], ["junk", sk])
            S.op("dve", lambda e: e.tensor_scalar(out=s_[:, 4:5], in0=s_[:, 3:4], scalar1=1.0 / 128, scalar2=EPS,
                                                  op0=ALU.mult, op1=ALU.add), [sk], [sk])
            S.op("act", lambda e: e.activation(out=s_[:, 4:5], in_=s_[:, 4:5], func=AF.Ln), [sk], [sk])
            S.op("act", lambda e: e.activation(out=s_[:, 5:6], in_=s_[:, 4:5], func=AF.Exp, scale=-0.5), [sk], [sk])
            S.op("dve", lambda e: e.scalar_tensor_tensor(out=mo[:, qs, :], in0=ot[:], scalar=s_[:, 5:6], in1=dgs[:],
                                                         op0=ALU.mult, op1=ALU.mult), [otk, sk, "dgs"], [mok])

        def qtile(h, tile, kt, ktk, qt, qtk, vt, vk):
            c0, nsub = tile
            q0, nq = c0 * 128, nsub * 128
            keys = list(range(NCH)) if c0 < 32 else [32, 33]
            nxt = [s_mm(c, keys[0], kt, ktk, qt, qtk, q0, nq) for c in range(2)]
            for ki, kc in enumerate(keys):
                cur = nxt
                if ki + 1 < len(keys):
                    nxt = [s_mm(c, keys[ki + 1], kt, ktk, qt, qtk, q0, nq) for c in range(2)]
                for c in range(2):
                    p_mm(h, c, kc, ki == 0, ki == len(keys) - 1, cur[c][0], cur[c][1], vt, vk, nq)
            ots, otsk = oTs.next()
            zs, zsk = Zs.next()
            S.op("act", lambda e: e.activation(out=ots[:, 0, 0:nq], in_=oTp[0][0][:, 0:nq], func=AF.Copy), ["oT0"],
                 [(otsk, 0)])
            S.op("dve", lambda e: e.tensor_copy(out=ots[:, 1, 0:nq], in_=oTp[1][0][:, 0:nq]), ["oT1"], [(otsk, 1)])
            S.op("dve", lambda e: e.reciprocal(out=zs[:, 0:nq], in_=Zp[:, 0:nq]), ["Zp"], [zsk])
            S.op("dve", lambda e: e.tensor_scalar(out=zs[:, 0:nq], in0=zs[:, 0:nq], scalar1=lamc[:, 0:1], scalar2=None,
                                                  op0=ALU.mult), [zsk, "lamc"], [zsk])
            mo, mok = mor.next()
            for qs in range(nsub):
                fin_sub(h, qs, ots, [(otsk, 0), (otsk, 1)], zs, zsk, mo, mok)
            S.store("pool", self.merged[q0:q0 + nsub * 128, h * 128:(h + 1) * 128].rearrange("(i p) e -> p i e", p=128),
                    mo[:, 0:nsub, :], [mok], [("mrg", h, c0)], skey=("st", mok))

        def head(h):
            kt, ktk = ktr.next()
            qt, qtk = qtr.next()
            vt, vk = vr.next()
            S.dma("sp", kt[:], self.KT[h * 128:(h + 1) * 128, :], [], [ktk])
            S.dma("sp", qt[:], self.QT[h * 128:(h + 1) * 128, :], [], [qtk])
            S.dma("sp", vt[:], self.Vs[:, h * 128:(h + 1) * 128].rearrange("(c p) e -> p c e", p=128), [], [vk])
            for tile in tiles:
                qtile(h, tile, kt, ktk, qt, qtk, vt, vk)

        for h in heads:
            head(h)
        S.finish()

    def stage_mlprep(self, chunks=None):
        chunks = list(range(NCH)) if chunks is None else chunks
        S = Stage(self, "mlp")
        C = self.consts(S, ["identb"])
        ident = C["identb"]
        cw = S.sb("cw", [128, 3, 4], F32)
        cb = S.sb("cb", [128, 4], F32)
        for j in range(3):
            S.load_cols(cw[:, j, :], ("cw", j), self.conv_w[j:j + 1, :])
        S.load_cols(cb[:], "cb", self.conv_b)
        xqr = Ring(S, "xq", [128, 4, 130], F32, 3)
        for (xq, xqk) in xqr.bufs:
            S.op("pool", lambda e, xq=xq: e.memset(xq[:], 0.0), [], [xqk])
        acr = Ring(S, "ac", [128, 128], F32, 3)
        qkr = Ring(S, "qk", [128, 4, 128], BF16, 3)
        ktr = Ring(S, "ktm", [128, 256], BF16, 3)
        pkr = Ring(S, "pk", [128, 256], BF16, 2, psum=True)
        mqv = self.MQK.rearrange("(g p) t -> p g t", p=128)
        mqbv = self.MQKB.rearrange("(g p) t -> p g t", p=128)

        def conv_g(g, xq, xqk, qk, qkk):
            ac, ack = acr.next()
            S.op("dve", lambda e: e.tensor_scalar(out=ac[:], in0=xq[:, g, 0:128], scalar1=cw[:, 0, g:g + 1], scalar2=None,
                                                  op0=ALU.mult), [xqk, ("cw", 0)], [ack])
            for j in (1, 2):
                S.op("dve", lambda e, j=j: e.scalar_tensor_tensor(out=ac[:], in0=xq[:, g, j:j + 128],
                                                                  scalar=cw[:, j, g:g + 1], in1=ac[:], op0=ALU.mult,
                                                                  op1=ALU.add), [xqk, ("cw", j), ack], [ack])
            S.op("act", lambda e: e.activation(out=qk[:, g, :], in_=ac[:], func=AF.Silu, bias=cb[:, g:g + 1]),
                 [ack, "cb"], [(qkk, g)])

        def chunk(c):
            t0 = c * 128
            lo = t0 - 1 if c not in (0, 32) else t0
            hi = t0 + 129 if c not in (31, 33) else t0 + 128
            xq, xqk = xqr.next()
            if lo == t0:
                S.op("pool", lambda e: e.memset(xq[:, :, 0:1], 0.0), [], [xqk])
            if hi == t0 + 128:
                S.op("pool", lambda e: e.memset(xq[:, :, 129:130], 0.0), [], [xqk])
            S.dma("sp", xq[:, :, 1 - (t0 - lo):1 - (t0 - lo) + (hi - lo)], mqv[:, :, lo:hi], [], [xqk])
            qk, qkk = qkr.next()
            for g in range(4):
                conv_g(g, xq, xqk, qk, qkk)
            S.store("pool", mqbv[:, :, t0:t0 + 128], qk[:], [(qkk, g) for g in range(4)], [("mqkb", c)],
                    skey=("st", qkk))
            pk, pkk = pkr.next()
            for g in (2, 3):
                S.op("pe", lambda e, g=g: e.transpose(out=pk[:, (g - 2) * 128:(g - 1) * 128], in_=qk[:, g, :],
                                                      identity=ident[:]), [(qkk, g), "c_identb"], [pkk])
            kt, ktk = ktr.next()
            S.op("dve", lambda e: e.tensor_copy(out=kt[:], in_=pk[:]), [pkk], [ktk])
            S.store("pool", self.MKT[t0:t0 + 128, :], kt[:], [ktk], [("mkt", c)], skey=("st", ktk))

        for c in chunks:
            chunk(c)
        gr = Ring(S, "g", [128, NCH, 8], F32, 1)
        g_, gk = gr.next()
        gv = self.GATE.rearrange("(c p) k -> p c k", p=128)
        S.dma("sp", g_[:], gv[:, :, 8:16], [], [gk])
        S.op("act", lambda e: e.activation(out=g_[:], in_=g_[:], func=AF.Exp, scale=-1.0), [gk], [gk])
        S.op("act", lambda e: e.activation(out=g_[:], in_=g_[:], func=AF.Ln, bias=1.0), [gk], [gk])
        S.op("dve", lambda e: e.tensor_scalar(out=g_[:], in0=g_[:], scalar1=-1.0, scalar2=None, op0=ALU.mult), [gk], [gk])
        S.store("sp", gv[:, :, 8:16], g_[:], [gk], [("lf", 0)])
        S.finish()

    def stage_mlstm(self, order_f=None, order_b=None):
        order_f = [32, 33] + list(range(32)) if order_f is None else order_f
        order_b = [33, 32] + list(range(31, -1, -1)) if order_b is None else order_b
        S = Stage(self, "mls")
        C = self.consts(S, ["U", "L", "ones"])
        Uf, Lf, onesf = C["U"], C["L"], C["ones"]
        mg = S.sb("mg", [128, 512], F32)
        S.load_bc(mg[:], "mg", self.ml_norm_g)
        hsum = S.sb("hsum", [128, NCH, 512], F32)
        Cst = S.sb("Cst", [128, 2, 132], F32)
        Cbf = S.sb("Cbf", [128, 2, 132], BF16)
        qkr = Ring(S, "qk", [128, 4, 128], BF16, 3)
        ktr = Ring(S, "ktm", [128, 256], BF16, 3)
        var = Ring(S, "va", [128, 4, 132], BF16, 3)
        for (va, vak) in var.bufs:
            S.op("pool", lambda e, va=va: e.memset(va[:, :, 128:129], 1.0), [], [(vak, "one")])
        gtr = Ring(S, "gt", [128, 16], F32, 4)
        csr = Ring(S, "cs", [128, 24], F32, 4)
        lfrr = Ring(S, "lfr", [128, 128], F32, 6)
        dmr = Ring(S, "dm", [128, 128], F32, 6)
        er = Ring(S, "E", [128, 128], F32, 6)
        epr = Ring(S, "Ep", [128, 128], F32, 6)
        ebr = Ring(S, "eb", [128, 128], F32, 6)
        qtr = Ring(S, "qtl", [128, 128], BF16, 10)
        atr = Ring(S, "AT", [128, 128], BF16, 10)
        khr = Ring(S, "kh", [128, 128], BF16, 10)
        rr = Ring(S, "r", [128, 2], F32, 8)
        bankX = S.ps("pDX", [128, 512], F32)
        pcs = bankX[:, 256:264]
        pBr = PRing(S, "pB", 128, 2, 1)
        pSr = PRing(S, "pS", 128, 2, 1)
        par = PRing(S, "pacc", 132, 2, 1)
        pDr = PRing(S, "pD", 132, 1, 1)
        pDr.bufs.append((bankX[:, 0:132], "pcs"))
        mqbv = self.MQKB.rearrange("(g p) t -> p g t", p=128)
        LN8 = math.log(0.125)

        def head_front(d, c, h, X):
            Mx, Mk, qk, qkk, kt, ktk, va, vak, cs, csk = X["base"]
            rows = slice((h % 2) * 64, (h % 2) * 64 + 64)
            hp = h // 2
            lfr, lfrk = lfrr.next()
            S.op("dve", lambda e: e.tensor_scalar(out=lfr[:], in0=onesf[:], scalar1=cs[:, 16 + h:17 + h], scalar2=None,
                                                  op0=ALU.mult), ["c_onesf", csk], [lfrk])
            pB, pBk = pBr.next()
            yield
            S.op("pe", lambda e: e.matmul(pB[:], lhsT=lfr[:], rhs=Mx[:], start=True, stop=True), [lfrk, Mk], [pBk])
            dm, dmk = dmr.next()
            yield
            S.op("dve", lambda e: e.tensor_scalar(out=dm[:], in0=pB[:], scalar1=cs[:, h:h + 1], scalar2=0.0,
                                                  op0=ALU.subtract, op1=ALU.min), [pBk, csk], [dmk])
            E, Ek = er.next()
            yield
            S.op("act", lambda e: e.activation(out=E[:], in_=dm[:], func=AF.Exp, bias=cs[:, 12 + h:13 + h]),
                 [dmk, csk], [Ek])
            Ep, Epk = epr.next()
            yield
            S.op("pool", lambda e: e.tensor_tensor(out=Ep[:], in0=E[:], in1=Mx[:], op=ALU.mult), [Ek, Mk], [Epk])
            eb, ebk = ebr.next()
            yield
            S.op("act", lambda e: e.activation(out=eb[rows, :], in_=pB[rows, :], func=AF.Exp), [pBk], [ebk])
            qt, qtk = qtr.next()
            yield
            S.op("dve", lambda e: e.scalar_tensor_tensor(out=qt[rows, :], in0=qk[rows, hp, :], scalar=0.125,
                                                         in1=eb[rows, :], op0=ALU.mult, op1=ALU.mult),
                 [(qkk, 0), ebk], [qtk])
            pS, pSk = pSr.next()
            yield
            S.op("pe", lambda e: e.matmul(pS[:], lhsT=qk[rows, 2 + hp, :], rhs=qk[rows, hp, :], start=True, stop=True),
                 [(qkk, 0)], [pSk])
            AT, ATk = atr.next()
            yield
            S.op("dve", lambda e: e.tensor_tensor(out=AT[:], in0=pS[:], in1=Ep[:], op=ALU.mult), [pSk, Epk], [ATk])
            kh, khk = khr.next()
            yield
            S.op("dve", lambda e: e.tensor_scalar(out=kh[:], in0=kt[:, hp * 128:(hp + 1) * 128],
                                                  scalar1=cs[:, 8 + h:9 + h], scalar2=None, op0=ALU.mult),
                 [ktk, csk], [khk])
            X[h] = (qt, qtk, AT, ATk, kh, khk)

        def head_back(d, c, h, X):
            Mx, Mk, qk, qkk, kt, ktk, va, vak, cs, csk = X["base"]
            qt, qtk, AT, ATk, kh, khk = X[h]
            rows = slice((h % 2) * 64, (h % 2) * 64 + 64)
            hp = h // 2
            pa, pak = par.next()
            S.op("pe", lambda e: e.matmul(pa[:, 0:129], lhsT=AT[:], rhs=va[:, h, 0:129], start=True, stop=False),
                 [ATk, vak, (vak, "one")], [pak])
            S.op("pe", lambda e: e.matmul(pa[:, 0:129], lhsT=qt[rows, :], rhs=Cbf[rows, hp, 0:129], start=False,
                                          stop=True), [qtk, ("Cbf", h)], [pak])
            pD, pDk = pDr.next()
            yield
            S.op("pe", lambda e: e.matmul(pD[:, 0:129], lhsT=kh[:], rhs=va[:, h, 0:129], start=True, stop=True),
                 [khk, vak, (vak, "one")], [pDk])
            yield
            S.op("dve", lambda e: e.scalar_tensor_tensor(out=Cst[rows, hp, 0:129], in0=Cst[rows, hp, 0:129],
                                                         scalar=cs[rows, 20 + h:21 + h], in1=pD[rows, 0:129],
                                                         op0=ALU.mult, op1=ALU.add), [("Cst", h), csk, pDk], [("Cst", h)])
            yield
            S.op("act", lambda e: e.activation(out=Cbf[rows, hp, 0:129], in_=Cst[rows, hp, 0:129], func=AF.Copy),
                 [("Cst", h)], [("Cbf", h)])
            r, rk = rr.next()
            yield
            S.op("dve", lambda e: e.tensor_scalar(out=r[:, 0:1], in0=pa[:, 128:129], scalar1=-1.0, scalar2=1.0,
                                                  op0=ALU.mult, op1=ALU.max), [pak], [rk])
            yield
            S.op("dve", lambda e: e.tensor_tensor(out=r[:, 0:1], in0=r[:, 0:1], in1=pa[:, 128:129], op=ALU.max),
                 [pak, rk], [rk])
            yield
            S.op("dve", lambda e: e.reciprocal(out=r[:, 1:2], in_=r[:, 0:1]), [rk], [rk])
            hs = hsum[:, c, h * 128:(h + 1) * 128]
            yield
            if d == 0:
                S.op("act", lambda e: e.activation(out=hs, in_=pa[:, 0:128], func=AF.Copy, scale=r[:, 1:2]),
                     [pak, rk], [("hs", c, h)])
            else:
                S.op("dve", lambda e: e.scalar_tensor_tensor(out=hs, in0=pa[:, 0:128], scalar=r[:, 1:2], in1=hs,
                                                             op0=ALU.mult, op1=ALU.add), [pak, rk, ("hs", c, h)],
                     [("hs", c, h)])

        def chunk(d, c):
            t0 = c * 128
            Mx, Mk = (Uf, "c_Uf") if d == 0 else (Lf, "c_Lf")
            qk, qkk = qkr.next()
            kt, ktk = ktr.next()
            va, vak = var.next()
            gt, gtk = gtr.next()
            S.dma("sp", qk[:], mqbv[:, :, t0:t0 + 128], [], [(qkk, 0)])
            S.dma("sp", kt[:], self.MKT[t0:t0 + 128, :], [], [ktk])
            S.dma("sp", va[:, :, 0:128], self.MV[t0:t0 + 128, :].rearrange("p (h e) -> p h e", h=4), [], [vak])
            S.dma("sp", gt[:], self.GATE[t0:t0 + 128, :], [], [gtk])
            cs, csk = csr.next()
            S.op("act", lambda e: e.activation(out=cs[:, 16:20], in_=gt[:, 8 + 4 * d:12 + 4 * d], func=AF.Copy),
                 [gtk], [(csk, "lf")])
            S.op("pe", lambda e: e.matmul(pcs[:, 0:4], lhsT=Mx[:], rhs=cs[:, 16:20], start=True, stop=True),
                 [Mk, (csk, "lf")], ["pcs"])
            S.op("pe", lambda e: e.matmul(pcs[:, 4:8], lhsT=onesf[:], rhs=cs[:, 16:20], start=True, stop=True,
                                          skip_group_check=True), ["c_onesf", (csk, "lf")], ["pcs"])
            S.op("dve", lambda e: e.tensor_copy(out=cs[:, 0:8], in_=pcs[:, 0:8]), ["pcs"], [csk])
            S.op("dve", lambda e: e.tensor_tensor(out=cs[:, 8:12], in0=cs[:, 4:8], in1=cs[:, 0:4], op=ALU.subtract),
                 [csk], [csk])
            S.op("dve", lambda e: e.tensor_tensor(out=cs[:, 8:12], in0=cs[:, 8:12], in1=gt[:, 4 * d:4 * d + 4],
                                                  op=ALU.add), [csk, gtk], [csk])
            S.op("act", lambda e: e.activation(out=cs[:, 8:12], in_=cs[:, 8:12], func=AF.Exp), [csk], [csk])
            S.op("dve", lambda e: e.tensor_scalar(out=cs[:, 12:16], in0=gt[:, 4 * d:4 * d + 4], scalar1=LN8,
                                                  scalar2=None, op0=ALU.add), [gtk, csk], [csk])
            S.op("act", lambda e: e.activation(out=cs[:, 20:24], in_=cs[:, 4:8], func=AF.Exp), [csk], [csk])
            return {"base": (Mx, Mk, qk, qkk, kt, ktk, va, vak, cs, csk)}

        rsr = Ring(S, "fr", [128, 12], F32, 2)
        opr = Ring(S, "op", [128, 512], F32, 2)
        yr = Ring(S, "y", [128, 512], F32, 2)
        mor = Ring(S, "mo", [128, 512], BF16, 2)
        junk = S.sb("junk", [128, 128], F32)

        def fin_head(c, h, fr, frk, op_, opk, y, yk):
            hs = hsum[:, c, h * 128:(h + 1) * 128]
            S.op("dve", lambda e: e.scalar_tensor_tensor(out=y[:, h * 128:(h + 1) * 128], in0=hs, scalar=fr[:, 8 + h:9 + h],
                                                         in1=mg[:, h * 128:(h + 1) * 128], op0=ALU.mult, op1=ALU.mult),
                 [("hs", c, h), frk, "mg"], [(yk, h)])

        def finalize(c):
            fr, frk = rsr.next()
            S.op("dve", lambda e: e.memset(fr[:], 0.0), [], [frk])
            for h in range(4):
                S.op("act", lambda e, h=h: e.activation(out=junk[:], in_=hsum[:, c, h * 128:(h + 1) * 128], func=AF.Square,
                                                        accum_out=fr[:, h:h + 1]), [("hs", c, h), frk], ["junk", frk])
            S.op("dve", lambda e: e.tensor_scalar(out=fr[:, 4:8], in0=fr[:, 0:4], scalar1=1.0 / 128, scalar2=EPS,
                                                  op0=ALU.mult, op1=ALU.add), [frk], [frk])
            S.op("act", lambda e: e.activation(out=fr[:, 4:8], in_=fr[:, 4:8], func=AF.Ln), [frk], [frk])
            S.op("act", lambda e: e.activation(out=fr[:, 8:12], in_=fr[:, 4:8], func=AF.Exp, scale=-0.5), [frk], [frk])
            op_, opk = opr.next()
            S.dma("sp", op_[:], self.OPRE[c * 128:(c + 1) * 128, :], [], [opk])
            S.op("act", lambda e: e.activation(out=op_[:], in_=op_[:], func=AF.Exp, scale=-1.0), [opk], [opk])
            S.op("pool", lambda e: e.tensor_scalar(out=op_[:], in0=op_[:], scalar1=1.0, scalar2=None, op0=ALU.add),
                 [opk], [opk])
            S.op("dve", lambda e: e.reciprocal(out=op_[:], in_=op_[:]), [opk], [opk])
            y, yk = yr.next()
            for h in range(4):
                fin_head(c, h, fr, frk, op_, opk, y, yk)
            mo, mok = mor.next()
            S.op("pool", lambda e: e.tensor_tensor(out=mo[:], in0=y[:], in1=op_[:], op=ALU.mult),
                 [(yk, h) for h in range(4)] + [opk], [mok])
            S.store("pool", self.merged[c * 128:(c + 1) * 128, 512:1024], mo[:], [mok], [("mrg2", c)], skey=("st", mok))

        for d, order in ((0, order_f), (1, order_b)):
            S.op("dve", lambda e: e.memset(Cst[:], 0.0), [("Cst", h) for h in range(4)], [("Cst", h) for h in range(4)])
            S.op("pool", lambda e: e.memset(Cbf[:], 0.0), [("Cbf", h) for h in range(4)], [("Cbf", h) for h in range(4)])
            if not order:
                continue
            X = chunk(d, order[0])
            for hh in (0, 2):
                rr_run([head_front(d, order[0], h, X) for h in (hh, hh + 1)])
            for i, c in enumerate(order):
                nc_ = order[i + 1] if i + 1 < len(order) else None
                Xn = chunk(d, nc_) if nc_ is not None else None
                for hh in (0, 2):
                    gens = [head_back(d, c, h, X) for h in (hh, hh + 1)]
                    if nc_ is not None:
                        gens += [head_front(d, nc_, h, Xn) for h in (hh, hh + 1)]
                    rr_run(gens)
                if d == 1:
                    finalize(c)
                X = Xn
        S.finish()

    def stage_outproj(self, l, wsrc, tiles):
        S = Stage(self, "op%d" % l)
        C = self.consts(S, ["identb"])
        ident = C["identb"]
        G = self.gate_bc(S, l, 1, "m", False)
        Wo = S.sb("Wo", [128, 8, D], BF16)
        self.load_w(S, Wo, "Wo", wsrc, D)
        xtr = Ring(S, "xt", [128, 4, D], F32, 2)
        mr = Ring(S, "m", [128, 4, D], BF16, 2)
        mTr = Ring(S, "mT", [128, 8, 512], BF16, 2)
        tmr = Ring(S, "tm", [128, 512], F32, 2)
        pTr = Ring(S, "pT", [128, 4, 128], BF16, 2, psum=True)
        por = Ring(S, "po", [128, 512], F32, 4, psum=True)

        def tr_k(k, m, mk_, mT, mTk, nsub):
            pT, pTk = pTr.next()
            for i in range(nsub):
                S.op("pe", lambda e, i=i: e.transpose(out=pT[:, i, :], in_=m[:, i, k * 128:(k + 1) * 128],
                                                      identity=ident[:]), [mk_, "c_identb"], [pTk])
            S.op("act" if k % 2 else "dve",
                 (lambda e: e.activation(out=mT[:, k, 0:nsub * 128], in_=pT[:, 0:nsub, :].rearrange("p i t -> p (i t)"),
                                         func=AF.Copy)) if k % 2 else
                 (lambda e: e.tensor_copy(out=mT[:, k, 0:nsub * 128], in_=pT[:, 0:nsub, :].rearrange("p i t -> p (i t)"))),
                 [pTk], [(mTk, k)])

        def out_chunk(i, n, xt, xk, mT, mTk, who):
            po, pok = por.next()
            for k in range(8):
                S.op("pe", lambda e, k=k: e.matmul(po[:], lhsT=mT[:, k, i * 128:(i + 1) * 128],
                                                   rhs=Wo[:, k, n * 512:(n + 1) * 512], start=(k == 0), stop=(k == 7)),
                     [(mTk, k), ("Wo", k)], [pok])
            tm, tmk = tmr.next()
            S.op("dve", lambda e: e.tensor_tensor(out=tm[:], in0=po[:], in1=G[:, who, n * 512:(n + 1) * 512],
                                                  op=ALU.mult), [pok, "mG%d" % who], [tmk])
            S.op("pool", lambda e: e.tensor_tensor(out=xt[:, i, n * 512:(n + 1) * 512],
                                                   in0=xt[:, i, n * 512:(n + 1) * 512], in1=tm[:], op=ALU.add),
                 [tmk, xk], [xk])

        def do_tile(tile):
            c0, nsub = tile
            who = 1 if c0 >= 32 else 0
            xt, xk = xtr.next()
            m, mk_ = mr.next()
            rows = slice(c0 * 128, (c0 + nsub) * 128)
            S.dma("sp", xt[:, 0:nsub, :], self.xres[rows, :].rearrange("(i p) d -> p i d", p=128), [], [xk])
            S.dma("sp", m[:, 0:nsub, :], self.merged[rows, :].rearrange("(i p) d -> p i d", p=128), [], [mk_])
            mT, mTk = mTr.next()
            for k in range(8):
                tr_k(k, m, mk_, mT, mTk, nsub)
            for i in range(nsub):
                for n in range(2):
                    out_chunk(i, n, xt, xk, mT, mTk, who)
            S.store("pool", self.xres[rows, :].rearrange("(i p) d -> p i d", p=128), xt[:, 0:nsub, :], [xk],
                    [("xres", c0)], skey=("st", xk))

        for tile in tiles:
            do_tile(tile)
        S.finish()

    def stage_odd_inproj(self):
        S = Stage(self, "oin")
        C = self.consts(S, ["identb"])
        ident = C["identb"]
        A, B = self.mod_cols(S, 1, 1, "m")
        R = self.front_rings(S)
        W = S.sb("W", [128, 8, ODD_IN], BF16)
        self.load_w(S, W, "W", self.owin_b, ODD_IN)
        wg = S.sb("wg", [17, 2, 512], BF16)
        for d in range(2):
            S.dma("pool", wg[0:16, d, :], self.gla_w_gate[d], [], [("wg", d, 0)])
            S.dma("pool", wg[16:17, d, :], self.gla_b_gate[d:d + 1, :], [], [("wg", d, 1)])
        lrr = Ring(S, "lrT", [17, 512], BF16, 4)
        obr = Ring(S, "ob", [128, 512], BF16, 4)
        ofr = Ring(S, "of", [128, 512], F32, 4)
        pjr = Ring(S, "pj", [128, 512], F32, 4, psum=True)
        plr = Ring(S, "pl", [16, 512], F32, 2, psum=True)
        QS = 128.0 ** -0.5

        def proj_fm(col, xh, xhk, ntok):
            pj, pjk = pjr.next()
            for k in range(8):
                S.op("pe", lambda e, k=k: e.matmul(pj[:, 0:ntok], lhsT=W[:, k, col:col + 128], rhs=xh[:, k, 0:ntok],
                                                   start=(k == 0), stop=(k == 7)), [("W", k), xhk], [pjk])
            return pj, pjk

        def proj_tm(col, n, i, xh, xhk):
            pj, pjk = pjr.next()
            for k in range(8):
                S.op("pe", lambda e, k=k: e.matmul(pj[:, 0:n], lhsT=xh[:, k, i * 128:(i + 1) * 128],
                                                   rhs=W[:, k, col:col + n], start=(k == 0), stop=(k == 7)),
                     [("W", k), xhk], [pjk])
            return pj, pjk

        def fm_chunk(g, col, dst, scale, xh, xhk, ntok, tok0):
            p1, p1k = proj_fm(col + g * 128, xh, xhk, ntok)
            ob, obk = obr.next()
            S.op("act", lambda e: e.activation(out=ob[:, 0:ntok], in_=p1[:, 0:ntok], func=AF.Copy, scale=scale),
                 [p1k], [obk])
            S.store("pool", dst[g * 128:(g + 1) * 128, tok0:tok0 + ntok], ob[:, 0:ntok], [obk],
                    [("fm", id(dst), g, tok0)], skey=("st", obk))

        def lr_dir(d, xh, xhk, ntok):
            pl, plk = plr.next()
            for k in range(8):
                S.op("pe", lambda e, k=k: e.matmul(pl[:, 0:ntok], lhsT=W[:, k, 3072 + 16 * d:3088 + 16 * d],
                                                   rhs=xh[:, k, 0:ntok], start=(k == 0), stop=(k == 7)),
                     [("W", k), xhk], [plk])
            lrT, lrk = lrr.next()
            S.op("pool", lambda e: e.memset(lrT[:], 1.0), [], [lrk])
            S.op("act", lambda e: e.activation(out=lrT[0:16, 0:ntok], in_=pl[:, 0:ntok], func=AF.Copy), [plk, lrk], [lrk])
            return lrT, lrk

        def gate_sub(d, i, lrT, lrk, tok0):
            r0 = tok0 + i * 128
            pj, pjk = pjr.next()
            S.op("pe", lambda e: e.matmul(pj[:], lhsT=lrT[0:17, i * 128:(i + 1) * 128], rhs=wg[0:17, d, :], start=True,
                                          stop=True), [lrk, ("wg", d, 0), ("wg", d, 1)], [pjk])
            of, ofk = ofr.next()
            S.op("act", lambda e: e.activation(out=of[:], in_=pj[:], func=AF.Exp, scale=-1.0), [pjk], [ofk])
            S.op("act", lambda e: e.activation(out=of[:], in_=of[:], func=AF.Ln, bias=1.0), [ofk], [ofk])
            S.op("dve", lambda e: e.tensor_scalar(out=of[:], in0=of[:], scalar1=-1.0 / 16.0, scalar2=None, op0=ALU.mult),
                 [ofk], [ofk])
            S.store("pool", self.GLA[d, r0:r0 + 128, :], of[:], [ofk], [("gla", d, r0)], skey=("st", ofk))

        def tm_one(col, dst, dcol, bf, i, xh, xhk, r0):
            pj, pjk = proj_tm(col, 512, i, xh, xhk)
            if bf:
                ob, obk = obr.next()
                S.op("act", lambda e: e.activation(out=ob[:], in_=pj[:], func=AF.Copy), [pjk], [obk])
                S.store("pool", dst[r0:r0 + 128, dcol:dcol + 512], ob[:], [obk], [("tm", col, r0)], skey=("st", obk))
            else:
                of, ofk = ofr.next()
                S.op("dve", lambda e: e.tensor_copy(out=of[:], in_=pj[:]), [pjk], [ofk])
                S.store("pool", dst[r0:r0 + 128, dcol:dcol + 512], of[:], [ofk], [("tm", col, r0)], skey=("st", ofk))

        def do_tile(tile, xt, xk):
            c0, nsub = tile
            who = 0 if c0 < 32 else 1
            ntok, tok0 = nsub * 128, c0 * 128
            xh, xhk = self.norm_front(S, R, xt, xk, nsub, who, A, B, "m", "m", ident)
            for g in range(4):
                fm_chunk(g, 0, self.GQT, QS, xh, xhk, ntok, tok0)
            for g in range(4):
                fm_chunk(g, 512, self.GKT, 1.0, xh, xhk, ntok, tok0)
            for d in range(2):
                lrT, lrk = lr_dir(d, xh, xhk, ntok)
                for i in range(nsub):
                    gate_sub(d, i, lrT, lrk, tok0)
            for i in range(nsub):
                r0 = tok0 + i * 128
                tm_one(512, self.GK, 0, True, i, xh, xhk, r0)
                tm_one(1024, self.GV, 0, True, i, xh, xhk, r0)
                tm_one(1536, self.GV, 512, True, i, xh, xhk, r0)
                tm_one(2048, self.GR, 0, False, i, xh, xhk, r0)
                tm_one(2560, self.GR, 512, False, i, xh, xhk, r0)

        nxt = self.load_x(S, R, ALL_TILES[0])
        for ti, tile in enumerate(ALL_TILES):
            xt, xk = nxt
            if ti + 1 < len(ALL_TILES):
                nxt = self.load_x(S, R, ALL_TILES[ti + 1])
            do_tile(tile, xt, xk)
        S.finish()

    def stage_gla(self, order_f=None, order_b=None):
        order_f = [32, 33] + list(range(32)) if order_f is None else order_f
        order_b = [33, 32] + list(range(31, -1, -1)) if order_b is None else order_b
        S = Stage(self, "gla")
        C = self.consts(S, ["U", "L", "SU", "SL"])
        gn = S.sb("gn", [128, 256], F32)
        S.load_bc(gn[:], "gn", self.gla_norm_g)
        Sst = S.sb("Sst", [128, 4, 256], F32)
        Sbf = S.sb("Sbf", [128, 4, 256], BF16)
        qTr = Ring(S, "qT", [128, 4, 128], BF16, 3)
        kTr = Ring(S, "kT", [128, 4, 128], BF16, 3)
        kMr = Ring(S, "kM", [128, 512], BF16, 3)
        vr = Ring(S, "v", [128, D], BF16, 3)
        lar = Ring(S, "la", [128, 512], F32, 3)
        epr = Ring(S, "Ep", [128, 128], F32, 12)
        enr = Ring(S, "En", [128, 128], F32, 6)
        ekr = Ring(S, "Ek", [128, 128], F32, 6)
        qtr = Ring(S, "qtl", [128, 128], BF16, 10)
        ktr = Ring(S, "ktl", [128, 128], BF16, 6)
        khr = Ring(S, "kh", [128, 128], BF16, 10)
        atr = Ring(S, "AT", [128, 128], BF16, 10)
        ofr = Ring(S, "of", [128, D], F32, 4)
        pBr = PRing(S, "pB", 128, 2, 1)
        pSr = PRing(S, "pSf", 128, 2, 1)
        pAr = PRing(S, "pA", 128, 2, 1)
        por = PRing(S, "po", 256, 2, 1)
        pDr = PRing(S, "pD", 256, 0, 1)
        for t_, (_, pk_) in zip(por.tensors, por.bufs):
            pDr.bufs.append((t_[:, 256:512], pk_))

        def head_front(d, c, h, X):
            need_out, qT, qTk, kT, kTk, kM, kMk, v, vk, la, lak, of, ofk = X["base"]
            Mx, Mk = (C["U"], "c_Uf") if d == 0 else (C["L"], "c_Lf")
            Ms, Msk = (C["SL"], "c_SLf") if d == 0 else (C["SU"], "c_SUf")
            lah = la[:, h * 128:(h + 1) * 128]
            pB, pBk = pBr.next()
            S.op("pe", lambda e: e.matmul(pB[:], lhsT=lah, rhs=Mx[:], start=True, stop=True), [lak, Mk], [pBk])
            pSf, pSfk = pSr.next()
            yield
            S.op("pe", lambda e: e.matmul(pSf[:], lhsT=Ms[:], rhs=lah, start=True, stop=True), [lak, Msk], [pSfk])
            Ep, Epk = epr.next()
            yield
            S.op("act", lambda e: e.activation(out=Ep[:], in_=pB[:], func=AF.Exp), [pBk], [Epk])
            Ek, Ekk = ekr.next()
            yield
            S.op("act", lambda e: e.activation(out=Ek[:], in_=pSf[:], func=AF.Exp), [pSfk], [Ekk])
            qt, qtk = qtr.next()
            yield
            S.op("dve", lambda e: e.tensor_tensor(out=qt[:], in0=qT[:, h, :], in1=Ep[:], op=ALU.mult), [qTk, Epk], [qtk])
            kh, khk = khr.next()
            yield
            S.op("dve", lambda e: e.tensor_tensor(out=kh[:], in0=kM[:, h * 128:(h + 1) * 128], in1=Ek[:], op=ALU.mult),
                 [kMk, Ekk], [khk])
            AT = ATk = None
            if need_out:
                En, Enk = enr.next()
                yield
                S.op("act", lambda e: e.activation(out=En[:], in_=pB[:], func=AF.Exp, scale=-1.0), [pBk], [Enk])
                kt, ktk = ktr.next()
                yield
                S.op("pool", lambda e: e.tensor_tensor(out=kt[:], in0=kT[:, h, :], in1=En[:], op=ALU.mult),
                     [kTk, Enk], [ktk])
                pA, pAk = pAr.next()
                yield
                S.op("pe", lambda e: e.matmul(pA[:], lhsT=kt[:], rhs=qt[:], start=True, stop=True), [ktk, qtk], [pAk])
                AT, ATk = atr.next()
                yield
                S.op("dve", lambda e: e.tensor_tensor(out=AT[:], in0=pA[:], in1=Mx[:], op=ALU.mult), [pAk, Mk], [ATk])
            X[h] = (Ep, Epk, qt, qtk, kh, khk, AT, ATk)

        def head_back(d, c, h, X):
            need_out, qT, qTk, kT, kTk, kM, kMk, v, vk, la, lak, of, ofk = X["base"]
            Ep, Epk, qt, qtk, kh, khk, AT, ATk = X[h]
            vh = v[:, h * 256:(h + 1) * 256]
            if need_out:
                po, pok = por.next()
                S.op("pe", lambda e: e.matmul(po[:], lhsT=AT[:], rhs=vh, start=True, stop=False), [ATk, vk], [pok])
                S.op("pe", lambda e: e.matmul(po[:], lhsT=qt[:], rhs=Sbf[:, h, :], start=False, stop=True),
                     [qtk, ("Sbf", h)], [pok])
                oh = of[:, h * 256:(h + 1) * 256]
                yield
                if d == 0:
                    S.op("act", lambda e: e.activation(out=oh, in_=po[:], func=AF.Copy), [pok], [(ofk, h)])
                else:
                    S.op("dve", lambda e: e.tensor_tensor(out=oh, in0=po[:], in1=oh, op=ALU.add), [pok, (ofk, h)],
                         [(ofk, h)])
            pD, pDk = pDr.next()
            yield
            S.op("pe", lambda e: e.matmul(pD[:], lhsT=kh[:], rhs=vh, start=True, stop=True), [khk, vk], [pDk])
            col = 127 if d == 0 else 0
            yield
            S.op("dve", lambda e: e.scalar_tensor_tensor(out=Sst[:, h, :], in0=Sst[:, h, :], scalar=Ep[:, col:col + 1],
                                                         in1=pD[:], op0=ALU.mult, op1=ALU.add),
                 [("Sst", h), Epk, pDk], [("Sst", h)])
            yield
            S.op("act", lambda e: e.activation(out=Sbf[:, h, :], in_=Sst[:, h, :], func=AF.Copy), [("Sst", h)],
                 [("Sbf", h)])

        frr = Ring(S, "fr", [128, 12], F32, 2)
        rr = Ring(S, "r", [128, D], F32, 2)
        srr = Ring(S, "sr", [128, D], F32, 2)
        yr = Ring(S, "y", [128, D], F32, 2)
        mor = Ring(S, "mo", [128, D], BF16, 2)
        junk = S.sb("junk", [128, 256], F32)

        def fin_sq(h, of, ofk, fr, frk):
            S.op("act", lambda e: e.activation(out=junk[:], in_=of[:, h * 256:(h + 1) * 256], func=AF.Square,
                                               accum_out=fr[:, h:h + 1]), [(ofk, h), frk], ["junk", frk])

        def fin_y(h, of, ofk, fr, frk, y, yk):
            S.op("dve", lambda e: e.scalar_tensor_tensor(out=y[:, h * 256:(h + 1) * 256], in0=of[:, h * 256:(h + 1) * 256],
                                                         scalar=fr[:, 8 + h:9 + h], in1=gn[:], op0=ALU.mult, op1=ALU.mult),
                 [(ofk, h), frk, "gn"], [(yk, h)])

        def finalize(c, of, ofk):
            fr, frk = frr.next()
            S.op("dve", lambda e: e.memset(fr[:], 0.0), [], [frk])
            for h in range(4):
                fin_sq(h, of, ofk, fr, frk)
            S.op("dve", lambda e: e.tensor_scalar(out=fr[:, 4:8], in0=fr[:, 0:4], scalar1=1.0 / 256, scalar2=EPS,
                                                  op0=ALU.mult, op1=ALU.add), [frk], [frk])
            S.op("act", lambda e: e.activation(out=fr[:, 4:8], in_=fr[:, 4:8], func=AF.Ln), [frk], [frk])
            S.op("act", lambda e: e.activation(out=fr[:, 8:12], in_=fr[:, 4:8], func=AF.Exp, scale=-0.5), [frk], [frk])
            r, rk = rr.next()
            sr, srk = srr.next()
            S.dma("sp", r[:], self.GR[c * 128:(c + 1) * 128, :], [], [rk])
            S.op("act", lambda e: e.activation(out=sr[:], in_=r[:], func=AF.Exp, scale=-1.0), [rk], [srk])
            S.op("pool", lambda e: e.tensor_scalar(out=sr[:], in0=sr[:], scalar1=1.0, scalar2=None, op0=ALU.add),
                 [srk], [srk])
            S.op("dve", lambda e: e.reciprocal(out=sr[:], in_=sr[:]), [srk], [srk])
            S.op("pool", lambda e: e.tensor_tensor(out=sr[:], in0=sr[:], in1=r[:], op=ALU.mult), [srk, rk], [srk])
            y, yk = yr.next()
            for h in range(4):
                fin_y(h, of, ofk, fr, frk, y, yk)
            mo, mok = mor.next()
            S.op("pool", lambda e: e.tensor_tensor(out=mo[:], in0=y[:], in1=sr[:], op=ALU.mult),
                 [(yk, h) for h in range(4)] + [srk], [mok])
            S.store("pool", self.merged[c * 128:(c + 1) * 128, :], mo[:], [mok], [("mrg", c)], skey=("st", mok))

        def chunk(d, c):
            t0 = c * 128
            need_out = c < 32
            qT, qTk = qTr.next()
            kT, kTk = kTr.next()
            kM, kMk = kMr.next()
            v, vk = vr.next()
            la, lak = lar.next()
            S.dma("sp", qT[:], self.GQT.rearrange("(h p) t -> p h t", p=128)[:, :, t0:t0 + 128], [], [qTk])
            S.dma("sp", kT[:], self.GKT.rearrange("(h p) t -> p h t", p=128)[:, :, t0:t0 + 128], [], [kTk])
            S.dma("sp", kM[:], self.GK[t0:t0 + 128, :], [], [kMk])
            S.dma("sp", v[:], self.GV[t0:t0 + 128, :], [], [vk])
            S.dma("sp", la[:], self.GLA[d, t0:t0 + 128, :], [], [lak])
            of, ofk = ofr.next()
            if need_out and d == 1:
                for h in range(4):
                    S.dma("sp", of[:, h * 256:(h + 1) * 256], self.OF[t0:t0 + 128, h * 256:(h + 1) * 256], [("OF", c)],
                          [(ofk, h)])
            return {"base": (need_out, qT, qTk, kT, kTk, kM, kMk, v, vk, la, lak, of, ofk)}

        def post(d, c, X):
            need_out, of, ofk = X["base"][0], X["base"][11], X["base"][12]
            t0 = c * 128
            if need_out:
                if d == 0:
                    S.store("pool", self.OF[t0:t0 + 128, :], of[:], [(ofk, h) for h in range(4)], [("OF", c)],
                            skey=("st", ofk))
                else:
                    finalize(c, of, ofk)

        for d, order in ((0, order_f), (1, order_b)):
            S.op("dve", lambda e: e.memset(Sst[:], 0.0), [("Sst", h) for h in range(4)], [("Sst", h) for h in range(4)])
            S.op("pool", lambda e: e.memset(Sbf[:], 0.0), [("Sbf", h) for h in range(4)], [("Sbf", h) for h in range(4)])
            if not order:
                continue
            X = chunk(d, order[0])
            for hh in (0, 2):
                rr_run([head_front(d, order[0], h, X) for h in (hh, hh + 1)])
            for i, c in enumerate(order):
                nc_ = order[i + 1] if i + 1 < len(order) else None
                Xn = chunk(d, nc_) if nc_ is not None else None
                for hh in (0, 2):
                    gens = [head_back(d, c, h, X) for h in (hh, hh + 1)]
                    if nc_ is not None:
                        gens += [head_front(d, nc_, h, Xn) for h in (hh, hh + 1)]
                    rr_run(gens)
                post(d, c, X)
                X = Xn
        S.finish()

    def stage_dump(self, extra=()):
        S = Stage(self, "dump")
        for nm in extra:
            src = getattr(self, nm)
            dst = self.nc.dram_tensor("dbg_" + nm, list(src.shape), src.dtype, kind="ExternalOutput").ap()
            if len(src.shape) == 2 and src.shape[0] > 512:
                for r in range(0, src.shape[0], 512):
                    n = min(512, src.shape[0] - r)
                    S.store("sp", dst[r:r + n, :], src[r:r + n, :], [], [("dd", nm, r)], skey="cp")
            else:
                S.store("sp", dst, src, [], [("dd", nm)], skey="cp")
        S.store("sp", self.dbg_mod, self.modrow.rearrange("l w n -> (l w) n"), [], [("dmod", 0)], skey="cp")
        for r in range(0, TT, 512):
            n = min(512, TT - r)
            S.store("sp", self.dbg_x[r:r + n, :], self.xres[r:r + n, :], [], [("dx", r)], skey="cp")
            S.store("sp", self.dbg_m[r:r + n, :], self.merged[r:r + n, :], [], [("dm", r)], skey="cp")
        S.finish()

    def build(self):
        upto = self.upto
        self.stage_prologue()
        self.stage_ada(0)
        steps = [
            ("ffn00", lambda: self.stage_ffn(0, 0, ALL_TILES, extra_casts=[(self.ewin_b, self.even_w_in, D, "ewin"), (self.ewout_b, self.even_w_out, D, "ewout"), (self.w1b[0, 1], self.ffn_w_in[0, 1], D, "w1b01"), (self.w2b[0, 1], self.ffn_w_out[0, 1], DFF, "w2b01")])),
            ("ein", lambda: self.stage_even_inproj()),
            ("att", lambda: self.stage_attn()),
            ("mlp", lambda: self.stage_mlprep()),
            ("mls", lambda: self.stage_mlstm()),
            ("op0", lambda: self.stage_outproj(0, self.ewout_b, ALL_TILES)),
            ("ffn01", lambda: self.stage_ffn(0, 1, ALL_TILES)),
            ("ada1", lambda: self.stage_ada(1, extra_casts=[(self.w1b[1, 0], self.ffn_w_in[1, 0], D, "w1b10"), (self.w2b[1, 0], self.ffn_w_out[1, 0], DFF, "w2b10")])),
            ("ffn10", lambda: self.stage_ffn(1, 0, ALL_TILES, extra_casts=[(self.owin_b, self.odd_w_in, D, "owin"), (self.owout_b, self.odd_w_out, D, "owout"), (self.w1b[1, 1], self.ffn_w_in[1, 1], D, "w1b11"), (self.w2b[1, 1], self.ffn_w_out[1, 1], DFF, "w2b11")])),
            ("oin", lambda: self.stage_odd_inproj()),
            ("gla", lambda: self.stage_gla()),
            ("op1", lambda: self.stage_outproj(1, self.owout_b, LAT_TILES)),
            ("ffn11", lambda: self.stage_ffn(1, 1, LAT_TILES, final_norm=True)),
        ]
        for name, fn in steps:
            fn()
            if upto == name:
                break
        if self.dbg:
            self.stage_dump()
        return self.nc


def host_constants():
    inv = 10000.0 ** (-np.arange(16, dtype=np.float32) * 2.0 / 32).astype(np.float32)
    t = np.arange(T)
    row = (t // 64).astype(np.float32)
    col = (t % 64).astype(np.float32)
    ar = row[None, :] * inv[:, None]
    ac = col[None, :] * inv[:, None]
    cr, sr, cc, sc = np.cos(ar), np.sin(ar), np.cos(ac), np.sin(ac)
    c64 = np.concatenate([cr, cr, cc, cc], axis=0)
    s64 = np.concatenate([-sr, sr, -sc, sc], axis=0)
    rope_c = np.concatenate([c64, c64], axis=0).astype(np.float32)
    rope_s = np.concatenate([s64, s64], axis=0).astype(np.float32)
    r = np.arange(128)
    ident = np.eye(128, dtype=np.float32)
    U = (r[:, None] <= r[None, :]).astype(np.float32)
    L = (r[:, None] >= r[None, :]).astype(np.float32)
    ones = np.ones((128, 128), np.float32)
    SU = (r[:, None] < r[None, :]).astype(np.float32)
    SL = (r[:, None] > r[None, :]).astype(np.float32)
    return rope_c, rope_s, np.stack([ident, U, L, ones, SU, SL])


def make_in_maps(inputs, cores):
    f = lambda a: np.ascontiguousarray(np.asarray(a, dtype=np.float32))
    rope_c, rope_s, cmask = host_constants()
    ew = np.asarray(inputs["even_w_in"], dtype=np.float32)[0]
    d = np.arange(512)
    blk, r = d // 32 * 32, d % 32
    perm = blk + (r + 16) % 32
    ewx = np.concatenate([ew, ew[:, 0:512][:, perm], ew[:, 512:1024][:, perm]], axis=1)
    shared = {
        "c_ctx": f(inputs["c_ctx"]).reshape(1, D),
        "ada_w": f(inputs["ada_w"]), "ada_b": f(inputs["ada_b"]),
        "norm_g": f(inputs["norm_g"]).reshape(2, 3 * D),
        "ffn_w_in": f(inputs["ffn_w_in"]), "ffn_w_out": f(inputs["ffn_w_out"]),
        "even_w_in": f(ewx), "even_w_out": f(inputs["even_w_out"])[0],
        "diff_lambda": f(inputs["diff_lambda"]).reshape(1, 256),
        "diff_norm_g": f(inputs["diff_norm_g"]).reshape(1, 128),
        "mlstm_conv_w": f(inputs["mlstm_conv_w"])[0], "mlstm_conv_b": f(inputs["mlstm_conv_b"]).reshape(1, 512),
        "mlstm_gate_b": f(inputs["mlstm_gate_b"]).reshape(1, 16),
        "mlstm_norm_g": f(inputs["mlstm_norm_g"]).reshape(1, 512),
        "odd_w_in": f(inputs["odd_w_in"])[0], "odd_w_out": f(inputs["odd_w_out"])[0],
        "gla_w_gate": f(inputs["gla_w_gate"])[0], "gla_b_gate": f(inputs["gla_b_gate"])[0],
        "gla_norm_g": f(inputs["gla_norm_g"]).reshape(1, 256), "final_g": f(inputs["final_g"]).reshape(1, D),
        "rope_c": rope_c, "rope_s": rope_s, "cmask": cmask,
    }
    maps = []
    for b in cores:
        m = dict(shared)
        m["x"] = f(inputs["x"][b])
        m["ctx"] = f(inputs["ctx"][b])
        m["c"] = f(inputs["c"][b]).reshape(1, D)
        maps.append(m)
    return maps


def kernel(**inputs):
    mk = MK()
    nc = mk.build()
    maps = make_in_maps(inputs, list(range(8)))
    res = run_bass_kernel_spmd(nc, maps, core_ids=list(range(8)))
    return np.stack([np.asarray(r["out"], dtype=np.float32) for r in res.results], axis=0)
```
